# Optimizing a Trainium2 kernel written in Bass

```python
import jax, jax.numpy as jnp
from jax import lax
import numpy as np

D_MODEL = 1024
BATCH = 16
SEQ = 2048
DEPTH = 4
DEC_BATCH = 2
DEC_SEQ = 16384
PAST_LEN = 128

GLA_HEADS = 4
GLA_WIDTH = D_MODEL // 2
GLA_DV = GLA_WIDTH // GLA_HEADS
GLA_DK = GLA_DV // 2
GLA_KWIDTH = GLA_HEADS * GLA_DK
GLA_GATE_RANK = 16
GLA_GATE_TAU = 16.0
GLA_CHUNK = 64
SGU_HEADS = 4
SGU_WIDTH = D_MODEL // 4
SGU_HEAD_DIM = SGU_WIDTH // SGU_HEADS
SGU_CHUNK = 128
POOL_WINDOWS = (2, 4, 8, 16)
POOL_GROUPS = len(POOL_WINDOWS)
POOL_WIDTH = D_MODEL // 4
POOL_GROUP_DIM = POOL_WIDTH // POOL_GROUPS
MIX_WIDTH = GLA_WIDTH + SGU_WIDTH + POOL_WIDTH
D_FF = ((8 * D_MODEL // 3 + 127) // 128) * 128
EPS = 1e-6
IN_PARTS = (GLA_KWIDTH, GLA_KWIDTH, GLA_WIDTH, GLA_WIDTH, 2 * GLA_GATE_RANK,
            SGU_WIDTH, SGU_WIDTH, POOL_WIDTH)
IN_WIDTH = sum(IN_PARTS)
IN_SPLITS = tuple(int(s) for s in np.cumsum(IN_PARTS)[:-1])

kernel_name = "hybrid_gla_sgu_pool_encoder"


def rmsnorm(x, g):
    xf = x.astype(jnp.float32)
    y = xf * lax.rsqrt(jnp.mean(xf * xf, axis=-1, keepdims=True) + EPS)
    return (y * g).astype(x.dtype)


def layernorm(x, g, b):
    xf = x.astype(jnp.float32)
    mu = jnp.mean(xf, axis=-1, keepdims=True)
    xc = xf - mu
    y = xc * lax.rsqrt(jnp.mean(xc * xc, axis=-1, keepdims=True) + EPS)
    return (y * g + b).astype(x.dtype)


def gla_direction(q, k, v, log_a):
    B, S, H, DK = q.shape
    DV = v.shape[-1]
    C = GLA_CHUNK
    N = S // C

    def to_chunks(t):
        return t.reshape(B, N, C, H, t.shape[-1]).transpose(0, 3, 1, 2, 4)

    q, k, v, log_a = to_chunks(q), to_chunks(k), to_chunks(v), to_chunks(log_a)
    G = jnp.cumsum(log_a, axis=3)
    G_last = G[:, :, :, -1:, :]
    q_dec = q * jnp.exp(G)
    k_dec = k * jnp.exp(-G)
    mask = jnp.tril(jnp.ones((C, C), dtype=bool))
    A = jnp.where(mask, jnp.einsum('bhnid,bhnjd->bhnij', q_dec, k_dec), 0.0)
    o = jnp.einsum('bhnij,bhnjv->bhniv', A, v)
    kv = jnp.einsum('bhnjd,bhnjv->bhndv', k * jnp.exp(G_last - G), v)
    chunk_decay = jnp.exp(G_last[:, :, :, 0, :])

    def step(state, inp):
        dec, kv_n = inp
        return dec[..., None] * state + kv_n, state

    init = jnp.zeros((B, H, DK, DV), jnp.float32)
    _, prev = lax.scan(step, init, (jnp.moveaxis(chunk_decay, 2, 0), jnp.moveaxis(kv, 2, 0)))
    prev = jnp.moveaxis(prev, 0, 2)
    o = o + jnp.einsum('bhnid,bhndv->bhniv', q_dec, prev)
    return o.transpose(0, 2, 3, 1, 4).reshape(B, S, H, DV)


def gla_mixer(q, k, v, g, lr, w2, b2, norm_g):
    B, S, _ = q.shape
    dt = q.dtype
    qf = q.astype(jnp.float32).reshape(B, S, GLA_HEADS, GLA_DK) * (GLA_DK ** -0.5)
    kf = k.astype(jnp.float32).reshape(B, S, GLA_HEADS, GLA_DK)
    vf = v.astype(jnp.float32).reshape(B, S, GLA_HEADS, GLA_DV)
    lr = lr.astype(jnp.float32).reshape(B, S, 2, GLA_GATE_RANK)
    z = jnp.einsum('bsdr,drk->bsdk', lr, w2.astype(jnp.float32)) + b2.astype(jnp.float32)
    log_a = (jax.nn.log_sigmoid(z) / GLA_GATE_TAU).reshape(B, S, 2, GLA_HEADS, GLA_DK)
    o_f = gla_direction(qf, kf, vf, log_a[:, :, 0])
    flip = lambda t: jnp.flip(t, axis=1)
    o_b = flip(gla_direction(flip(qf), flip(kf), flip(vf), flip(log_a[:, :, 1])))
    o = o_f + o_b
    o = o * lax.rsqrt(jnp.mean(o * o, axis=-1, keepdims=True) + EPS)
    o = o.reshape(B, S, GLA_WIDTH) * norm_g * jax.nn.silu(g.astype(jnp.float32))
    return o.astype(dt)


def spatial_gating(u, v, ln_g, ln_b, w_s, b_s):
    B, S, _ = u.shape
    N = S // SGU_CHUNK
    v = layernorm(v, ln_g, ln_b)
    vh = v.reshape(B, N, SGU_CHUNK, SGU_HEADS, SGU_HEAD_DIM)
    mixed = jnp.einsum('hts,bnshc->bnthc', w_s, vh) + jnp.transpose(b_s)[None, None, :, :, None]
    return (u * mixed.reshape(B, S, SGU_WIDTH)).astype(u.dtype)


def pool_mixer(xp, pool_w, pool_scale):
    B, S, _ = xp.shape
    xf = xp.astype(jnp.float32).reshape(B, S, POOL_GROUPS, POOL_GROUP_DIM)
    cs = jnp.concatenate([jnp.zeros((B, 1, POOL_GROUPS, POOL_GROUP_DIM), jnp.float32),
                          jnp.cumsum(xf, axis=1)], axis=1)
    t = jnp.arange(S)
    pooled = []
    for gi, w in enumerate(POOL_WINDOWS):
        h = w // 2
        lo = jnp.clip(t - h, 0, S)
        hi = jnp.clip(t + h, 0, S)
        cnt = (hi - lo).astype(jnp.float32)[None, :, None]
        c = cs[:, :, gi]
        pooled.append((jnp.take(c, hi, axis=1) - jnp.take(c, lo, axis=1)) / cnt)
    pooled = jnp.stack(pooled, axis=2)
    d = pooled - xf
    y = jnp.einsum('bsgc,gcd->bsgd', d, pool_w.astype(jnp.float32)).reshape(B, S, POOL_WIDTH)
    return (y * pool_scale).astype(xp.dtype)


def conv_glu_ffn(h, w1, conv_w, conv_b, w2):
    a, up = jnp.split(h @ w1, 2, axis=-1)
    ap = jnp.pad(a, ((0, 0), (1, 1), (0, 0)))
    a = ap[:, :-2] * conv_w[0] + ap[:, 1:-1] * conv_w[1] + ap[:, 2:] * conv_w[2] + conv_b
    return (jax.nn.silu(a) * up) @ w2


def trunk(x, norm1_g, w_in, gla_gate_w2, gla_gate_b, gla_norm_g, sgu_ln_g, sgu_ln_b,
          sgu_w, sgu_b, pool_w, pool_scale, w_o, norm2_g, w_ffn_in, conv_w, conv_b,
          w_ffn_out, norm_f):
    for l in range(DEPTH):
        h = rmsnorm(x, norm1_g[l])
        p = h @ w_in[l]
        q, k, v, g, lr, u, vs, xp = jnp.split(p, IN_SPLITS, axis=-1)
        a_out = gla_mixer(q, k, v, g, lr, gla_gate_w2[l], gla_gate_b[l], gla_norm_g[l])
        b_out = spatial_gating(jax.nn.gelu(u), jax.nn.gelu(vs), sgu_ln_g[l], sgu_ln_b[l],
                               sgu_w[l], sgu_b[l])
        c_out = pool_mixer(xp, pool_w[l], pool_scale[l])
        x = x + jnp.concatenate([a_out, b_out, c_out], axis=-1) @ w_o[l]
        h = rmsnorm(x, norm2_g[l])
        x = x + conv_glu_ffn(h, w_ffn_in[l], conv_w[l], conv_b[l], w_ffn_out[l])
    return rmsnorm(x, norm_f)


def setup_inputs(seed: int = 0) -> dict:
    key = jax.random.key(seed)
    ks = jax.random.split(key, 24)
    f32 = jnp.float32
    nrm = lambda k, shape, s: jax.random.normal(k, shape, f32) * s
    return {
        "x_prompt": jax.random.normal(ks[0], (BATCH, SEQ, D_MODEL), f32),
        "x_sample": jax.random.normal(ks[1], (DEC_BATCH, DEC_SEQ, D_MODEL), f32),
        "norm1_g": 1.0 + nrm(ks[2], (DEPTH, D_MODEL), 0.02),
        "w_in": nrm(ks[3], (DEPTH, D_MODEL, IN_WIDTH), D_MODEL ** -0.5),
        "gla_gate_w2": nrm(ks[4], (DEPTH, 2, GLA_GATE_RANK, GLA_KWIDTH), GLA_GATE_RANK ** -0.5),
        "gla_gate_b": nrm(ks[5], (DEPTH, 2, GLA_KWIDTH), 0.1),
        "gla_norm_g": 1.0 + nrm(ks[6], (DEPTH, GLA_WIDTH), 0.02),
        "sgu_ln_g": 1.0 + nrm(ks[7], (DEPTH, SGU_WIDTH), 0.02),
        "sgu_ln_b": nrm(ks[8], (DEPTH, SGU_WIDTH), 0.02),
        "sgu_w": nrm(ks[9], (DEPTH, SGU_HEADS, SGU_CHUNK, SGU_CHUNK), SGU_CHUNK ** -0.5),
        "sgu_b": 1.0 + nrm(ks[10], (DEPTH, SGU_HEADS, SGU_CHUNK), 0.02),
        "pool_w": nrm(ks[11], (DEPTH, POOL_GROUPS, POOL_GROUP_DIM, POOL_GROUP_DIM), POOL_GROUP_DIM ** -0.5),
        "pool_scale": 1.0 + nrm(ks[12], (DEPTH, POOL_WIDTH), 0.1),
        "w_o": nrm(ks[13], (DEPTH, MIX_WIDTH, D_MODEL), MIX_WIDTH ** -0.5),
        "norm2_g": 1.0 + nrm(ks[14], (DEPTH, D_MODEL), 0.02),
        "w_ffn_in": nrm(ks[15], (DEPTH, D_MODEL, 2 * D_FF), D_MODEL ** -0.5),
        "conv_w": nrm(ks[16], (DEPTH, 3, D_FF), 3 ** -0.5),
        "conv_b": nrm(ks[17], (DEPTH, D_FF), 0.02),
        "w_ffn_out": nrm(ks[18], (DEPTH, D_FF, D_MODEL), D_FF ** -0.5),
        "norm_f": 1.0 + nrm(ks[19], (D_MODEL,), 0.02),
    }


def reference(x_prompt, x_sample, norm1_g, w_in, gla_gate_w2, gla_gate_b, gla_norm_g,
              sgu_ln_g, sgu_ln_b, sgu_w, sgu_b, pool_w, pool_scale, w_o, norm2_g,
              w_ffn_in, conv_w, conv_b, w_ffn_out, norm_f):
    y_prompt = trunk(x_prompt, norm1_g, w_in, gla_gate_w2, gla_gate_b, gla_norm_g,
                     sgu_ln_g, sgu_ln_b, sgu_w, sgu_b, pool_w, pool_scale, w_o, norm2_g,
                     w_ffn_in, conv_w, conv_b, w_ffn_out, norm_f)
    y_sample = trunk(x_sample, norm1_g, w_in, gla_gate_w2, gla_gate_b, gla_norm_g,
                     sgu_ln_g, sgu_ln_b, sgu_w, sgu_b, pool_w, pool_scale, w_o, norm2_g,
                     w_ffn_in, conv_w, conv_b, w_ffn_out, norm_f)
    return (y_prompt, y_sample)
```

```python
import numpy as np
import concourse.bass as bass
import concourse.mybir as mybir
from concourse.bass_utils import run_bass_kernel_spmd

F32 = mybir.dt.float32
BF16 = mybir.dt.bfloat16
AF = mybir.ActivationFunctionType
ALU = mybir.AluOpType

D = 1024
KC = 8
DEPTH = 4
BLK = 2048
TILE = 512
EPS = 1e-6


class Buf:
    __slots__ = ("name", "writer", "readers", "excl")

    def __init__(self, name, excl=False):
        self.name = name
        self.writer = None
        self.readers = []
        self.excl = excl


class Op:
    __slots__ = ("eng", "fn", "deps", "semkey", "val", "needs_inc", "is_dma", "inc")


class Rec:
    ENGS = ("pe", "act", "dve", "pool", "sp")

    def __init__(self):
        self.ops = {e: [] for e in self.ENGS}
        self.dma_counts = {}
        self.last_dma = {}
        self.pending = {e: [] for e in self.ENGS}

    def _dep(self, op, d, raw):
        if d is None or d is op:
            return
        if not d.is_dma and d.eng == op.eng:
            if op.eng in ("pe", "sp") or not raw:
                return
        op.deps.append(d)
        d.needs_inc = True

    def add(self, eng, fn, reads=(), writes=(), dma_sem=None):
        op = Op()
        op.eng = eng
        op.fn = fn
        op.deps = []
        op.needs_inc = False
        op.is_dma = dma_sem is not None
        op.val = None
        if op.is_dma:
            c = self.dma_counts.get(dma_sem, 0) + 16
            self.dma_counts[dma_sem] = c
            op.semkey = dma_sem
            op.val = c
            op.inc = 16
            op.needs_inc = True
        else:
            op.semkey = eng
            op.inc = 1
        if self.pending[eng]:
            for d in self.pending[eng]:
                if d.is_dma or d.eng != eng:
                    op.deps.append(d)
                    d.needs_inc = True
            self.pending[eng] = []
        if op.is_dma:
            self.last_dma[dma_sem] = op
        for b in reads:
            self._dep(op, b.writer, True)
            if b.excl:
                for r in b.readers:
                    if r.eng != eng:
                        self._dep(op, r, False)
        for b in writes:
            self._dep(op, b.writer, False)
            for r in b.readers:
                self._dep(op, r, False)
        for b in reads:
            b.readers.append(op)
        for b in writes:
            b.writer = op
            b.readers = []
        self.ops[eng].append(op)
        return op

    def barrier(self):
        deps = []
        for e in self.ENGS:
            for op in reversed(self.ops[e]):
                if not op.is_dma:
                    deps.append(op)
                    break
        deps += list(self.last_dma.values())
        for e in self.ENGS:
            self.pending[e] = list(deps)

    def finalize(self):
        for e in self.ENGS:
            c = 0
            for op in self.ops[e]:
                if not op.is_dma and op.needs_inc:
                    c += 1
                    op.val = c

    def emit(self, eng, handle, sems):
        waited = {}
        for op in self.ops[eng]:
            need = {}
            for d in op.deps:
                v = need.get(d.semkey, 0)
                if d.val > v:
                    need[d.semkey] = d.val
            for k, v in need.items():
                if waited.get(k, 0) < v:
                    handle.wait_ge(sems[k], v)
                    waited[k] = v
            inst = op.fn(handle)
            if op.needs_inc:
                inst.then_inc(sems[op.semkey], op.inc)
        return waited


W_Q, W_K, W_V, W_G, W_LR, W_U, W_VS, W_XP = 0, 256, 512, 1024, 1536, 1568, 1824, 2080
INW = 2336
DFF = 2816
NF = 22
HB = 1024
WV = (342, 341, 341)
WS = (0, 342, 683)


class TK:
    __slots__ = ("t", "b")

    def __init__(self, t, name):
        self.t = t
        self.b = Buf(name)


class Arena:
    def __init__(self, t, nf32):
        self.t = t
        self.n = nf32
        self.off = 0

    def alloc(self, name, shape, dt=F32):
        nel = 1
        for s_ in shape[1:]:
            nel *= s_
        nbytes = nel * (2 if dt == BF16 else 4)
        nf = (nbytes + 3) // 4
        if self.off % 2:
            self.off += 1
        v = self.t[0:shape[0], self.off:self.off + nf]
        self.off += nf
        assert self.off <= self.n, (name, self.off, self.n)
        if dt == BF16:
            v = v.bitcast(BF16)[:, 0:nel]
        if len(shape) == 3:
            v = v.rearrange("p (a b) -> p a b", a=shape[1])
        elif len(shape) == 4:
            v = v.rearrange("p (a b c) -> p a b c", a=shape[1], b=shape[2])
        return TK(v, name)


def col_index(L):
    o = {}
    o["n1"] = 0
    o["n2"] = L * 8
    o["nf"] = 2 * L * 8
    o["cw"] = o["nf"] + 8
    o["cb"] = o["cw"] + L * 66
    o["gn"] = o["cb"] + L * 22
    o["ps"] = o["gn"] + L * 4
    o["n"] = o["ps"] + L * 2
    return o


def build_program(cfg):
    from contextlib import ExitStack
    T = cfg["T"]
    L = cfg["L"]
    NT = T // TILE
    NH = T // HB
    do_mixer = cfg.get("mixer", True)
    do_ffn = cfg.get("ffn", True)
    CI = col_index(L)
    nc = bass.Bass("TRN2", target_bir_lowering=False)
    R = Rec()

    def din(name, shape, dt=F32):
        return nc.dram_tensor(name, list(shape), dt, kind="ExternalInput").ap()

    def dscr(name, shape, dt=F32):
        return nc.dram_tensor(name, list(shape), dt, kind="Internal").ap()

    x_in = din("x_tok", [T, D])
    y_out = nc.dram_tensor("y_tok", [T, D], F32, kind="ExternalOutput").ap()
    ident_d = din("ident", [128, 128])
    maskf_d = din("maskf", [128, 512])
    maskb_d = din("maskb", [128, 512])
    cols_d = din("cols", [128, CI["n"]])
    fTL_d = din("flagT_L", [128, NT])
    fTR_d = din("flagT_R", [128, NT])
    fHL_d = din("flagH_L", [128, NH])
    fHR_d = din("flagH_R", [128, NH])
    if L > 0:
        invc_d = din("inv_cnt", [128, 2, T])
        w_in_d = din("w_in", [L, D, INW])
        w_o_d = din("w_o", [L, D, D])
        w_f1_d = din("w_ffn_in", [L, D, 2 * DFF])
        w_f2_d = din("w_ffn_out", [L, DFF, D])
        w2aug_d = din("w2aug", [L, 33, 512])
        wsT_d = din("wsT", [L, 128, 4, 128])
        sgub_d = din("sgu_bias", [L, 128, 2, 128])
        lng_d = din("lng", [L, 128, 256])
        lnb_d = din("lnb", [L, 128, 256])
        bands_d = din("bands", [128, 3, 4, 128])
        poolw_d = din("poolw", [L, 128, 2, 64])
    xa = dscr("xa_scr", [D, T + 2])
    xb = dscr("xb_scr", [D, T + 2])
    xa_v = xa.rearrange("(k p) t -> p k t", p=128)
    xb_v = xb.rearrange("(k p) t -> p k t", p=128)
    sbst_d = dscr("sbst_scr", [NT, 128, 256])
    stash_d = dscr("stash_scr", [NT, 128, 256], BF16)

    with ExitStack() as es:
        def sbt(name, shape, dt=F32):
            return TK(es.enter_context(nc.sbuf_tensor("s_" + name, list(shape), dt)), name)

        sems = {}
        for n in ["pe", "act", "dve", "pool"]:
            sems[n] = es.enter_context(nc.semaphore(n))
        banks = [es.enter_context(nc.psum_tensor(f"bank{i}", [128, 512], F32)) for i in range(8)]
        bbank = [Buf(f"bank{i}", excl=True) for i in range(8)]
        bctr = [0]

        def nb():
            i = bctr[0] % 8
            bctr[0] += 1
            return i

        def MM(out, lhsT, rhs, start, stop, rd, wr):
            R.add("pe", lambda h: h.matmul(out, lhsT=lhsT, rhs=rhs, start=start, stop=stop), reads=rd, writes=wr)

        def TR(out, in_, idn, rd, wr):
            R.add("pe", lambda h: h.transpose(out=out, in_=in_, identity=idn), reads=rd, writes=wr)

        def ACT(out, in_, func, rd, wr, scale=None, bias=None):
            kw = {}
            if scale is not None:
                kw["scale"] = scale
            if bias is not None:
                kw["bias"] = bias
            R.add("act", lambda h: h.activation(out=out, in_=in_, func=func, **kw), reads=rd, writes=wr)

        def CP(eng, out, in_, rd, wr):
            if eng == "act":
                R.add("act", lambda h: h.copy(out=out, in_=in_), reads=rd, writes=wr)
            else:
                R.add(eng, lambda h: h.tensor_copy(out=out, in_=in_), reads=rd, writes=wr)

        def TT(eng, out, in0, in1, op, rd, wr):
            R.add(eng, lambda h: h.tensor_tensor(out=out, in0=in0, in1=in1, op=op), reads=rd, writes=wr)

        def STT(out, in0, scalar, in1, op0, op1, rd, wr):
            R.add("dve", lambda h: h.scalar_tensor_tensor(out=out, in0=in0, scalar=scalar, in1=in1, op0=op0, op1=op1),
                  reads=rd, writes=wr)

        def TS(eng, out, in0, s1, op0, rd, wr, s2=None, op1=None):
            if op1 is None:
                R.add(eng, lambda h: h.tensor_scalar(out=out, in0=in0, scalar1=s1, scalar2=None, op0=op0), reads=rd, writes=wr)
            else:
                R.add(eng, lambda h: h.tensor_scalar(out=out, in0=in0, scalar1=s1, scalar2=s2, op0=op0, op1=op1),
                      reads=rd, writes=wr)

        def MSET(eng, ap, val, wr):
            R.add(eng, lambda h: h.memset(ap, val), writes=wr)

        def DMA(q, out, in_, rd, wr, sem, **kw):
            R.add(q, lambda h: h.dma_start(out=out, in_=in_, **kw), reads=rd, writes=wr, dma_sem=sem)

        ident = sbt("ident", [128, 128])
        ident_bf = sbt("ident_bf", [128, 128], BF16)
        maskf = sbt("maskf", [128, 512])
        maskb = sbt("maskb", [128, 512])
        ones_bf = sbt("ones_bf", [128, 128], BF16)
        epsc = sbt("epsc", [128, 1])
        onec = sbt("onec", [128, 1])
        cols = sbt("cols", [128, CI["n"]])
        fTL = sbt("fTL", [128, NT])
        fTR = sbt("fTR", [128, NT])
        fHL = sbt("fHL", [128, NH])
        fHR = sbt("fHR", [128, NH])
        zer = sbt("zer", [128, 256])
        for tk, d_ in ((ident, ident_d), (maskf, maskf_d), (maskb, maskb_d), (cols, cols_d), (fTL, fTL_d),
                       (fTR, fTR_d), (fHL, fHL_d), (fHR, fHR_d)):
            DMA("sp", tk.t[:], d_, [], [tk.b], "ld_" + tk.b.name)
        MSET("dve", ones_bf.t[:], 1.0, [ones_bf.b])
        MSET("dve", epsc.t[:], EPS, [epsc.b])
        MSET("dve", onec.t[:], 1.0, [onec.b])
        MSET("dve", zer.t[:], 0.0, [zer.b])
        CP("dve", ident_bf.t[:], ident.t[:], [ident.b], [ident_bf.b])
        trif = maskf.t[:, 0:128]
        trib = maskb.t[:, 0:128]
        for scr, nm in ((xa, "a"), (xb, "b")):
            for kk in range(KC):
                DMA("sp", scr[kk * 128:(kk + 1) * 128, 0:1], zer.t[:, 0:1], [zer.b], [], "st_zer",
                    allow_slow_non_contiguous=True)
                DMA("sp", scr[kk * 128:(kk + 1) * 128, T + 1:T + 2], zer.t[:, 0:1], [zer.b], [], "st_zer",
                    allow_slow_non_contiguous=True)

        def ccol(i):
            return cols.t[:, i:i + 1]

        ARN = 34500
        arena_t = es.enter_context(nc.sbuf_tensor("arena", [128, ARN], F32))

        def norm_tile(x3, xbuf, n, gbase, sq, rstd, out3, outbuf):
            ACT(sq.t[:, :, 0:n], x3, AF.Square, [xbuf], [sq.b])
            bk = nb()
            for k in range(KC):
                MM(banks[bk][:, 0:n], ones_bf.t[:], sq.t[:, k, 0:n], k == 0, k == KC - 1, [sq.b, ones_bf.b], [bbank[bk]])
            ACT(rstd.t[:, 0:n], banks[bk][:, 0:n], AF.Sqrt, [bbank[bk], epsc.b], [rstd.b], scale=1.0 / D, bias=epsc.t[:])
            R.add("dve", lambda h: h.reciprocal(out=rstd.t[:, 0:n], in_=rstd.t[:, 0:n]), reads=[rstd.b], writes=[rstd.b])
            for k in range(KC):
                STT(out3[:, k, :], x3[:, k, :], ccol(gbase + k), rstd.t[:, 0:n], ALU.mult, ALU.mult,
                    [xbuf, rstd.b, cols.b], [outbuf])

        A0 = Arena(arena_t, ARN)
        xin_t = [A0.alloc(f"xin{i}", [128, 4, D]) for i in range(2)]
        xT0 = [A0.alloc(f"xT0_{i}", [128, KC, TILE]) for i in range(2)]
        x_in_v = x_in.rearrange("(n g p) d -> n p g d", p=128, g=4)
        for t in range(NT):
            s = t % 2
            DMA("sp", xin_t[s].t[:], x_in_v[t], [], [xin_t[s].b], f"ld_xin{s}")
            for k in range(KC):
                bk = nb()
                for g in range(4):
                    TR(banks[bk][:, g * 128:(g + 1) * 128], xin_t[s].t[:, g, k * 128:(k + 1) * 128], ident.t[:],
                       [xin_t[s].b, ident.b], [bbank[bk]])
                CP("act" if k % 2 == 0 else "dve", xT0[s].t[:, k, :], banks[bk][:], [bbank[bk]], [xT0[s].b])
            DMA("sp", xa_v[:, :, 1 + t * TILE:1 + (t + 1) * TILE], xT0[s].t[:], [xT0[s].b], [], f"st_xT0_{s}")
        R.barrier()

        if L > 0:
            win = sbt("win", [128, KC, INW], BF16)
            wo = sbt("wo", [128, KC, D], BF16)
            w2aug = sbt("w2aug", [33, 512])
            wsT = sbt("wsT", [128, 4, 128], BF16)
            sgub = sbt("sgub", [128, 2, 128])
            lng = sbt("lng", [128, 256])
            lnb = sbt("lnb", [128, 256])
            bands = sbt("bands", [128, 3, 4, 128], BF16)
            poolw = sbt("poolw", [128, 2, 64], BF16)
            DMA("pool", bands.t[:], bands_d, [], [bands.b], "ld_bands")

        for l in range(L):
            if do_mixer:
                wv = w_in_d[l].rearrange("(k p) c -> p k c", p=128)
                for k in range(KC):
                    DMA("pool", win.t[:, k, :], wv[:, k, :], [], [win.b], "ld_win")
                wv = w_o_d[l].rearrange("(k p) c -> p k c", p=128)
                for k in range(0, KC, 2):
                    DMA("pool", wo.t[:, k:k + 2, :], wv[:, k:k + 2, :], [], [wo.b], "ld_wo")
                DMA("sp", w2aug.t[:], w2aug_d[l], [], [w2aug.b], "ld_w2aug")
                DMA("pool", wsT.t[:], wsT_d[l], [], [wsT.b], "ld_wsT")
                DMA("sp", sgub.t[:], sgub_d[l], [], [sgub.b], "ld_sgub")
                DMA("sp", lng.t[:], lng_d[l], [], [lng.b], "ld_lng")
                DMA("sp", lnb.t[:], lnb_d[l], [], [lnb.b], "ld_lnb")
                DMA("pool", poolw.t[:], poolw_d[l], [], [poolw.b], "ld_poolw")

            if do_mixer:
                A = Arena(arena_t, ARN)
                xT = [A.alloc("xT", [128, KC, TILE])] * 2
                rstd = A.alloc("rstd", [128, TILE])
                hT = A.alloc("hT", [128, KC, TILE], BF16)
                qT = A.alloc("qT", [128, 2, TILE])
                kT = A.alloc("kT", [128, 2, TILE])
                lrT = A.alloc("lrT", [33, TILE])
                sg = A.alloc("sg", [128, 4, TILE])
                gu = A.alloc("gu", [128, 2, TILE])
                xpT = A.alloc("xpT", [128, 2, TILE])
                vtok = [A.alloc(f"vtok{g}", [128, 512], BF16) for g in range(4)]
                e_t = A.alloc("e_t", [128, 512])
                sp_t = A.alloc("sp_t", [128, 512])
                Ep = A.alloc("Ep", [128, 4, 128])
                Em = A.alloc("Em", [128, 4, 128])
                qd = [A.alloc(f"qd{g}", [128, 4, 128], BF16) for g in range(4)]
                kd = [A.alloc(f"kd{g}", [128, 4, 128], BF16) for g in range(4)]
                kdt = [A.alloc(f"kdt{g}", [128, 512], BF16) for g in range(4)]
                egl = [A.alloc(f"egl{g}", [128, 4]) for g in range(4)]
                AT = [A.alloc(f"AT{g}", [128, 8, 128], BF16) for g in range(4)]
                Sf = A.alloc("Sf", [128, 2, 128])
                Sb = A.alloc("Sb", [128, 2, 128])
                Stmp = A.alloc("Stmp", [128, 2, 128])
                kvsb = [A.alloc(f"kvsb{g}", [128, 2, 128]) for g in range(4)]
                Sstg = [A.alloc(f"Sstg{i}", [128, 2, 128]) for i in range(2)]
                Sbf = [[A.alloc(f"Sbf{d_}{g}", [128, 2, 2, 128], BF16) for g in range(4)] for d_ in range(2)]
                osb = A.alloc("osb", [128, 512])
                osq = A.alloc("osq", [128, 512], BF16)
                rs = A.alloc("rs", [128, 512])
                otmp = A.alloc("otmp", [128, 512])
                vsg = A.alloc("vsg", [128, 256])
                vtmp = A.alloc("vtmp", [128, 256])
                vn = A.alloc("vn", [128, 256], BF16)
                stats = A.alloc("stats", [128, 8])
                mv = A.alloc("mv", [128, 4])
                stmp = A.alloc("stmp", [128, 2, 128])
                xpk = [[A.alloc(f"xpk{p}{g}", [128, 256], BF16) for g in range(4)] for p in range(2)]
                xpl = A.alloc("xpl", [128, 256], BF16)
                xpr = [A.alloc(f"xpr{i}", [128, 256], BF16) for i in range(2)]
                stsh = [A.alloc(f"stsh{i}", [128, 256], BF16) for i in range(2)]
                ptmp = A.alloc("ptmp", [128, 2, 128])
                dT = A.alloc("dT", [128, 2, TILE], BF16)
                mixT = A.alloc("mixT", [128, KC, TILE], BF16)
                sq = mixT
                invc = [A.alloc("invc", [128, 2, TILE])] * 2

                MSET("pool", lrT.t[32:33, :], 1.0, [lrT.b])
                for d_ in range(2):
                    for g in range(4):
                        MSET("pool", Sbf[d_][g].t[:], 0.0, [Sbf[d_][g].b])

                def proj_fm(c0, m, k_lo=0):
                    bk = nb()
                    for k in range(KC):
                        MM(banks[bk][0:m, :], win.t[:, k, c0:c0 + m], hT.t[:, k, :], k == 0, k == KC - 1,
                           [win.b, hT.b], [bbank[bk]])
                    return bk

                def proj_tm(g, c0, n):
                    bk = nb()
                    for k in range(KC):
                        MM(banks[bk][:, 0:n], hT.t[:, k, g * 128:(g + 1) * 128], win.t[:, k, c0:c0 + n], k == 0,
                           k == KC - 1, [win.b, hT.b], [bbank[bk]])
                    return bk

                def gates(g, dirs):
                    c0 = 0 if 0 in dirs else 256
                    n = 256 * len(dirs)
                    bk = nb()
                    MM(banks[bk][:, 0:n], lrT.t[:, g * 128:(g + 1) * 128], w2aug.t[:, c0:c0 + n], True, True,
                       [lrT.b, w2aug.b], [bbank[bk]])
                    ACT(e_t.t[:, 0:n], banks[bk][:, 0:n], AF.Exp, [bbank[bk]], [e_t.b], scale=-1.0)
                    ACT(sp_t.t[:, 0:n], e_t.t[:, 0:n], AF.Ln, [e_t.b, onec.b], [sp_t.b], bias=onec.t[:])
                    bk2 = nb()
                    for i_, d_ in enumerate(dirs):
                        for hh in range(2):
                            bi = d_ * 2 + hh
                            MM(banks[bk2][:, bi * 128:(bi + 1) * 128], sp_t.t[:, (i_ * 2 + hh) * 128:(i_ * 2 + hh + 1) * 128],
                               trif if d_ == 0 else trib, True, True, [sp_t.b, maskf.b, maskb.b], [bbank[bk2]])
                    b0 = dirs[0] * 2
                    b1 = dirs[-1] * 2 + 2
                    gv = banks[bk2][:, b0 * 128:b1 * 128].rearrange("p (a b) -> p a b", b=128)
                    ACT(Ep.t[:, b0:b1, :], gv, AF.Exp, [bbank[bk2]], [Ep.b], scale=-1.0 / 16)
                    ACT(Em.t[:, b0:b1, :], gv, AF.Exp, [bbank[bk2]], [Em.b], scale=1.0 / 16)
                    if 0 in dirs:
                        CP("pool", egl[g].t[:, 0:2], Ep.t[:, 0:2, 127], [Ep.b], [egl[g].b])
                    if 1 in dirs:
                        CP("pool", egl[g].t[:, 2:4], Ep.t[:, 2:4, 0], [Ep.b], [egl[g].b])
                    return b0, b1

                def kd_and_kv(g, dirs, b0, b1, want_kv):
                    for d_ in dirs:
                        TT("dve", kd[g].t[:, d_ * 2:d_ * 2 + 2, :], kT.t[:, :, g * 128:(g + 1) * 128],
                           Em.t[:, d_ * 2:d_ * 2 + 2, :], ALU.mult, [kT.b, Em.b], [kd[g].b])
                    bk = nb()
                    bv = banks[bk][:].bitcast(BF16)
                    for bi in range(b0, b1):
                        TR(bv[:, bi * 128:(bi + 1) * 128], kd[g].t[:, bi, :], ident_bf.t[:], [kd[g].b, ident_bf.b], [bbank[bk]])
                    CP("act", kdt[g].t[:, b0 * 128:b1 * 128], bv[:, b0 * 128:b1 * 128], [bbank[bk]], [kdt[g].b])
                    res = {}
                    for d_ in dirs:
                        if not want_kv[d_]:
                            continue
                        bk = nb()
                        for hd in range(4):
                            p0 = (hd % 2) * 64
                            MM(banks[bk][p0:p0 + 64, (hd // 2) * 128:(hd // 2 + 1) * 128],
                               kdt[g].t[:, d_ * 256 + hd * 64:d_ * 256 + (hd + 1) * 64], vtok[g].t[:, hd * 128:(hd + 1) * 128],
                               True, True, [kdt[g].b, vtok[g].b], [bbank[bk]])
                        res[d_] = bk
                    return res

                def state_step(S, kv3, kvbuf, g, d_):
                    TT("dve", Stmp.t[:], S.t[:], kv3, ALU.add, [S.b, kvbuf], [Stmp.b])
                    for hh in range(2):
                        TS("dve", S.t[:, hh, :], Stmp.t[:, hh, :], egl[g].t[:, d_ * 2 + hh:d_ * 2 + hh + 1], ALU.mult,
                           [Stmp.b, egl[g].b], [S.b])

                def state_to_bf(S, dst):
                    CP("pool", dst.t[0:64, :, 0, :], S.t[0:64, :, :], [S.b], [dst.b])
                    CP("pool", dst.t[64:128, :, 1, :], S.t[64:128, :, :], [S.b], [dst.b])

                MSET("dve", Sb.t[:], 0.0, [Sb.b])
                DMA("sp", stash_d[NT - 1], zer.t[:, 0:128].bitcast(BF16), [zer.b], [], "st_zer")
                S1LV = cfg.get("s1", 9)
                for t in (reversed(range(NT)) if S1LV > 0 else []):
                    s = t % 2
                    DMA("sp", xT[s].t[:], xa_v[:, :, 1 + t * TILE:1 + (t + 1) * TILE], [], [xT[s].b], f"ld_xT{s}")
                    norm_tile(xT[s].t[:], xT[s].b, TILE, CI["n1"] + l * 8, sq, rstd, hT.t, hT.b)
                    if S1LV < 2:
                        continue
                    for c in range(2):
                        bk = proj_fm(W_K + c * 128, 128)
                        CP("act", kT.t[:, c, :], banks[bk][:], [bbank[bk]], [kT.b])
                    bk = proj_fm(W_LR, 32)
                    CP("act", lrT.t[0:32, :], banks[bk][0:32, :], [bbank[bk]], [lrT.b])
                    for g in range(4):
                        bk = proj_tm(g, W_V, 512)
                        CP("dve", vtok[g].t[:], banks[bk][:], [bbank[bk]], [vtok[g].b])
                    if t >= 1:
                        bk = proj_tm(0, W_XP, 256)
                        TS("dve", stsh[s].t[:], banks[bk][:, 0:256], fTR.t[:, t - 1:t], ALU.mult, [bbank[bk], fTR.b], [stsh[s].b])
                        DMA("sp", stash_d[t - 1], stsh[s].t[:], [stsh[s].b], [], f"st_stsh{s}")
                    if S1LV < 3:
                        continue
                    TS("dve", Sb.t[:], Sb.t[:], fTR.t[:, t:t + 1], ALU.mult, [Sb.b, fTR.b], [Sb.b])
                    CP("pool", Sstg[s].t[:], Sb.t[:], [Sb.b], [Sstg[s].b])
                    DMA("sp", sbst_d[t].rearrange("p (a b) -> p a b", b=128), Sstg[s].t[:], [Sstg[s].b], [], f"st_Sstg{s}")
                    if S1LV < 4:
                        continue
                    for g in reversed(range(4)):
                        b0, b1 = gates(g, (1,))
                        if S1LV < 5:
                            continue
                        kvb = kd_and_kv(g, (1,), b0, b1, {1: True})
                        if S1LV < 6:
                            continue
                        state_step(Sb, banks[kvb[1]][:, 0:256].rearrange("p (a b) -> p a b", b=128), bbank[kvb[1]], g, 1)
                R.barrier()

                MSET("dve", Sf.t[:], 0.0, [Sf.b])
                MSET("pool", xpl.t[:], 0.0, [xpl.b])
                for t in (range(NT) if cfg.get("s2", 9) > 0 else []):
                    s = t % 2
                    par = t % 2
                    DMA("sp", xT[s].t[:], xa_v[:, :, 1 + t * TILE:1 + (t + 1) * TILE], [], [xT[s].b], f"ld_xT{s}")
                    DMA("sp", invc[s].t[:], invc_d[:, :, t * TILE:(t + 1) * TILE], [], [invc[s].b], f"ld_invc{s}")
                    DMA("sp", Sstg[s].t[:], sbst_d[t].rearrange("p (a b) -> p a b", b=128), [], [Sstg[s].b], f"ld_Sstg{s}")
                    DMA("sp", xpr[s].t[:], stash_d[t], [], [xpr[s].b], f"ld_xpr{s}")
                    norm_tile(xT[s].t[:], xT[s].b, TILE, CI["n1"] + l * 8, sq, rstd, hT.t, hT.b)
                    if cfg.get("s2", 9) < 2:
                        continue
                    for c in range(2):
                        bk = proj_fm(W_Q + c * 128, 128)
                        ACT(qT.t[:, c, :], banks[bk][:], AF.Copy, [bbank[bk]], [qT.b], scale=0.125)
                    for c in range(2):
                        bk = proj_fm(W_K + c * 128, 128)
                        CP("act", kT.t[:, c, :], banks[bk][:], [bbank[bk]], [kT.b])
                    bk = proj_fm(W_LR, 32)
                    CP("act", lrT.t[0:32, :], banks[bk][0:32, :], [bbank[bk]], [lrT.b])
                    for c in range(4):
                        bk = proj_fm(W_G + c * 128, 128)
                        ACT(sg.t[:, c, :], banks[bk][:], AF.Silu, [bbank[bk]], [sg.b])
                    for c in range(2):
                        bk = proj_fm(W_U + c * 128, 128)
                        ACT(gu.t[:, c, :], banks[bk][:], AF.Gelu_apprx_tanh, [bbank[bk]], [gu.b])
                    for c in range(2):
                        bk = proj_fm(W_XP + c * 128, 128)
                        CP("act", xpT.t[:, c, :], banks[bk][:], [bbank[bk]], [xpT.b])
                    if cfg.get("s2", 9) < 3:
                        continue
                    for g in range(4):
                        gc = slice(g * 128, (g + 1) * 128)
                        bk = proj_tm(g, W_V, 512)
                        CP("dve", vtok[g].t[:], banks[bk][:], [bbank[bk]], [vtok[g].b])
                        bk = proj_tm(g, W_VS, 512)
                        ACT(vsg.t[:], banks[bk][:, 0:256], AF.Gelu_apprx_tanh, [bbank[bk]], [vsg.b])
                        CP("pool" if False else "dve", xpk[par][g].t[:], banks[bk][:, 256:512], [bbank[bk]], [xpk[par][g].b])
                        dbg = cfg.get("dbg", 0)
                        if dbg == 1:
                            MSET("dve", mv.t[:], 1.0, [mv.b])
                        else:
                            R.add("dve", lambda h: h.bn_stats(out=stats.t[:, 0:6], in_=vsg.t[:]), reads=[vsg.b], writes=[stats.b])
                            R.add("dve", lambda h: h.bn_aggr(out=mv.t[:, 0:2], in_=stats.t[:, 0:6]), reads=[stats.b], writes=[mv.b])
                            if dbg == 2:
                                MSET("dve", mv.t[:, 2:4], 1.0, [mv.b])
                            else:
                                ACT(mv.t[:, 2:3], mv.t[:, 1:2], AF.Sqrt, [mv.b, epsc.b], [mv.b], bias=epsc.t[:])
                                R.add("dve", lambda h: h.reciprocal(out=mv.t[:, 3:4], in_=mv.t[:, 2:3]), reads=[mv.b], writes=[mv.b])
                        TS("dve", vtmp.t[:], vsg.t[:], mv.t[:, 0:1], ALU.subtract, [vsg.b, mv.b], [vtmp.b], s2=mv.t[:, 3:4], op1=ALU.mult)
                        TT("dve", vtmp.t[:], vtmp.t[:], lng.t[:], ALU.mult, [vtmp.b, lng.b], [vtmp.b])
                        TT("dve", vn.t[:], vtmp.t[:], lnb.t[:], ALU.add, [vtmp.b, lnb.b], [vn.b])
                        bk = nb()
                        for hd in range(4):
                            p0 = (hd % 2) * 64
                            MM(banks[bk][p0:p0 + 64, (hd // 2) * 128:(hd // 2 + 1) * 128], vn.t[:, hd * 64:(hd + 1) * 64],
                               wsT.t[:, hd, :], True, True, [vn.b, wsT.b], [bbank[bk]])
                        TT("dve", stmp.t[:], banks[bk][:, 0:256].rearrange("p (a b) -> p a b", b=128), sgub.t[:], ALU.add,
                           [bbank[bk], sgub.b], [stmp.b])
                        TT("dve", mixT.t[:, 4:6, gc], stmp.t[:], gu.t[:, :, gc], ALU.mult, [stmp.b, gu.b], [mixT.b])
                    if cfg.get("s2", 9) < 4:
                        continue
                    kvf = {}
                    kvbk = {}
                    for g in range(4):
                        gc = slice(g * 128, (g + 1) * 128)
                        b0, b1 = gates(g, (0, 1))
                        for d_ in range(2):
                            TT("dve", qd[g].t[:, d_ * 2:d_ * 2 + 2, :], qT.t[:, :, gc], Ep.t[:, d_ * 2:d_ * 2 + 2, :], ALU.mult,
                               [qT.b, Ep.b], [qd[g].b])
                        kv = kd_and_kv(g, (0, 1), b0, b1, {0: True, 1: g >= 1})
                        for e in range(2):
                            bk = nb()
                            p0 = e * 64
                            for bi in range(4):
                                MM(banks[bk][:, bi * 128:(bi + 1) * 128], kd[g].t[p0:p0 + 64, bi, :], qd[g].t[p0:p0 + 64, bi, :],
                                   True, True, [kd[g].b, qd[g].b], [bbank[bk]])
                            av = banks[bk][:].rearrange("p (a b) -> p a b", b=128)
                            TT("dve", AT[g].t[:, e * 4:e * 4 + 2, :], av[:, 0:2, :], maskf.t[:, 0:256].rearrange("p (a b) -> p a b", b=128),
                               ALU.mult, [bbank[bk], maskf.b], [AT[g].b])
                            TT("dve", AT[g].t[:, e * 4 + 2:e * 4 + 4, :], av[:, 2:4, :], maskb.t[:, 0:256].rearrange("p (a b) -> p a b", b=128),
                               ALU.mult, [bbank[bk], maskb.b], [AT[g].b])
                        if g == 0:
                            if t >= 1:
                                TS("dve", Sf.t[:], Sf.t[:], fTL.t[:, t:t + 1], ALU.mult, [Sf.b, fTL.b], [Sf.b])
                        state_to_bf(Sf, Sbf[0][g])
                        state_step(Sf, banks[kv[0]][:, 0:256].rearrange("p (a b) -> p a b", b=128), bbank[kv[0]], g, 0)
                        if g >= 1:
                            CP("act", kvsb[g].t[:], banks[kv[1]][:, 0:256].rearrange("p (a b) -> p a b", b=128),
                               [bbank[kv[1]]], [kvsb[g].b])
                    if cfg.get("s2", 9) < 5:
                        continue
                    CP("dve", Sb.t[:], Sstg[s].t[:], [Sstg[s].b], [Sb.b])
                    state_to_bf(Sb, Sbf[1][3])
                    for g in (3, 2, 1):
                        state_step(Sb, kvsb[g].t[:], kvsb[g].b, g, 1)
                        state_to_bf(Sb, Sbf[1][g - 1])
                    if cfg.get("s2", 9) < 6:
                        continue
                    for g in range(4):
                        gc = slice(g * 128, (g + 1) * 128)
                        bk = nb()
                        for hd in range(4):
                            hh = hd // 2
                            e = hd % 2
                            o_ap = banks[bk][:, hd * 128:(hd + 1) * 128]
                            MM(o_ap, vtok[g].t[:, hd * 128:(hd + 1) * 128], AT[g].t[:, e * 4 + 0 + hh, :], True, False,
                               [vtok[g].b, AT[g].b], [bbank[bk]])
                            MM(o_ap, vtok[g].t[:, hd * 128:(hd + 1) * 128], AT[g].t[:, e * 4 + 2 + hh, :], False, False,
                               [vtok[g].b, AT[g].b], [bbank[bk]])
                            MM(o_ap, Sbf[0][g].t[:, hh, e, :], qd[g].t[:, 0 + hh, :], False, False,
                               [Sbf[0][g].b, qd[g].b], [bbank[bk]])
                            MM(o_ap, Sbf[1][g].t[:, hh, e, :], qd[g].t[:, 2 + hh, :], False, True,
                               [Sbf[1][g].b, qd[g].b], [bbank[bk]])
                        CP("act", osb.t[:], banks[bk][:], [bbank[bk]], [osb.b])
                        ACT(osq.t[:], banks[bk][:], AF.Square, [bbank[bk]], [osq.b])
                        bk2 = nb()
                        MM(banks[bk2][:], ones_bf.t[:], osq.t[:], True, True, [osq.b, ones_bf.b], [bbank[bk2]])
                        ACT(rs.t[:], banks[bk2][:], AF.Sqrt, [bbank[bk2], epsc.b], [rs.b], scale=1.0 / 128, bias=epsc.t[:])
                        R.add("dve", lambda h: h.reciprocal(out=rs.t[:], in_=rs.t[:]), reads=[rs.b], writes=[rs.b])
                        TT("dve", otmp.t[:], osb.t[:], rs.t[:], ALU.mult, [osb.b, rs.b], [otmp.b])
                        for hd in range(4):
                            STT(mixT.t[:, hd, gc], otmp.t[:, hd * 128:(hd + 1) * 128], ccol(CI["gn"] + l * 4 + hd), sg.t[:, hd, gc],
                                ALU.mult, ALU.mult, [otmp.b, cols.b, sg.b], [mixT.b])
                    if cfg.get("s2", 9) < 7:
                        continue
                    if t >= 1:
                        TS("dve", xpl.t[:], xpk[1 - par][3].t[:], fTL.t[:, t:t + 1], ALU.mult, [xpk[1 - par][3].b, fTL.b], [xpl.b])
                    for g in range(4):
                        gc = slice(g * 128, (g + 1) * 128)
                        prv = xpl if g == 0 else xpk[par][g - 1]
                        nxt = xpr[s] if g == 3 else xpk[par][g + 1]
                        cur = xpk[par][g]
                        bk = nb()
                        for pg in range(4):
                            p0 = (pg % 2) * 64
                            o_ap = banks[bk][p0:p0 + 64, (pg // 2) * 128:(pg // 2 + 1) * 128]
                            for wi, src in enumerate((prv, cur, nxt)):
                                MM(o_ap, src.t[:, pg * 64:(pg + 1) * 64], bands.t[:, wi, pg, :], wi == 0, wi == 2,
                                   [src.b, bands.b], [bbank[bk]])
                        TT("dve", ptmp.t[:], banks[bk][:, 0:256].rearrange("p (a b) -> p a b", b=128), invc[s].t[:, :, gc], ALU.mult,
                           [bbank[bk], invc[s].b], [ptmp.b])
                        TT("dve", dT.t[:, :, gc], ptmp.t[:], xpT.t[:, :, gc], ALU.subtract, [ptmp.b, xpT.b], [dT.b])
                    for j in range(2):
                        bk = nb()
                        for e in range(2):
                            p0 = e * 64
                            MM(banks[bk][p0:p0 + 64, :], poolw.t[p0:p0 + 64, j, :], dT.t[p0:p0 + 64, j, :], True, True,
                               [poolw.b, dT.b], [bbank[bk]])
                        ACT(mixT.t[:, 6 + j, :], banks[bk][:], AF.Identity, [bbank[bk], cols.b], [mixT.b],
                            scale=ccol(CI["ps"] + l * 2 + j))
                    if cfg.get("s2", 9) < 8:
                        continue
                    for dch in range(KC):
                        bk = nb()
                        for k in range(KC):
                            MM(banks[bk][:], wo.t[:, k, dch * 128:(dch + 1) * 128], mixT.t[:, k, :], k == 0, k == KC - 1,
                               [wo.b, mixT.b], [bbank[bk]])
                        TT("dve", xT[s].t[:, dch, :], xT[s].t[:, dch, :], banks[bk][:], ALU.add, [xT[s].b, bbank[bk]], [xT[s].b])
                    DMA("sp", xb_v[:, :, 1 + t * TILE:1 + (t + 1) * TILE], xT[s].t[:], [xT[s].b], [], f"st_xT{s}")
                R.barrier()
            else:
                A = Arena(arena_t, ARN)
                xT = [A.alloc(f"xT{i}", [128, KC, TILE]) for i in range(2)]
                for t in range(NT):
                    s = t % 2
                    DMA("sp", xT[s].t[:], xa_v[:, :, 1 + t * TILE:1 + (t + 1) * TILE], [], [xT[s].b], f"ld_xT{s}")
                    DMA("sp", xb_v[:, :, 1 + t * TILE:1 + (t + 1) * TILE], xT[s].t[:], [xT[s].b], [], f"st_xT{s}")
                R.barrier()

            if do_ffn:
                A = Arena(arena_t, ARN)
                h2T = A.alloc("h2T", [128, KC, HB + 2], BF16)
                gT = A.alloc("gT", [128, NF, HB], BF16)
                xw = [A.alloc(f"xw{i}", [128, KC, 342]) for i in range(2)]
                sq3 = A.alloc("sq3", [128, KC, 342], BF16)
                rstd3 = A.alloc("rstd3", [128, 342])
                w1a = [A.alloc(f"w1a{i}", [128, KC, 256], BF16) for i in range(2)]
                w1u = [A.alloc(f"w1u{i}", [128, KC, 256], BF16) for i in range(2)]
                w2c = [A.alloc(f"w2c{i}", [128, NF, 128], BF16) for i in range(2)]
                xd = [A.alloc(f"xd{i}", [128, HB]) for i in range(2)]
                t1 = [A.alloc(f"t1_{i}", [128, 344]) for i in range(2)]
                t2 = [A.alloc(f"t2_{i}", [128, 344]) for i in range(2)]
                w1v = w_f1_d[l].rearrange("(k p) c -> p k c", p=128)
                w2v = w_f2_d[l].rearrange("(f p) c -> p f c", p=128)
                cnt3 = 0
                for hb in range(NH):
                    cb = hb * HB
                    for w in range(3):
                        s = (hb * 3 + w) % 2
                        DMA("sp", xw[s].t[:], xb_v[:, :, cb + 342 * w:cb + 342 * (w + 1)], [], [xw[s].b], f"ld_xw{s}")
                        norm_tile(xw[s].t[:], xw[s].b, 342, CI["n2"] + l * 8, sq3, rstd3, h2T.t[:, :, 342 * w:342 * (w + 1)], h2T.b)
                    for fp in range(NF // 2):
                        ws_ = (hb * (NF // 2) + fp) % 2
                        for k in range(0, KC, 4):
                            DMA("pool", w1a[ws_].t[:, k:k + 4, :], w1v[:, k:k + 4, fp * 256:(fp + 1) * 256], [], [w1a[ws_].b], f"ld_w1a{ws_}")
                            DMA("pool", w1u[ws_].t[:, k:k + 4, :], w1v[:, k:k + 4, DFF + fp * 256:DFF + (fp + 1) * 256], [], [w1u[ws_].b], f"ld_w1u{ws_}")
                        for fi in range(2):
                            f = fp * 2 + fi
                            for w in range(3):
                                nv = WV[w]
                                v0 = WS[w]
                                ts_ = cnt3 % 2
                                cnt3 += 1
                                bka = nb()
                                for k in range(KC):
                                    MM(banks[bka][:, 0:nv + 2], w1a[ws_].t[:, k, fi * 128:(fi + 1) * 128], h2T.t[:, k, v0:v0 + nv + 2],
                                       k == 0, k == KC - 1, [w1a[ws_].b, h2T.b], [bbank[bka]])
                                bku = nb()
                                for k in range(KC):
                                    MM(banks[bku][:, 0:nv], w1u[ws_].t[:, k, fi * 128:(fi + 1) * 128], h2T.t[:, k, v0 + 1:v0 + 1 + nv],
                                       k == 0, k == KC - 1, [w1u[ws_].b, h2T.b], [bbank[bku]])
                                if w == 0:
                                    TS("dve", banks[bka][:, 0:1], banks[bka][:, 0:1], fHL.t[:, hb:hb + 1], ALU.mult,
                                       [bbank[bka], fHL.b], [bbank[bka]])
                                if w == 2:
                                    TS("dve", banks[bka][:, nv + 1:nv + 2], banks[bka][:, nv + 1:nv + 2], fHR.t[:, hb:hb + 1], ALU.mult,
                                       [bbank[bka], fHR.b], [bbank[bka]])
                                cwi = CI["cw"] + l * 66
                                ACT(t1[ts_].t[:, 0:nv], banks[bka][:, 1:nv + 1], AF.Identity, [bbank[bka], cols.b], [t1[ts_].b],
                                    scale=ccol(cwi + 22 + f), bias=ccol(CI["cb"] + l * 22 + f))
                                STT(t2[ts_].t[:, 0:nv], banks[bka][:, 0:nv], ccol(cwi + f), t1[ts_].t[:, 0:nv], ALU.mult, ALU.add,
                                    [bbank[bka], cols.b, t1[ts_].b], [t2[ts_].b])
                                STT(t1[ts_].t[:, 0:nv], banks[bka][:, 2:nv + 2], ccol(cwi + 44 + f), t2[ts_].t[:, 0:nv], ALU.mult, ALU.add,
                                    [bbank[bka], cols.b, t2[ts_].b], [t1[ts_].b])
                                ACT(t2[ts_].t[:, 0:nv], t1[ts_].t[:, 0:nv], AF.Silu, [t1[ts_].b], [t2[ts_].b])
                                TT("dve", gT.t[:, f, v0:v0 + nv], t2[ts_].t[:, 0:nv], banks[bku][:, 0:nv], ALU.mult,
                                   [t2[ts_].b, bbank[bku]], [gT.b])
                    for dch in range(KC):
                        ds_ = (hb * KC + dch) % 2
                        DMA("pool", w2c[ds_].t[:], w2v[:, :, dch * 128:(dch + 1) * 128], [], [w2c[ds_].b], f"ld_w2c{ds_}")
                        DMA("sp", xd[ds_].t[:], xb[dch * 128:(dch + 1) * 128, 1 + cb:1 + cb + HB], [], [xd[ds_].b], f"ld_xd{ds_}")
                        for q in range(2):
                            bk = nb()
                            for f in range(NF):
                                MM(banks[bk][:], w2c[ds_].t[:, f, :], gT.t[:, f, q * 512:(q + 1) * 512], f == 0, f == NF - 1,
                                   [w2c[ds_].b, gT.b], [bbank[bk]])
                            TT("dve", xd[ds_].t[:, q * 512:(q + 1) * 512], xd[ds_].t[:, q * 512:(q + 1) * 512], banks[bk][:], ALU.add,
                               [xd[ds_].b, bbank[bk]], [xd[ds_].b])
                        DMA("sp", xa[dch * 128:(dch + 1) * 128, 1 + cb:1 + cb + HB], xd[ds_].t[:], [xd[ds_].b], [], f"st_xd{ds_}")
                R.barrier()
            else:
                A = Arena(arena_t, ARN)
                xTc = [A.alloc(f"xTc{i}", [128, KC, TILE]) for i in range(2)]
                for t in range(NT):
                    s = t % 2
                    DMA("sp", xTc[s].t[:], xb_v[:, :, 1 + t * TILE:1 + (t + 1) * TILE], [], [xTc[s].b], f"ld_xTc{s}")
                    DMA("sp", xa_v[:, :, 1 + t * TILE:1 + (t + 1) * TILE], xTc[s].t[:], [xTc[s].b], [], f"st_xTc{s}")
                R.barrier()

        A = Arena(arena_t, ARN)
        xTf = [A.alloc(f"xTf{i}", [128, KC, TILE]) for i in range(2)]
        sqf = A.alloc("sqf", [128, KC, TILE], BF16)
        rstdf = A.alloc("rstdf", [128, TILE])
        yT = A.alloc("yT", [128, KC, TILE])
        yo = [A.alloc(f"yo{i}", [128, 4, D]) for i in range(2)]
        y_out_v = y_out.rearrange("(n g p) d -> n p g d", p=128, g=4)
        for t in range(NT):
            s = t % 2
            DMA("sp", xTf[s].t[:], xa_v[:, :, 1 + t * TILE:1 + (t + 1) * TILE], [], [xTf[s].b], f"ld_xTf{s}")
            ACT(sqf.t[:], xTf[s].t[:], AF.Square, [xTf[s].b], [sqf.b])
            bk = nb()
            for k in range(KC):
                MM(banks[bk][:], ones_bf.t[:], sqf.t[:, k, :], k == 0, k == KC - 1, [sqf.b, ones_bf.b], [bbank[bk]])
            ACT(rstdf.t[:], banks[bk][:], AF.Sqrt, [bbank[bk], epsc.b], [rstdf.b], scale=1.0 / D, bias=epsc.t[:])
            R.add("dve", lambda h: h.reciprocal(out=rstdf.t[:], in_=rstdf.t[:]), reads=[rstdf.b], writes=[rstdf.b])
            for k in range(KC):
                STT(yT.t[:, k, :], xTf[s].t[:, k, :], ccol(CI["nf"] + k), rstdf.t[:], ALU.mult, ALU.mult,
                    [xTf[s].b, rstdf.b, cols.b], [yT.b])
            for g in range(4):
                for half in range(2):
                    bk = nb()
                    for kk in range(4):
                        k = half * 4 + kk
                        TR(banks[bk][:, kk * 128:(kk + 1) * 128], yT.t[:, k, g * 128:(g + 1) * 128], ident.t[:],
                           [yT.b, ident.b], [bbank[bk]])
                    CP("act" if half == 0 else "dve", yo[s].t[:, g, half * 512:(half + 1) * 512], banks[bk][:], [bbank[bk]], [yo[s].b])
            DMA("sp", y_out_v[t], yo[s].t[:], [yo[s].b], [], f"st_yo{s}")

        R.finalize()
        for n in R.dma_counts:
            sems[n] = es.enter_context(nc.semaphore(n))

        with nc.Block() as block:
            @block.tensor
            def _(h):
                R.emit("pe", h, sems)

            @block.scalar
            def _(h):
                R.emit("act", h, sems)

            @block.vector
            def _(h):
                R.emit("dve", h, sems)

            @block.gpsimd
            def _(h):
                R.emit("pool", h, sems)

            @block.sync
            def _(h):
                R.emit("sp", h, sems)
                for k_, v_ in R.dma_counts.items():
                    h.wait_ge(sems[k_], v_)
    return nc


POOL_WINDOWS = (2, 4, 8, 16)


def shared_inputs(inp, L):
    f32 = np.float32
    CI = col_index(L)
    cols = np.zeros((128, CI["n"]), f32)

    def colmaj(v):
        return np.ascontiguousarray(np.asarray(v, f32).reshape(-1, 128).T)

    for l in range(L):
        cols[:, CI["n1"] + l * 8:CI["n1"] + (l + 1) * 8] = colmaj(inp["norm1_g"][l])
        cols[:, CI["n2"] + l * 8:CI["n2"] + (l + 1) * 8] = colmaj(inp["norm2_g"][l])
        for j in range(3):
            cols[:, CI["cw"] + (l * 3 + j) * 22:CI["cw"] + (l * 3 + j + 1) * 22] = colmaj(inp["conv_w"][l, j])
        cols[:, CI["cb"] + l * 22:CI["cb"] + (l + 1) * 22] = colmaj(inp["conv_b"][l])
        cols[:, CI["gn"] + l * 4:CI["gn"] + (l + 1) * 4] = colmaj(inp["gla_norm_g"][l])
        cols[:, CI["ps"] + l * 2:CI["ps"] + (l + 1) * 2] = colmaj(inp["pool_scale"][l])
    cols[:, CI["nf"]:CI["nf"] + 8] = colmaj(inp["norm_f"])
    jj = np.arange(128)
    trif = (jj[:, None] <= jj[None, :]).astype(f32)
    trib = (jj[:, None] >= jj[None, :]).astype(f32)
    sh = {
        "ident": np.eye(128, dtype=f32),
        "maskf": np.ascontiguousarray(np.tile(trif, (1, 4))),
        "maskb": np.ascontiguousarray(np.tile(trib, (1, 4))),
        "cols": cols,
    }
    if L == 0:
        return sh
    sh["w_in"] = np.ascontiguousarray(inp["w_in"][:L], f32)
    sh["w_o"] = np.ascontiguousarray(inp["w_o"][:L], f32)
    sh["w_ffn_in"] = np.ascontiguousarray(inp["w_ffn_in"][:L], f32)
    sh["w_ffn_out"] = np.ascontiguousarray(inp["w_ffn_out"][:L], f32)
    w2aug = np.zeros((L, 33, 512), f32)
    w2aug[:, 0:16, 0:256] = inp["gla_gate_w2"][:L, 0]
    w2aug[:, 16:32, 256:512] = inp["gla_gate_w2"][:L, 1]
    w2aug[:, 32, 0:256] = inp["gla_gate_b"][:L, 0]
    w2aug[:, 32, 256:512] = inp["gla_gate_b"][:L, 1]
    sh["w2aug"] = w2aug
    sh["wsT"] = np.ascontiguousarray(np.transpose(np.asarray(inp["sgu_w"][:L], f32), (0, 3, 1, 2)))
    sb = np.asarray(inp["sgu_b"][:L], f32)
    sgub = np.zeros((L, 128, 2, 128), f32)
    for j in range(2):
        for e in range(2):
            sgub[:, e * 64:(e + 1) * 64, j, :] = sb[:, 2 * j + e][:, None, :]
    sh["sgu_bias"] = sgub
    sh["lng"] = np.ascontiguousarray(np.broadcast_to(np.asarray(inp["sgu_ln_g"][:L], f32)[:, None, :], (L, 128, 256)))
    sh["lnb"] = np.ascontiguousarray(np.broadcast_to(np.asarray(inp["sgu_ln_b"][:L], f32)[:, None, :], (L, 128, 256)))
    bands = np.zeros((128, 3, 4, 128), f32)
    s_ = jj[:, None]
    t_ = jj[None, :]
    for pg, w in enumerate(POOL_WINDOWS):
        hw = w // 2
        for wi, off in enumerate((-128, 0, 128)):
            ss = s_ + off
            bands[:, wi, pg, :] = ((ss >= t_ - hw) & (ss <= t_ + hw - 1)).astype(f32)
    sh["bands"] = bands
    pw = np.asarray(inp["pool_w"][:L], f32)
    poolw = np.zeros((L, 128, 2, 64), f32)
    for j in range(2):
        for e in range(2):
            poolw[:, e * 64:(e + 1) * 64, j, :] = pw[:, 2 * j + e]
    sh["poolw"] = poolw
    return sh


def core_inputs(blocks, T, L):
    f32 = np.float32
    NB = T // BLK
    NT = T // TILE
    NH = T // HB
    assert len(blocks) == NB
    x = np.zeros((T, D), f32)
    contL = np.zeros(NB, f32)
    contR = np.zeros(NB, f32)
    invc = np.ones((4, T), f32)
    for i, (sid, bi, nbs, xb_) in enumerate(blocks):
        if xb_ is not None:
            x[i * BLK:(i + 1) * BLK] = xb_
        if i > 0 and sid is not None and blocks[i - 1][0] == sid and blocks[i - 1][1] == bi - 1:
            contL[i] = 1.0
        if i + 1 < NB and sid is not None and blocks[i + 1][0] == sid and blocks[i + 1][1] == bi + 1:
            contR[i] = 1.0
        S = nbs * BLK
        pos = bi * BLK + np.arange(BLK)
        for pg, w in enumerate(POOL_WINDOWS):
            hw = w // 2
            lo = np.clip(pos - hw, 0, S)
            hi = np.clip(pos + hw, 0, S)
            invc[pg, i * BLK:(i + 1) * BLK] = 1.0 / (hi - lo).astype(f32)
    fTL = np.ones(NT, f32)
    fTR = np.ones(NT, f32)
    fHL = np.ones(NH, f32)
    fHR = np.ones(NH, f32)
    for b in range(NB):
        fTL[b * 4] = contL[b]
        fTR[b * 4 + 3] = contR[b]
        fHL[b * 2] = contL[b]
        fHR[b * 2 + 1] = contR[b]
    bc = lambda v: np.ascontiguousarray(np.broadcast_to(v[None, :], (128, v.shape[0])))
    d = {"x_tok": x, "flagT_L": bc(fTL), "flagT_R": bc(fTR), "flagH_L": bc(fHL), "flagH_R": bc(fHR)}
    if L > 0:
        ic = np.zeros((128, 2, T), f32)
        for j in range(2):
            for e in range(2):
                ic[e * 64:(e + 1) * 64, j, :] = invc[2 * j + e][None, :]
        d["inv_cnt"] = ic
    return d


_PROG = {}


def run_cores(inp, core_blocks, T, L, cfg_extra=None):
    cfg = dict(T=T, L=L)
    if cfg_extra:
        cfg.update(cfg_extra)
    key = tuple(sorted(cfg.items()))
    if key not in _PROG:
        _PROG[key] = build_program(cfg)
    nc = _PROG[key]
    sh = shared_inputs(inp, L)
    in_maps = []
    for blocks in core_blocks:
        m = dict(sh)
        m.update(core_inputs(blocks, T, L))
        in_maps.append(m)
    res = run_bass_kernel_spmd(nc, in_maps, core_ids=list(range(len(core_blocks))))
    return [r["y_tok"] for r in res.results]


def kernel(**inputs):
    inp = {k: np.asarray(v) for k, v in inputs.items()}
    xp = inp["x_prompt"]
    xs = inp["x_sample"]
    T = 16384
    NB = T // BLK
    core_blocks = []
    for c in range(2):
        core_blocks.append([(("s", c), b, NB, xs[c, b * BLK:(b + 1) * BLK]) for b in range(NB)])
    counts = [3, 3, 3, 3, 2, 2]
    nxt = 0
    owner = {}
    for c, n in enumerate(counts):
        bl = []
        for i in range(n):
            owner[nxt] = (c + 2, i)
            bl.append((("p", nxt), 0, 1, xp[nxt]))
            nxt += 1
        while len(bl) < NB:
            bl.append((None, 0, 1, None))
        core_blocks.append(bl)
    ys = run_cores(inp, core_blocks, T, DEPTH)
    y_prompt = np.zeros(xp.shape, np.float32)
    y_sample = np.zeros(xs.shape, np.float32)
    for c in range(2):
        y_sample[c] = ys[c]
    for sid, (c, i) in owner.items():
        y_prompt[sid] = ys[c][i * BLK:(i + 1) * BLK]
    return (y_prompt, y_sample)
```

```python
import numpy as np
import concourse.bass as bass
import concourse.mybir as mybir
from concourse.bass_utils import run_bass_kernel_spmd

F32 = mybir.dt.float32
BF16 = mybir.dt.bfloat16
AF = mybir.ActivationFunctionType
ALU = mybir.AluOpType

D = 1024
KC = 8
DEPTH = 4
BLK = 2048
TILE = 512
TILE2 = 256
GP = 2
EPS = 1e-6


class Buf:
    __slots__ = ("name", "writer", "readers", "excl")

    def __init__(self, name, excl=False):
        self.name = name
        self.writer = None
        self.readers = []
        self.excl = excl


class Op:
    __slots__ = ("eng", "fn", "deps", "semkey", "val", "needs_inc", "is_dma", "inc")


class Rec:
    ENGS = ("pe", "act", "dve", "pool", "sp")

    def __init__(self):
        self.ops = {e: [] for e in self.ENGS}
        self.dma_counts = {}
        self.last_dma = {}
        self.pending = {e: [] for e in self.ENGS}

    def _dep(self, op, d, raw):
        if d is None or d is op:
            return
        if not d.is_dma and d.eng == op.eng:
            if op.eng in ("pe", "sp") or not raw:
                return
        op.deps.append(d)
        d.needs_inc = True

    def add(self, eng, fn, reads=(), writes=(), dma_sem=None):
        op = Op()
        op.eng = eng
        op.fn = fn
        op.deps = []
        op.needs_inc = False
        op.is_dma = dma_sem is not None
        op.val = None
        if op.is_dma:
            c = self.dma_counts.get(dma_sem, 0) + 16
            self.dma_counts[dma_sem] = c
            op.semkey = dma_sem
            op.val = c
            op.inc = 16
            op.needs_inc = True
        else:
            op.semkey = eng
            op.inc = 1
        if self.pending[eng]:
            for d in self.pending[eng]:
                if d.is_dma or d.eng != eng:
                    op.deps.append(d)
                    d.needs_inc = True
            self.pending[eng] = []
        if op.is_dma:
            self.last_dma[dma_sem] = op
        for b in reads:
            self._dep(op, b.writer, True)
            if b.excl:
                for r in b.readers:
                    if r.eng != eng:
                        self._dep(op, r, False)
        for b in writes:
            self._dep(op, b.writer, False)
            for r in b.readers:
                self._dep(op, r, False)
        for b in reads:
            b.readers.append(op)
        for b in writes:
            b.writer = op
            b.readers = []
        self.ops[eng].append(op)
        return op

    def barrier(self):
        deps = []
        for e in self.ENGS:
            for op in reversed(self.ops[e]):
                if not op.is_dma:
                    deps.append(op)
                    break
        deps += list(self.last_dma.values())
        for e in self.ENGS:
            self.pending[e] = list(deps)

    def finalize(self):
        for e in self.ENGS:
            c = 0
            for op in self.ops[e]:
                if not op.is_dma and op.needs_inc:
                    c += 1
                    op.val = c

    def emit(self, eng, handle, sems):
        waited = {}
        for op in self.ops[eng]:
            need = {}
            for d in op.deps:
                v = need.get(d.semkey, 0)
                if d.val > v:
                    need[d.semkey] = d.val
            for k, v in need.items():
                if waited.get(k, 0) < v:
                    handle.wait_ge(sems[k], v)
                    waited[k] = v
            inst = op.fn(handle)
            if op.needs_inc:
                inst.then_inc(sems[op.semkey], op.inc)
        return waited


W_Q, W_K, W_V, W_G, W_LR, W_U, W_VS, W_XP = 0, 256, 512, 1024, 1536, 1568, 1824, 2080
INW = 2336
DFF = 2816
NF = 22
HB = 1024
WV = (342, 341, 341)
WS = (0, 342, 683)


class TK:
    __slots__ = ("t", "b")

    def __init__(self, t, name):
        self.t = t
        self.b = Buf(name)


class Arena:
    def __init__(self, t, nf32):
        self.t = t
        self.n = nf32
        self.off = 0

    def alloc(self, name, shape, dt=F32):
        nel = 1
        for s_ in shape[1:]:
            nel *= s_
        nbytes = nel * (2 if dt == BF16 else 4)
        nf = (nbytes + 3) // 4
        if self.off % 2:
            self.off += 1
        v = self.t[0:shape[0], self.off:self.off + nf]
        self.off += nf
        assert self.off <= self.n, (name, self.off, self.n)
        if dt == BF16:
            v = v.bitcast(BF16)[:, 0:nel]
        if len(shape) == 3:
            v = v.rearrange("p (a b) -> p a b", a=shape[1])
        elif len(shape) == 4:
            v = v.rearrange("p (a b c) -> p a b c", a=shape[1], b=shape[2])
        return TK(v, name)


def col_index(L):
    o = {}
    o["n1"] = 0
    o["n2"] = L * 8
    o["nf"] = 2 * L * 8
    o["cw"] = o["nf"] + 8
    o["cb"] = o["cw"] + L * 66
    o["gn"] = o["cb"] + L * 22
    o["ps"] = o["gn"] + L * 4
    o["n"] = o["ps"] + L * 2
    return o


def build_program(cfg):
    from contextlib import ExitStack
    T = cfg["T"]
    L = cfg["L"]
    NT = T // TILE
    NH = T // HB
    do_mixer = cfg.get("mixer", True)
    do_ffn = cfg.get("ffn", True)
    CI = col_index(L)
    nc = bass.Bass("TRN2", target_bir_lowering=False)
    R = Rec()

    def din(name, shape, dt=F32):
        return nc.dram_tensor(name, list(shape), dt, kind="ExternalInput").ap()

    def dscr(name, shape, dt=F32):
        return nc.dram_tensor(name, list(shape), dt, kind="Internal").ap()

    x_in = din("x_tok", [T, D])
    y_out = nc.dram_tensor("y_tok", [T, D], F32, kind="ExternalOutput").ap()
    ident_d = din("ident", [128, 128])
    maskf_d = din("maskf", [128, 512])
    maskb_d = din("maskb", [128, 512])
    cols_d = din("cols", [128, CI["n"]])
    fTL_d = din("flagT_L", [128, T // TILE2])
    fTR_d = din("flagT_R", [128, T // TILE2])
    fHL_d = din("flagH_L", [128, NH])
    fHR_d = din("flagH_R", [128, NH])
    if L > 0:
        invc_d = din("inv_cnt", [128, 2, T])
        w_in_d = din("w_in", [L, D, INW])
        w_o_d = din("w_o", [L, D, D])
        w_f1_d = din("w_ffn_in", [L, D, 2 * DFF])
        w_f2_d = din("w_ffn_out", [L, DFF, D])
        w2aug_d = din("w2aug", [L, 33, 512])
        wsT_d = din("wsT", [L, 128, 4, 128])
        sgub_d = din("sgu_bias", [L, 128, 2, 128])
        lng_d = din("lng", [L, 128, 256])
        lnb_d = din("lnb", [L, 128, 256])
        bands_d = din("bands", [128, 3, 4, 128])
        poolw_d = din("poolw", [L, 128, 2, 64])
    xa = dscr("xa_scr", [D, T + 2])
    xb = dscr("xb_scr", [D, T + 2])
    xa_v = xa.rearrange("(k p) t -> p k t", p=128)
    xb_v = xb.rearrange("(k p) t -> p k t", p=128)
    sbst_d = dscr("sbst_scr", [T // TILE2, 128, 256])
    stash_d = dscr("stash_scr", [T // TILE2, 128, 256], BF16)

    with ExitStack() as es:
        def sbt(name, shape, dt=F32):
            return TK(es.enter_context(nc.sbuf_tensor("s_" + name, list(shape), dt)), name)

        sems = {}
        for n in ["pe", "act", "dve", "pool"]:
            sems[n] = es.enter_context(nc.semaphore(n))
        banks = [es.enter_context(nc.psum_tensor(f"bank{i}", [128, 512], F32)) for i in range(8)]
        bbank = [Buf(f"bank{i}", excl=True) for i in range(8)]
        bctr = [0]

        def nb():
            i = bctr[0] % 8
            bctr[0] += 1
            return i

        def MM(out, lhsT, rhs, start, stop, rd, wr):
            R.add("pe", lambda h: h.matmul(out, lhsT=lhsT, rhs=rhs, start=start, stop=stop), reads=rd, writes=wr)

        def TR(out, in_, idn, rd, wr):
            R.add("pe", lambda h: h.transpose(out=out, in_=in_, identity=idn), reads=rd, writes=wr)

        def ACT(out, in_, func, rd, wr, scale=None, bias=None):
            kw = {}
            if scale is not None:
                kw["scale"] = scale
            if bias is not None:
                kw["bias"] = bias
            R.add("act", lambda h: h.activation(out=out, in_=in_, func=func, **kw), reads=rd, writes=wr)

        def CP(eng, out, in_, rd, wr):
            if eng == "act":
                R.add("act", lambda h: h.copy(out=out, in_=in_), reads=rd, writes=wr)
            else:
                R.add(eng, lambda h: h.tensor_copy(out=out, in_=in_), reads=rd, writes=wr)

        def TT(eng, out, in0, in1, op, rd, wr):
            R.add(eng, lambda h: h.tensor_tensor(out=out, in0=in0, in1=in1, op=op), reads=rd, writes=wr)

        def STT(out, in0, scalar, in1, op0, op1, rd, wr):
            R.add("dve", lambda h: h.scalar_tensor_tensor(out=out, in0=in0, scalar=scalar, in1=in1, op0=op0, op1=op1),
                  reads=rd, writes=wr)

        def TS(eng, out, in0, s1, op0, rd, wr, s2=None, op1=None):
            if op1 is None and eng == "pool" and op0 == ALU.mult:
                s2, op1 = 1.0, ALU.mult
            if op1 is None:
                R.add(eng, lambda h: h.tensor_scalar(out=out, in0=in0, scalar1=s1, scalar2=None, op0=op0), reads=rd, writes=wr)
            else:
                R.add(eng, lambda h: h.tensor_scalar(out=out, in0=in0, scalar1=s1, scalar2=s2, op0=op0, op1=op1),
                      reads=rd, writes=wr)

        def MSET(eng, ap, val, wr):
            R.add(eng, lambda h: h.memset(ap, val), writes=wr)

        def DMA(q, out, in_, rd, wr, sem, **kw):
            R.add(q, lambda h: h.dma_start(out=out, in_=in_, **kw), reads=rd, writes=wr, dma_sem=sem)

        ident = sbt("ident", [128, 128])
        ident_bf = sbt("ident_bf", [128, 128], BF16)
        maskf = sbt("maskf", [128, 512])
        maskb = sbt("maskb", [128, 512])
        ones_bf = sbt("ones_bf", [128, 128], BF16)
        epsc = sbt("epsc", [128, 1])
        onec = sbt("onec", [128, 1])
        cols = sbt("cols", [128, CI["n"]])
        fTL = sbt("fTL", [128, T // TILE2])
        fTR = sbt("fTR", [128, T // TILE2])
        fHL = sbt("fHL", [128, NH])
        fHR = sbt("fHR", [128, NH])
        zer = sbt("zer", [128, 256])
        for tk, d_ in ((ident, ident_d), (maskf, maskf_d), (maskb, maskb_d), (cols, cols_d), (fTL, fTL_d),
                       (fTR, fTR_d), (fHL, fHL_d), (fHR, fHR_d)):
            DMA("sp", tk.t[:], d_, [], [tk.b], "ld_" + tk.b.name)
        MSET("dve", ones_bf.t[:], 1.0, [ones_bf.b])
        MSET("dve", epsc.t[:], EPS, [epsc.b])
        MSET("dve", onec.t[:], 1.0, [onec.b])
        MSET("dve", zer.t[:], 0.0, [zer.b])
        CP("dve", ident_bf.t[:], ident.t[:], [ident.b], [ident_bf.b])
        trif = maskf.t[:, 0:128]
        trib = maskb.t[:, 0:128]
        for scr, nm in ((xa, "a"), (xb, "b")):
            for kk in range(KC):
                DMA("sp", scr[kk * 128:(kk + 1) * 128, 0:1], zer.t[:, 0:1], [zer.b], [], "st_zer",
                    allow_slow_non_contiguous=True)
                DMA("sp", scr[kk * 128:(kk + 1) * 128, T + 1:T + 2], zer.t[:, 0:1], [zer.b], [], "st_zer",
                    allow_slow_non_contiguous=True)

        def ccol(i):
            return cols.t[:, i:i + 1]

        ARN = 34500
        arena_t = es.enter_context(nc.sbuf_tensor("arena", [128, ARN], F32))

        def norm_tile(x3, xbuf, n, gbase, sq, rstd, out3, outbuf):
            ACT(sq.t[:, :, 0:n], x3, AF.Square, [xbuf], [sq.b])
            bk = nb()
            for k in range(KC):
                MM(banks[bk][:, 0:n], ones_bf.t[:], sq.t[:, k, 0:n], k == 0, k == KC - 1, [sq.b, ones_bf.b], [bbank[bk]])
            ACT(rstd.t[:, 0:n], banks[bk][:, 0:n], AF.Sqrt, [bbank[bk], epsc.b], [rstd.b], scale=1.0 / D, bias=epsc.t[:])
            R.add("dve", lambda h: h.reciprocal(out=rstd.t[:, 0:n], in_=rstd.t[:, 0:n]), reads=[rstd.b], writes=[rstd.b])
            for k in range(KC):
                STT(out3[:, k, :], x3[:, k, :], ccol(gbase + k), rstd.t[:, 0:n], ALU.mult, ALU.mult,
                    [xbuf, rstd.b, cols.b], [outbuf])

        A0 = Arena(arena_t, ARN)
        xin_t = [A0.alloc(f"xin{i}", [128, 4, D]) for i in range(2)]
        xT0 = [A0.alloc(f"xT0_{i}", [128, KC, TILE]) for i in range(2)]
        x_in_v = x_in.rearrange("(n g p) d -> n p g d", p=128, g=4)
        for t in range(NT):
            s = t % 2
            DMA("sp", xin_t[s].t[:], x_in_v[t], [], [xin_t[s].b], f"ld_xin{s}")
            for k in range(KC):
                bk = nb()
                for g in range(4):
                    TR(banks[bk][:, g * 128:(g + 1) * 128], xin_t[s].t[:, g, k * 128:(k + 1) * 128], ident.t[:],
                       [xin_t[s].b, ident.b], [bbank[bk]])
                CP("act" if k % 2 == 0 else "dve", xT0[s].t[:, k, :], banks[bk][:], [bbank[bk]], [xT0[s].b])
            DMA("sp", xa_v[:, :, 1 + t * TILE:1 + (t + 1) * TILE], xT0[s].t[:], [xT0[s].b], [], f"st_xT0_{s}")
        R.barrier()

        if L > 0:
            win = sbt("win", [128, KC, INW], BF16)
            wo = sbt("wo", [128, KC, D], BF16)
            w2aug = sbt("w2aug", [33, 512])
            wsT = sbt("wsT", [128, 4, 128], BF16)
            sgub = sbt("sgub", [128, 2, 128])
            lng = sbt("lng", [128, 256])
            lnb = sbt("lnb", [128, 256])
            bands = sbt("bands", [128, 3, 4, 128], BF16)
            poolw = sbt("poolw", [128, 2, 64], BF16)
            DMA("pool", bands.t[:], bands_d, [], [bands.b], "ld_bands")

        for l in range(L):
            if do_mixer:
                wv = w_in_d[l].rearrange("(k p) c -> p k c", p=128)
                for k in range(KC):
                    DMA("pool", win.t[:, k, :], wv[:, k, :], [], [win.b], "ld_win")
                wv = w_o_d[l].rearrange("(k p) c -> p k c", p=128)
                for k in range(0, KC, 2):
                    DMA("pool", wo.t[:, k:k + 2, :], wv[:, k:k + 2, :], [], [wo.b], "ld_wo")
                DMA("sp", w2aug.t[:], w2aug_d[l], [], [w2aug.b], "ld_w2aug")
                DMA("pool", wsT.t[:], wsT_d[l], [], [wsT.b], "ld_wsT")
                DMA("sp", sgub.t[:], sgub_d[l], [], [sgub.b], "ld_sgub")
                DMA("sp", lng.t[:], lng_d[l], [], [lng.b], "ld_lng")
                DMA("sp", lnb.t[:], lnb_d[l], [], [lnb.b], "ld_lnb")
                DMA("pool", poolw.t[:], poolw_d[l], [], [poolw.b], "ld_poolw")

            if do_mixer:
                A = Arena(arena_t, ARN)
                T2 = TILE2
                NT2 = T // T2

                class _Set:
                    pass

                sets = []
                for i_ in range(2):
                    S_ = _Set()
                    S_.i = i_
                    S_.qT = A.alloc(f"qT{i_}", [128, 2, T2])
                    S_.kT = A.alloc(f"kT{i_}", [128, 2, T2])
                    S_.lrT = A.alloc(f"lrT{i_}", [33, T2])
                    S_.sg = A.alloc(f"sg{i_}", [128, 4, T2])
                    S_.gu = A.alloc(f"gu{i_}", [128, 2, T2])
                    S_.xpT = A.alloc(f"xpT{i_}", [128, 2, T2])
                    S_.vtok = [A.alloc(f"vtok{i_}{g}", [128, 512], BF16) for g in range(GP)]
                    S_.vn = [A.alloc(f"vn{i_}{g}", [128, 256], BF16) for g in range(GP)]
                    S_.xpk = [A.alloc(f"xpk{i_}{g}", [128, 256], BF16) for g in range(GP)]
                    S_.xpl = A.alloc(f"xpl{i_}", [128, 256], BF16)
                    S_.xpr = A.alloc(f"xpr{i_}", [128, 256], BF16)
                    S_.stsh = A.alloc(f"stsh{i_}", [128, 256], BF16)
                    S_.invc = A.alloc(f"invc{i_}", [128, 2, T2])
                    S_.Sstg = A.alloc(f"Sstg{i_}", [128, 2, 128])
                    sets.append(S_)
                xT3 = [A.alloc(f"xT{i_}", [128, KC, T2]) for i_ in range(3)]

                def ld_x(t):
                    X_ = xT3[t % 3]
                    DMA("sp", X_.t[:], xa_v[:, :, 1 + t * T2:1 + (t + 1) * T2], [], [X_.b], f"ld_xT{t % 3}")

                hT = A.alloc("hT", [128, KC, T2], BF16)
                sq = hT
                rstd = A.alloc("rstd", [128, T2])
                e_tL = [A.alloc(f"e_t{g}", [128, 512]) for g in range(GP)]
                sp_tL = [A.alloc(f"sp_t{g}", [128, 512]) for g in range(GP)]
                EpL = [A.alloc(f"Ep{g}", [128, 4, 128]) for g in range(GP)]
                EmL = [A.alloc(f"Em{g}", [128, 4, 128]) for g in range(GP)]
                qd = [A.alloc(f"qd{g}", [128, 4, 128], BF16) for g in range(GP)]
                kd = [A.alloc(f"kd{g}", [128, 4, 128], BF16) for g in range(GP)]
                kdt = [A.alloc(f"kdt{g}", [128, 512], BF16) for g in range(GP)]
                egl = [A.alloc(f"egl{g}", [128, 4]) for g in range(GP)]
                AT = [A.alloc(f"AT{g}", [128, 8, 128], BF16) for g in range(GP)]
                Sf = A.alloc("Sf", [128, 2, 128])
                Sb = A.alloc("Sb", [128, 2, 128])
                Stmp = A.alloc("Stmp", [128, 2, 128])
                kvs = [[A.alloc(f"kvs{d_}{g}", [128, 2, 128]) for g in range(GP)] for d_ in range(2)]
                Sbf = [[A.alloc(f"Sbf{d_}{g}", [128, 2, 2, 128], BF16) for g in range(GP)] for d_ in range(2)]
                osbL = [A.alloc(f"osb{g}", [128, 512]) for g in range(GP)]
                osqL = [A.alloc(f"osq{g}", [128, 512], BF16) for g in range(GP)]
                rsL = [A.alloc(f"rs{g}", [128, 512]) for g in range(GP)]
                otmpL = [A.alloc(f"otmp{g}", [128, 512]) for g in range(GP)]
                vsg = A.alloc("vsg", [128, 256])
                vtmp = A.alloc("vtmp", [128, 256])
                stats = A.alloc("stats", [128, 8])
                mv = A.alloc("mv", [128, 4])
                stmp = A.alloc("stmp", [128, 2, 128])
                ptmp = A.alloc("ptmp", [128, 2, 128])
                dT = A.alloc("dT", [128, 2, T2], BF16)
                mixT = A.alloc("mixT", [128, KC, T2], BF16)

                for S_ in sets:
                    MSET("pool", S_.lrT.t[32:33, :], 1.0, [S_.lrT.b])
                for d_ in range(2):
                    for g in range(GP):
                        MSET("pool", Sbf[d_][g].t[:], 0.0, [Sbf[d_][g].b])

                def proj_fm(c0, m):
                    bk = nb()
                    for k in range(KC):
                        MM(banks[bk][0:m, 0:T2], win.t[:, k, c0:c0 + m], hT.t[:, k, :], k == 0, k == KC - 1,
                           [win.b, hT.b], [bbank[bk]])
                    return bk

                def proj_tm(g, c0, n):
                    bk = nb()
                    for k in range(KC):
                        MM(banks[bk][:, 0:n], hT.t[:, k, g * 128:(g + 1) * 128], win.t[:, k, c0:c0 + n], k == 0,
                           k == KC - 1, [win.b, hT.b], [bbank[bk]])
                    return bk

                def chunk_gen(S_, g, dirs, want_q, want_kv):
                    e_t, sp_t, Ep, Em = e_tL[g], sp_tL[g], EpL[g], EmL[g]
                    gc = slice(g * 128, (g + 1) * 128)
                    c0 = 0 if 0 in dirs else 256
                    n = 256 * len(dirs)
                    bk = nb()
                    MM(banks[bk][:, 0:n], S_.lrT.t[:, gc], w2aug.t[:, c0:c0 + n], True, True,
                       [S_.lrT.b, w2aug.b], [bbank[bk]])
                    ACT(e_t.t[:, 0:n], banks[bk][:, 0:n], AF.Exp, [bbank[bk]], [e_t.b], scale=-1.0)
                    ACT(sp_t.t[:, 0:n], e_t.t[:, 0:n], AF.Ln, [e_t.b, onec.b], [sp_t.b], bias=onec.t[:])
                    yield
                    bk2 = nb()
                    for i_, d_ in enumerate(dirs):
                        for hh in range(2):
                            bi = d_ * 2 + hh
                            MM(banks[bk2][:, bi * 128:(bi + 1) * 128], sp_t.t[:, (i_ * 2 + hh) * 128:(i_ * 2 + hh + 1) * 128],
                               trif if d_ == 0 else trib, True, True, [sp_t.b, maskf.b, maskb.b], [bbank[bk2]])
                    b0 = dirs[0] * 2
                    b1 = dirs[-1] * 2 + 2
                    gv = banks[bk2][:, b0 * 128:b1 * 128].rearrange("p (a b) -> p a b", b=128)
                    ACT(Ep.t[:, b0:b1, :], gv, AF.Exp, [bbank[bk2]], [Ep.b], scale=-1.0 / 16)
                    ACT(Em.t[:, b0:b1, :], gv, AF.Exp, [bbank[bk2]], [Em.b], scale=1.0 / 16)
                    if 0 in dirs:
                        CP("pool", egl[g].t[:, 0:2], Ep.t[:, 0:2, 127], [Ep.b], [egl[g].b])
                    if 1 in dirs:
                        CP("pool", egl[g].t[:, 2:4], Ep.t[:, 2:4, 0], [Ep.b], [egl[g].b])
                    yield
                    for d_ in dirs:
                        TT("dve", kd[g].t[:, d_ * 2:d_ * 2 + 2, :], S_.kT.t[:, :, gc],
                           Em.t[:, d_ * 2:d_ * 2 + 2, :], ALU.mult, [S_.kT.b, Em.b], [kd[g].b])
                        if want_q:
                            TT("dve", qd[g].t[:, d_ * 2:d_ * 2 + 2, :], S_.qT.t[:, :, gc], Ep.t[:, d_ * 2:d_ * 2 + 2, :], ALU.mult,
                               [S_.qT.b, Ep.b], [qd[g].b])
                    bk = nb()
                    bv = banks[bk][:].bitcast(BF16)
                    for bi in range(b0, b1):
                        TR(bv[:, bi * 128:(bi + 1) * 128], kd[g].t[:, bi, :], ident_bf.t[:], [kd[g].b, ident_bf.b], [bbank[bk]])
                    CP("act", kdt[g].t[:, b0 * 128:b1 * 128], bv[:, b0 * 128:b1 * 128], [bbank[bk]], [kdt[g].b])
                    yield
                    for d_ in dirs:
                        if not want_kv[d_]:
                            continue
                        bk = nb()
                        for hd in range(4):
                            p0 = (hd % 2) * 64
                            MM(banks[bk][p0:p0 + 64, (hd // 2) * 128:(hd // 2 + 1) * 128],
                               kdt[g].t[:, d_ * 256 + hd * 64:d_ * 256 + (hd + 1) * 64], S_.vtok[g].t[:, hd * 128:(hd + 1) * 128],
                               True, True, [kdt[g].b, S_.vtok[g].b], [bbank[bk]])
                        for hh in range(2):
                            ACT(kvs[d_][g].t[:, hh, :], banks[bk][:, hh * 128:(hh + 1) * 128], AF.Identity,
                                [bbank[bk], egl[g].b], [kvs[d_][g].b], scale=egl[g].t[:, d_ * 2 + hh:d_ * 2 + hh + 1])
                    yield
                    if want_q:
                        for e in range(2):
                            bk = nb()
                            p0 = e * 64
                            for bi in range(4):
                                MM(banks[bk][:, bi * 128:(bi + 1) * 128], kd[g].t[p0:p0 + 64, bi, :], qd[g].t[p0:p0 + 64, bi, :],
                                   True, True, [kd[g].b, qd[g].b], [bbank[bk]])
                            av = banks[bk][:].rearrange("p (a b) -> p a b", b=128)
                            TT("dve", AT[g].t[:, e * 4:e * 4 + 2, :], av[:, 0:2, :], maskf.t[:, 0:256].rearrange("p (a b) -> p a b", b=128),
                               ALU.mult, [bbank[bk], maskf.b], [AT[g].b])
                            TT("dve", AT[g].t[:, e * 4 + 2:e * 4 + 4, :], av[:, 2:4, :], maskb.t[:, 0:256].rearrange("p (a b) -> p a b", b=128),
                               ALU.mult, [bbank[bk], maskb.b], [AT[g].b])
                            yield

                def out_gen(S_, g):
                    osb, osq, rs, otmp = osbL[g], osqL[g], rsL[g], otmpL[g]
                    gc = slice(g * 128, (g + 1) * 128)
                    bk = nb()
                    for hd in range(4):
                        hh = hd // 2
                        e = hd % 2
                        o_ap = banks[bk][:, hd * 128:(hd + 1) * 128]
                        MM(o_ap, S_.vtok[g].t[:, hd * 128:(hd + 1) * 128], AT[g].t[:, e * 4 + 0 + hh, :], True, False,
                           [S_.vtok[g].b, AT[g].b], [bbank[bk]])
                        MM(o_ap, S_.vtok[g].t[:, hd * 128:(hd + 1) * 128], AT[g].t[:, e * 4 + 2 + hh, :], False, False,
                           [S_.vtok[g].b, AT[g].b], [bbank[bk]])
                        MM(o_ap, Sbf[0][g].t[:, hh, e, :], qd[g].t[:, 0 + hh, :], False, False,
                           [Sbf[0][g].b, qd[g].b], [bbank[bk]])
                        MM(o_ap, Sbf[1][g].t[:, hh, e, :], qd[g].t[:, 2 + hh, :], False, True,
                           [Sbf[1][g].b, qd[g].b], [bbank[bk]])
                    CP("act", osb.t[:], banks[bk][:], [bbank[bk]], [osb.b])
                    ACT(osq.t[:], banks[bk][:], AF.Square, [bbank[bk]], [osq.b])
                    yield
                    bk2 = nb()
                    MM(banks[bk2][:], ones_bf.t[:], osq.t[:], True, True, [osq.b, ones_bf.b], [bbank[bk2]])
                    ACT(rs.t[:], banks[bk2][:], AF.Sqrt, [bbank[bk2], epsc.b], [rs.b], scale=1.0 / 128, bias=epsc.t[:])
                    R.add("dve", lambda h: h.reciprocal(out=rs.t[:], in_=rs.t[:]), reads=[rs.b], writes=[rs.b])
                    TT("dve", otmp.t[:], osb.t[:], rs.t[:], ALU.mult, [osb.b, rs.b], [otmp.b])
                    yield
                    for hd in range(4):
                        STT(mixT.t[:, hd, gc], otmp.t[:, hd * 128:(hd + 1) * 128], ccol(CI["gn"] + l * 4 + hd), S_.sg.t[:, hd, gc],
                            ALU.mult, ALU.mult, [otmp.b, cols.b, S_.sg.b], [mixT.b])
                    yield

                def lockstep(gens):
                    gens = list(gens)
                    while gens:
                        for g_ in list(gens):
                            try:
                                next(g_)
                                yield
                            except StopIteration:
                                gens.remove(g_)

                def state_step(S, g, d_):
                    for hh in range(2):
                        STT(S.t[:, hh, :], S.t[:, hh, :], egl[g].t[:, d_ * 2 + hh:d_ * 2 + hh + 1], kvs[d_][g].t[:, hh, :],
                            ALU.mult, ALU.add, [S.b, egl[g].b, kvs[d_][g].b], [S.b])

                def state_to_bf(S, dst):
                    CP("act", dst.t[0:64, :, 0, :], S.t[0:64, :, :], [S.b], [dst.b])
                    CP("act", dst.t[64:128, :, 1, :], S.t[64:128, :, :], [S.b], [dst.b])

                def bank3(bk):
                    return banks[bk][:, 0:256].rearrange("p (a b) -> p a b", b=128)

                def pipeline(order, genA, genB):
                    for _ in genA(order[0]):
                        pass
                    for i_, t in enumerate(order):
                        gb = genB(t)
                        ga = genA(order[i_ + 1]) if i_ + 1 < len(order) else iter(())
                        tog = 0
                        for hint in gb:
                            if hint is None:
                                tog ^= 1
                                hint = tog
                            for _ in range(hint):
                                try:
                                    next(ga)
                                except StopIteration:
                                    break
                        for _ in ga:
                            pass

                def s1A(t):
                    S_ = sets[t % 2]
                    X_ = xT3[t % 3]
                    if t - 1 >= 0:
                        ld_x(t - 1)
                    norm_tile(X_.t[:], X_.b, T2, CI["n1"] + l * 8, sq, rstd, hT.t, hT.b)
                    yield
                    for c in range(2):
                        bk = proj_fm(W_K + c * 128, 128)
                        CP("act", S_.kT.t[:, c, :], banks[bk][:, 0:T2], [bbank[bk]], [S_.kT.b])
                    bk = proj_fm(W_LR, 32)
                    CP("act", S_.lrT.t[0:32, :], banks[bk][0:32, 0:T2], [bbank[bk]], [S_.lrT.b])
                    yield
                    for g in range(GP):
                        bk = proj_tm(g, W_V, 512)
                        CP("dve", S_.vtok[g].t[:], banks[bk][:], [bbank[bk]], [S_.vtok[g].b])
                        yield
                    if t >= 1:
                        bk = proj_tm(0, W_XP, 256)
                        TS("dve", S_.stsh.t[:], banks[bk][:, 0:256], fTR.t[:, t - 1:t], ALU.mult, [bbank[bk], fTR.b], [S_.stsh.b])
                        DMA("sp", stash_d[t - 1], S_.stsh.t[:], [S_.stsh.b], [], f"st_stsh{S_.i}")

                def s1B(t):
                    S_ = sets[t % 2]
                    TS("dve", Sb.t[:], Sb.t[:], fTR.t[:, t:t + 1], ALU.mult, [Sb.b, fTR.b], [Sb.b])
                    CP("pool", S_.Sstg.t[:], Sb.t[:], [Sb.b], [S_.Sstg.b])
                    DMA("sp", sbst_d[t].rearrange("p (a b) -> p a b", b=128), S_.Sstg.t[:], [S_.Sstg.b], [], f"st_Sstg{S_.i}")
                    yield from lockstep([chunk_gen(S_, g, (1,), False, {1: True}) for g in reversed(range(GP))])
                    for g in reversed(range(GP)):
                        state_step(Sb, g, 1)
                    yield 2

                MSET("dve", Sb.t[:], 0.0, [Sb.b])
                DMA("sp", stash_d[NT2 - 1], zer.t[:, 0:128].bitcast(BF16), [zer.b], [], "st_zer")
                if cfg.get("s1", 9) > 0:
                    ld_x(NT2 - 1)
                    pipeline(list(reversed(range(NT2))), s1A, s1B)
                R.barrier()

                def s2A(t):
                    S_ = sets[t % 2]
                    O_ = sets[1 - t % 2]
                    sl = S_.i
                    X_ = xT3[t % 3]
                    if t + 1 < NT2:
                        ld_x(t + 1)
                    DMA("sp", S_.invc.t[:], invc_d[:, :, t * T2:(t + 1) * T2], [], [S_.invc.b], f"ld_invc{sl}")
                    DMA("sp", S_.Sstg.t[:], sbst_d[t].rearrange("p (a b) -> p a b", b=128), [], [S_.Sstg.b], f"ld_Sstg{sl}")
                    DMA("sp", S_.xpr.t[:], stash_d[t], [], [S_.xpr.b], f"ld_xpr{sl}")
                    if t >= 1:
                        TS("pool", S_.xpl.t[:], O_.xpk[GP - 1].t[:], fTL.t[:, t:t + 1], ALU.mult, [O_.xpk[GP - 1].b, fTL.b], [S_.xpl.b])
                    else:
                        MSET("pool", S_.xpl.t[:], 0.0, [S_.xpl.b])
                    norm_tile(X_.t[:], X_.b, T2, CI["n1"] + l * 8, sq, rstd, hT.t, hT.b)
                    yield
                    for c in range(2):
                        bk = proj_fm(W_Q + c * 128, 128)
                        ACT(S_.qT.t[:, c, :], banks[bk][:, 0:T2], AF.Copy, [bbank[bk]], [S_.qT.b], scale=0.125)
                    for c in range(2):
                        bk = proj_fm(W_K + c * 128, 128)
                        CP("act", S_.kT.t[:, c, :], banks[bk][:, 0:T2], [bbank[bk]], [S_.kT.b])
                    yield
                    bk = proj_fm(W_LR, 32)
                    CP("act", S_.lrT.t[0:32, :], banks[bk][0:32, 0:T2], [bbank[bk]], [S_.lrT.b])
                    for c in range(4):
                        bk = proj_fm(W_G + c * 128, 128)
                        ACT(S_.sg.t[:, c, :], banks[bk][:, 0:T2], AF.Silu, [bbank[bk]], [S_.sg.b])
                        if c == 1:
                            yield
                    yield
                    for c in range(2):
                        bk = proj_fm(W_U + c * 128, 128)
                        ACT(S_.gu.t[:, c, :], banks[bk][:, 0:T2], AF.Gelu_apprx_tanh, [bbank[bk]], [S_.gu.b])
                    for c in range(2):
                        bk = proj_fm(W_XP + c * 128, 128)
                        CP("act", S_.xpT.t[:, c, :], banks[bk][:, 0:T2], [bbank[bk]], [S_.xpT.b])
                    yield
                    for g in range(GP):
                        bk = proj_tm(g, W_V, 512)
                        CP("dve", S_.vtok[g].t[:], banks[bk][:], [bbank[bk]], [S_.vtok[g].b])
                        yield
                        bk = proj_tm(g, W_VS, 512)
                        ACT(vsg.t[:], banks[bk][:, 0:256], AF.Gelu_apprx_tanh, [bbank[bk]], [vsg.b])
                        CP("act", S_.xpk[g].t[:], banks[bk][:, 256:512], [bbank[bk]], [S_.xpk[g].b])
                        R.add("dve", lambda h: h.bn_stats(out=stats.t[:, 0:6], in_=vsg.t[:]), reads=[vsg.b], writes=[stats.b])
                        R.add("dve", lambda h: h.bn_aggr(out=mv.t[:, 0:2], in_=stats.t[:, 0:6]), reads=[stats.b], writes=[mv.b])
                        ACT(mv.t[:, 2:3], mv.t[:, 1:2], AF.Sqrt, [mv.b, epsc.b], [mv.b], bias=epsc.t[:])
                        R.add("dve", lambda h: h.reciprocal(out=mv.t[:, 3:4], in_=mv.t[:, 2:3]), reads=[mv.b], writes=[mv.b])
                        TS("dve", vtmp.t[:], vsg.t[:], mv.t[:, 0:1], ALU.subtract, [vsg.b, mv.b], [vtmp.b], s2=mv.t[:, 3:4], op1=ALU.mult)
                        TT("pool", vtmp.t[:], vtmp.t[:], lng.t[:], ALU.mult, [vtmp.b, lng.b], [vtmp.b])
                        TT("pool", S_.vn[g].t[:], vtmp.t[:], lnb.t[:], ALU.add, [vtmp.b, lnb.b], [S_.vn[g].b])
                        yield

                def s2B(t):
                    S_ = sets[t % 2]
                    sl = S_.i
                    X_ = xT3[t % 3]
                    for g in range(GP):
                        gc = slice(g * 128, (g + 1) * 128)
                        bk = nb()
                        for hd in range(4):
                            p0 = (hd % 2) * 64
                            MM(banks[bk][p0:p0 + 64, (hd // 2) * 128:(hd // 2 + 1) * 128], S_.vn[g].t[:, hd * 64:(hd + 1) * 64],
                               wsT.t[:, hd, :], True, True, [S_.vn[g].b, wsT.b], [bbank[bk]])
                        TT("dve", stmp.t[:], bank3(bk), sgub.t[:], ALU.add, [bbank[bk], sgub.b], [stmp.b])
                        TT("dve", mixT.t[:, 4:6, gc], stmp.t[:], S_.gu.t[:, :, gc], ALU.mult, [stmp.b, S_.gu.b], [mixT.b])
                    yield
                    for g in range(GP):
                        gc = slice(g * 128, (g + 1) * 128)
                        prv = S_.xpl if g == 0 else S_.xpk[g - 1]
                        nxt = S_.xpr if g == GP - 1 else S_.xpk[g + 1]
                        cur = S_.xpk[g]
                        bk = nb()
                        for pg in range(4):
                            p0 = (pg % 2) * 64
                            o_ap = banks[bk][p0:p0 + 64, (pg // 2) * 128:(pg // 2 + 1) * 128]
                            for wi, src in enumerate((prv, cur, nxt)):
                                MM(o_ap, src.t[:, pg * 64:(pg + 1) * 64], bands.t[:, wi, pg, :], wi == 0, wi == 2,
                                   [src.b, bands.b], [bbank[bk]])
                        TT("dve", ptmp.t[:], bank3(bk), S_.invc.t[:, :, gc], ALU.mult, [bbank[bk], S_.invc.b], [ptmp.b])
                        TT("dve", dT.t[:, :, gc], ptmp.t[:], S_.xpT.t[:, :, gc], ALU.subtract, [ptmp.b, S_.xpT.b], [dT.b])
                    yield
                    yield from lockstep([chunk_gen(S_, g, (0, 1), True, {0: True, 1: g >= 1}) for g in range(GP)])
                    for j in range(2):
                        bk = nb()
                        for e in range(2):
                            p0 = e * 64
                            MM(banks[bk][p0:p0 + 64, 0:T2], poolw.t[p0:p0 + 64, j, :], dT.t[p0:p0 + 64, j, :], True, True,
                               [poolw.b, dT.b], [bbank[bk]])
                        ACT(mixT.t[:, 6 + j, :], banks[bk][:, 0:T2], AF.Identity, [bbank[bk], cols.b], [mixT.b],
                            scale=ccol(CI["ps"] + l * 2 + j))
                    if t >= 1:
                        TS("dve", Sf.t[:], Sf.t[:], fTL.t[:, t:t + 1], ALU.mult, [Sf.b, fTL.b], [Sf.b])
                    for g in range(GP):
                        state_to_bf(Sf, Sbf[0][g])
                        state_step(Sf, g, 0)
                    CP("dve", Sb.t[:], S_.Sstg.t[:], [S_.Sstg.b], [Sb.b])
                    state_to_bf(Sb, Sbf[1][GP - 1])
                    for g in range(GP - 1, 0, -1):
                        state_step(Sb, g, 1)
                        state_to_bf(Sb, Sbf[1][g - 1])
                    yield 3
                    yield from lockstep([out_gen(S_, g) for g in range(GP)])
                    for dch in range(KC):
                        bk = nb()
                        for k in range(KC):
                            MM(banks[bk][:, 0:T2], wo.t[:, k, dch * 128:(dch + 1) * 128], mixT.t[:, k, :], k == 0, k == KC - 1,
                               [wo.b, mixT.b], [bbank[bk]])
                        TT("dve", X_.t[:, dch, :], X_.t[:, dch, :], banks[bk][:, 0:T2], ALU.add, [X_.b, bbank[bk]], [X_.b])
                        if dch % 2 == 1:
                            yield
                    DMA("sp", xb_v[:, :, 1 + t * T2:1 + (t + 1) * T2], X_.t[:], [X_.b], [], f"st_xT{t % 3}")

                MSET("dve", Sf.t[:], 0.0, [Sf.b])
                if cfg.get("s2", 9) > 0:
                    ld_x(0)
                    pipeline(list(range(NT2)), s2A, s2B)
                R.barrier()
            else:
                A = Arena(arena_t, ARN)
                xT = [A.alloc(f"xT{i}", [128, KC, TILE]) for i in range(2)]
                for t in range(NT):
                    s = t % 2
                    DMA("sp", xT[s].t[:], xa_v[:, :, 1 + t * TILE:1 + (t + 1) * TILE], [], [xT[s].b], f"ld_xT{s}")
                    DMA("sp", xb_v[:, :, 1 + t * TILE:1 + (t + 1) * TILE], xT[s].t[:], [xT[s].b], [], f"st_xT{s}")
                R.barrier()

            if do_ffn:
                A = Arena(arena_t, ARN)
                h2T = A.alloc("h2T", [128, KC, HB + 2], BF16)
                gT = A.alloc("gT", [128, NF, HB], BF16)
                xw = [A.alloc(f"xw{i}", [128, KC, 342]) for i in range(2)]
                sq3 = A.alloc("sq3", [128, KC, 342], BF16)
                rstd3 = A.alloc("rstd3", [128, 342])
                w1a = [A.alloc(f"w1a{i}", [128, KC, 256], BF16) for i in range(2)]
                w1u = [A.alloc(f"w1u{i}", [128, KC, 256], BF16) for i in range(2)]
                w2c = [A.alloc(f"w2c{i}", [128, NF, 128], BF16) for i in range(2)]
                xd = [A.alloc(f"xd{i}", [128, HB]) for i in range(2)]
                t1 = [A.alloc(f"t1_{i}", [128, 344]) for i in range(4)]
                t2 = [A.alloc(f"t2_{i}", [128, 344]) for i in range(4)]
                w1v = w_f1_d[l].rearrange("(k p) c -> p k c", p=128)
                w2v = w_f2_d[l].rearrange("(f p) c -> p f c", p=128)
                cnt3 = 0
                for hb in range(NH):
                    cb = hb * HB
                    for w in range(3):
                        s = (hb * 3 + w) % 2
                        DMA("sp", xw[s].t[:], xb_v[:, :, cb + 342 * w:cb + 342 * (w + 1)], [], [xw[s].b], f"ld_xw{s}")
                        norm_tile(xw[s].t[:], xw[s].b, 342, CI["n2"] + l * 8, sq3, rstd3, h2T.t[:, :, 342 * w:342 * (w + 1)], h2T.b)
                    for fp in range(NF // 2):
                        ws_ = (hb * (NF // 2) + fp) % 2
                        for k in range(0, KC, 4):
                            DMA("pool", w1a[ws_].t[:, k:k + 4, :], w1v[:, k:k + 4, fp * 256:(fp + 1) * 256], [], [w1a[ws_].b], f"ld_w1a{ws_}")
                            DMA("pool", w1u[ws_].t[:, k:k + 4, :], w1v[:, k:k + 4, DFF + fp * 256:DFF + (fp + 1) * 256], [], [w1u[ws_].b], f"ld_w1u{ws_}")
                        for fi in range(2):
                            f = fp * 2 + fi
                            for w in range(3):
                                nv = WV[w]
                                v0 = WS[w]
                                ts_ = cnt3 % 4
                                cnt3 += 1
                                bka = nb()
                                for k in range(KC):
                                    MM(banks[bka][:, 0:nv + 2], w1a[ws_].t[:, k, fi * 128:(fi + 1) * 128], h2T.t[:, k, v0:v0 + nv + 2],
                                       k == 0, k == KC - 1, [w1a[ws_].b, h2T.b], [bbank[bka]])
                                bku = nb()
                                for k in range(KC):
                                    MM(banks[bku][:, 0:nv], w1u[ws_].t[:, k, fi * 128:(fi + 1) * 128], h2T.t[:, k, v0 + 1:v0 + 1 + nv],
                                       k == 0, k == KC - 1, [w1u[ws_].b, h2T.b], [bbank[bku]])
                                if w == 0:
                                    TS("dve", banks[bka][:, 0:1], banks[bka][:, 0:1], fHL.t[:, hb:hb + 1], ALU.mult,
                                       [bbank[bka], fHL.b], [bbank[bka]])
                                if w == 2:
                                    TS("dve", banks[bka][:, nv + 1:nv + 2], banks[bka][:, nv + 1:nv + 2], fHR.t[:, hb:hb + 1], ALU.mult,
                                       [bbank[bka], fHR.b], [bbank[bka]])
                                cwi = CI["cw"] + l * 66
                                ACT(t1[ts_].t[:, 0:nv], banks[bka][:, 1:nv + 1], AF.Identity, [bbank[bka], cols.b], [t1[ts_].b],
                                    scale=ccol(cwi + 22 + f), bias=ccol(CI["cb"] + l * 22 + f))
                                STT(t2[ts_].t[:, 0:nv], banks[bka][:, 0:nv], ccol(cwi + f), t1[ts_].t[:, 0:nv], ALU.mult, ALU.add,
                                    [bbank[bka], cols.b, t1[ts_].b], [t2[ts_].b])
                                STT(t1[ts_].t[:, 0:nv], banks[bka][:, 2:nv + 2], ccol(cwi + 44 + f), t2[ts_].t[:, 0:nv], ALU.mult, ALU.add,
                                    [bbank[bka], cols.b, t2[ts_].b], [t1[ts_].b])
                                ACT(t2[ts_].t[:, 0:nv], t1[ts_].t[:, 0:nv], AF.Silu, [t1[ts_].b], [t2[ts_].b])
                                TT("dve", gT.t[:, f, v0:v0 + nv], t2[ts_].t[:, 0:nv], banks[bku][:, 0:nv], ALU.mult,
                                   [t2[ts_].b, bbank[bku]], [gT.b])
                    def ld_b(dch):
                        ds__ = dch % 2
                        DMA("pool", w2c[ds__].t[:], w2v[:, :, dch * 128:(dch + 1) * 128], [], [w2c[ds__].b], f"ld_w2c{ds__}")
                        DMA("sp", xd[ds__].t[:], xb[dch * 128:(dch + 1) * 128, 1 + cb:1 + cb + HB], [], [xd[ds__].b], f"ld_xd{ds__}")

                    ld_b(0)
                    for dch in range(KC):
                        ds_ = dch % 2
                        if dch + 1 < KC:
                            ld_b(dch + 1)
                        for q in range(2):
                            bk = nb()
                            for f in range(NF):
                                MM(banks[bk][:], w2c[ds_].t[:, f, :], gT.t[:, f, q * 512:(q + 1) * 512], f == 0, f == NF - 1,
                                   [w2c[ds_].b, gT.b], [bbank[bk]])
                            TT("dve", xd[ds_].t[:, q * 512:(q + 1) * 512], xd[ds_].t[:, q * 512:(q + 1) * 512], banks[bk][:], ALU.add,
                               [xd[ds_].b, bbank[bk]], [xd[ds_].b])
                        DMA("sp", xa[dch * 128:(dch + 1) * 128, 1 + cb:1 + cb + HB], xd[ds_].t[:], [xd[ds_].b], [], f"st_xd{ds_}")
                R.barrier()
            else:
                A = Arena(arena_t, ARN)
                xTc = [A.alloc(f"xTc{i}", [128, KC, TILE]) for i in range(2)]
                for t in range(NT):
                    s = t % 2
                    DMA("sp", xTc[s].t[:], xb_v[:, :, 1 + t * TILE:1 + (t + 1) * TILE], [], [xTc[s].b], f"ld_xTc{s}")
                    DMA("sp", xa_v[:, :, 1 + t * TILE:1 + (t + 1) * TILE], xTc[s].t[:], [xTc[s].b], [], f"st_xTc{s}")
                R.barrier()

        A = Arena(arena_t, ARN)
        xTf = [A.alloc(f"xTf{i}", [128, KC, TILE]) for i in range(2)]
        sqf = A.alloc("sqf", [128, KC, TILE], BF16)
        rstdf = A.alloc("rstdf", [128, TILE])
        yT = A.alloc("yT", [128, KC, TILE])
        yo = [A.alloc(f"yo{i}", [128, 4, D]) for i in range(2)]
        y_out_v = y_out.rearrange("(n g p) d -> n p g d", p=128, g=4)
        for t in range(NT):
            s = t % 2
            DMA("sp", xTf[s].t[:], xa_v[:, :, 1 + t * TILE:1 + (t + 1) * TILE], [], [xTf[s].b], f"ld_xTf{s}")
            ACT(sqf.t[:], xTf[s].t[:], AF.Square, [xTf[s].b], [sqf.b])
            bk = nb()
            for k in range(KC):
                MM(banks[bk][:], ones_bf.t[:], sqf.t[:, k, :], k == 0, k == KC - 1, [sqf.b, ones_bf.b], [bbank[bk]])
            ACT(rstdf.t[:], banks[bk][:], AF.Sqrt, [bbank[bk], epsc.b], [rstdf.b], scale=1.0 / D, bias=epsc.t[:])
            R.add("dve", lambda h: h.reciprocal(out=rstdf.t[:], in_=rstdf.t[:]), reads=[rstdf.b], writes=[rstdf.b])
            for k in range(KC):
                STT(yT.t[:, k, :], xTf[s].t[:, k, :], ccol(CI["nf"] + k), rstdf.t[:], ALU.mult, ALU.mult,
                    [xTf[s].b, rstdf.b, cols.b], [yT.b])
            for g in range(4):
                for half in range(2):
                    bk = nb()
                    for kk in range(4):
                        k = half * 4 + kk
                        TR(banks[bk][:, kk * 128:(kk + 1) * 128], yT.t[:, k, g * 128:(g + 1) * 128], ident.t[:],
                           [yT.b, ident.b], [bbank[bk]])
                    CP("act" if half == 0 else "dve", yo[s].t[:, g, half * 512:(half + 1) * 512], banks[bk][:], [bbank[bk]], [yo[s].b])
            DMA("sp", y_out_v[t], yo[s].t[:], [yo[s].b], [], f"st_yo{s}")

        R.finalize()
        for n in R.dma_counts:
            sems[n] = es.enter_context(nc.semaphore(n))

        with nc.Block() as block:
            @block.tensor
            def _(h):
                R.emit("pe", h, sems)

            @block.scalar
            def _(h):
                R.emit("act", h, sems)

            @block.vector
            def _(h):
                R.emit("dve", h, sems)

            @block.gpsimd
            def _(h):
                R.emit("pool", h, sems)

            @block.sync
            def _(h):
                R.emit("sp", h, sems)
                for k_, v_ in R.dma_counts.items():
                    h.wait_ge(sems[k_], v_)
    return nc


POOL_WINDOWS = (2, 4, 8, 16)


def shared_inputs(inp, L):
    f32 = np.float32
    CI = col_index(L)
    cols = np.zeros((128, CI["n"]), f32)

    def colmaj(v):
        return np.ascontiguousarray(np.asarray(v, f32).reshape(-1, 128).T)

    for l in range(L):
        cols[:, CI["n1"] + l * 8:CI["n1"] + (l + 1) * 8] = colmaj(inp["norm1_g"][l])
        cols[:, CI["n2"] + l * 8:CI["n2"] + (l + 1) * 8] = colmaj(inp["norm2_g"][l])
        for j in range(3):
            cols[:, CI["cw"] + (l * 3 + j) * 22:CI["cw"] + (l * 3 + j + 1) * 22] = colmaj(inp["conv_w"][l, j])
        cols[:, CI["cb"] + l * 22:CI["cb"] + (l + 1) * 22] = colmaj(inp["conv_b"][l])
        cols[:, CI["gn"] + l * 4:CI["gn"] + (l + 1) * 4] = colmaj(inp["gla_norm_g"][l])
        cols[:, CI["ps"] + l * 2:CI["ps"] + (l + 1) * 2] = colmaj(inp["pool_scale"][l])
    cols[:, CI["nf"]:CI["nf"] + 8] = colmaj(inp["norm_f"])
    jj = np.arange(128)
    trif = (jj[:, None] <= jj[None, :]).astype(f32)
    trib = (jj[:, None] >= jj[None, :]).astype(f32)
    sh = {
        "ident": np.eye(128, dtype=f32),
        "maskf": np.ascontiguousarray(np.tile(trif, (1, 4))),
        "maskb": np.ascontiguousarray(np.tile(trib, (1, 4))),
        "cols": cols,
    }
    if L == 0:
        return sh
    sh["w_in"] = np.ascontiguousarray(inp["w_in"][:L], f32)
    sh["w_o"] = np.ascontiguousarray(inp["w_o"][:L], f32)
    sh["w_ffn_in"] = np.ascontiguousarray(inp["w_ffn_in"][:L], f32)
    sh["w_ffn_out"] = np.ascontiguousarray(inp["w_ffn_out"][:L], f32)
    w2aug = np.zeros((L, 33, 512), f32)
    w2aug[:, 0:16, 0:256] = inp["gla_gate_w2"][:L, 0]
    w2aug[:, 16:32, 256:512] = inp["gla_gate_w2"][:L, 1]
    w2aug[:, 32, 0:256] = inp["gla_gate_b"][:L, 0]
    w2aug[:, 32, 256:512] = inp["gla_gate_b"][:L, 1]
    sh["w2aug"] = w2aug
    sh["wsT"] = np.ascontiguousarray(np.transpose(np.asarray(inp["sgu_w"][:L], f32), (0, 3, 1, 2)))
    sb = np.asarray(inp["sgu_b"][:L], f32)
    sgub = np.zeros((L, 128, 2, 128), f32)
    for j in range(2):
        for e in range(2):
            sgub[:, e * 64:(e + 1) * 64, j, :] = sb[:, 2 * j + e][:, None, :]
    sh["sgu_bias"] = sgub
    sh["lng"] = np.ascontiguousarray(np.broadcast_to(np.asarray(inp["sgu_ln_g"][:L], f32)[:, None, :], (L, 128, 256)))
    sh["lnb"] = np.ascontiguousarray(np.broadcast_to(np.asarray(inp["sgu_ln_b"][:L], f32)[:, None, :], (L, 128, 256)))
    bands = np.zeros((128, 3, 4, 128), f32)
    s_ = jj[:, None]
    t_ = jj[None, :]
    for pg, w in enumerate(POOL_WINDOWS):
        hw = w // 2
        for wi, off in enumerate((-128, 0, 128)):
            ss = s_ + off
            bands[:, wi, pg, :] = ((ss >= t_ - hw) & (ss <= t_ + hw - 1)).astype(f32)
    sh["bands"] = bands
    pw = np.asarray(inp["pool_w"][:L], f32)
    poolw = np.zeros((L, 128, 2, 64), f32)
    for j in range(2):
        for e in range(2):
            poolw[:, e * 64:(e + 1) * 64, j, :] = pw[:, 2 * j + e]
    sh["poolw"] = poolw
    return sh


def core_inputs(blocks, T, L):
    f32 = np.float32
    NB = T // BLK
    NT = T // TILE
    NH = T // HB
    assert len(blocks) == NB
    x = np.zeros((T, D), f32)
    contL = np.zeros(NB, f32)
    contR = np.zeros(NB, f32)
    invc = np.ones((4, T), f32)
    for i, (sid, bi, nbs, xb_) in enumerate(blocks):
        if xb_ is not None:
            x[i * BLK:(i + 1) * BLK] = xb_
        if i > 0 and sid is not None and blocks[i - 1][0] == sid and blocks[i - 1][1] == bi - 1:
            contL[i] = 1.0
        if i + 1 < NB and sid is not None and blocks[i + 1][0] == sid and blocks[i + 1][1] == bi + 1:
            contR[i] = 1.0
        S = nbs * BLK
        pos = bi * BLK + np.arange(BLK)
        for pg, w in enumerate(POOL_WINDOWS):
            hw = w // 2
            lo = np.clip(pos - hw, 0, S)
            hi = np.clip(pos + hw, 0, S)
            invc[pg, i * BLK:(i + 1) * BLK] = 1.0 / (hi - lo).astype(f32)
    NT2 = T // TILE2
    TPB = BLK // TILE2
    fTL = np.ones(NT2, f32)
    fTR = np.ones(NT2, f32)
    fHL = np.ones(NH, f32)
    fHR = np.ones(NH, f32)
    for b in range(NB):
        fTL[b * TPB] = contL[b]
        fTR[b * TPB + TPB - 1] = contR[b]
        fHL[b * 2] = contL[b]
        fHR[b * 2 + 1] = contR[b]
    bc = lambda v: np.ascontiguousarray(np.broadcast_to(v[None, :], (128, v.shape[0])))
    d = {"x_tok": x, "flagT_L": bc(fTL), "flagT_R": bc(fTR), "flagH_L": bc(fHL), "flagH_R": bc(fHR)}
    if L > 0:
        ic = np.zeros((128, 2, T), f32)
        for j in range(2):
            for e in range(2):
                ic[e * 64:(e + 1) * 64, j, :] = invc[2 * j + e][None, :]
        d["inv_cnt"] = ic
    return d


_PROG = {}


def run_cores(inp, core_blocks, T, L, cfg_extra=None):
    cfg = dict(T=T, L=L)
    if cfg_extra:
        cfg.update(cfg_extra)
    key = tuple(sorted(cfg.items()))
    if key not in _PROG:
        _PROG[key] = build_program(cfg)
    nc = _PROG[key]
    sh = shared_inputs(inp, L)
    in_maps = []
    for blocks in core_blocks:
        m = dict(sh)
        m.update(core_inputs(blocks, T, L))
        in_maps.append(m)
    res = run_bass_kernel_spmd(nc, in_maps, core_ids=list(range(len(core_blocks))))
    return [r["y_tok"] for r in res.results]


def kernel(**inputs):
    inp = {k: np.asarray(v) for k, v in inputs.items()}
    xp = inp["x_prompt"]
    xs = inp["x_sample"]
    T = 16384
    NB = T // BLK
    core_blocks = []
    for c in range(2):
        core_blocks.append([(("s", c), b, NB, xs[c, b * BLK:(b + 1) * BLK]) for b in range(NB)])
    counts = [3, 3, 3, 3, 2, 2]
    nxt = 0
    owner = {}
    for c, n in enumerate(counts):
        bl = []
        for i in range(n):
            owner[nxt] = (c + 2, i)
            bl.append((("p", nxt), 0, 1, xp[nxt]))
            nxt += 1
        while len(bl) < NB:
            bl.append((None, 0, 1, None))
        core_blocks.append(bl)
    ys = run_cores(inp, core_blocks, T, DEPTH)
    y_prompt = np.zeros(xp.shape, np.float32)
    y_sample = np.zeros(xs.shape, np.float32)
    for c in range(2):
        y_sample[c] = ys[c]
    for sid, (c, i) in owner.items():
        y_prompt[sid] = ys[c][i * BLK:(i + 1) * BLK]
    return (y_prompt, y_sample)
```

```python
import numpy as np
import concourse.bass as bass
import concourse.mybir as mybir
from concourse.bass_utils import run_bass_kernel_spmd

F32 = mybir.dt.float32
BF16 = mybir.dt.bfloat16
AF = mybir.ActivationFunctionType
ALU = mybir.AluOpType

D = 1024
KC = 8
DEPTH = 4
BLK = 2048
TILE = 512
TILE2 = 256
GP = 2
EPS = 1e-6


class Buf:
    __slots__ = ("name", "writer", "readers", "excl")

    def __init__(self, name, excl=False):
        self.name = name
        self.writer = None
        self.readers = []
        self.excl = excl


class Op:
    __slots__ = ("eng", "fn", "deps", "semkey", "val", "needs_inc", "is_dma", "inc")


class Rec:
    ENGS = ("pe", "act", "dve", "pool", "sp")

    def __init__(self):
        self.ops = {e: [] for e in self.ENGS}
        self.dma_counts = {}
        self.last_dma = {}
        self.pending = {e: [] for e in self.ENGS}

    def _dep(self, op, d, raw):
        if d is None or d is op:
            return
        if not d.is_dma and d.eng == op.eng:
            if op.eng in ("pe", "sp") or not raw:
                return
        op.deps.append(d)
        d.needs_inc = True

    def add(self, eng, fn, reads=(), writes=(), dma_sem=None):
        op = Op()
        op.eng = eng
        op.fn = fn
        op.deps = []
        op.needs_inc = False
        op.is_dma = dma_sem is not None
        op.val = None
        if op.is_dma:
            c = self.dma_counts.get(dma_sem, 0) + 16
            self.dma_counts[dma_sem] = c
            op.semkey = dma_sem
            op.val = c
            op.inc = 16
            op.needs_inc = True
        else:
            op.semkey = eng
            op.inc = 1
        if self.pending[eng]:
            for d in self.pending[eng]:
                if d.is_dma or d.eng != eng:
                    op.deps.append(d)
                    d.needs_inc = True
            self.pending[eng] = []
        if op.is_dma:
            self.last_dma[dma_sem] = op
        for b in reads:
            self._dep(op, b.writer, True)
            if b.excl:
                for r in b.readers:
                    if r.eng != eng:
                        self._dep(op, r, False)
        for b in writes:
            self._dep(op, b.writer, False)
            for r in b.readers:
                self._dep(op, r, False)
        for b in reads:
            b.readers.append(op)
        for b in writes:
            b.writer = op
            b.readers = []
        self.ops[eng].append(op)
        return op

    def barrier(self):
        deps = []
        for e in self.ENGS:
            for op in reversed(self.ops[e]):
                if not op.is_dma:
                    deps.append(op)
                    break
        deps += list(self.last_dma.values())
        for e in self.ENGS:
            self.pending[e] = list(deps)

    def finalize(self):
        for e in self.ENGS:
            c = 0
            for op in self.ops[e]:
                if not op.is_dma and op.needs_inc:
                    c += 1
                    op.val = c

    def emit(self, eng, handle, sems):
        waited = {}
        for op in self.ops[eng]:
            need = {}
            for d in op.deps:
                v = need.get(d.semkey, 0)
                if d.val > v:
                    need[d.semkey] = d.val
            for k, v in need.items():
                if waited.get(k, 0) < v:
                    handle.wait_ge(sems[k], v)
                    waited[k] = v
            inst = op.fn(handle)
            if op.needs_inc:
                inst.then_inc(sems[op.semkey], op.inc)
        return waited


W_Q, W_K, W_V, W_G, W_LR, W_U, W_VS, W_XP = 0, 256, 512, 1024, 1536, 1568, 1824, 2080
INW = 2336
DFF = 2816
NF = 22
HB = 1024
WV = (342, 341, 341)
WS = (0, 342, 683)


class TK:
    __slots__ = ("t", "b")

    def __init__(self, t, name):
        self.t = t
        self.b = Buf(name)


class Arena:
    def __init__(self, t, nf32):
        self.t = t
        self.n = nf32
        self.off = 0

    def alloc(self, name, shape, dt=F32):
        nel = 1
        for s_ in shape[1:]:
            nel *= s_
        nbytes = nel * (2 if dt == BF16 else 4)
        nf = (nbytes + 3) // 4
        if self.off % 2:
            self.off += 1
        v = self.t[0:shape[0], self.off:self.off + nf]
        self.off += nf
        assert self.off <= self.n, (name, self.off, self.n)
        if dt == BF16:
            v = v.bitcast(BF16)[:, 0:nel]
        if len(shape) == 3:
            v = v.rearrange("p (a b) -> p a b", a=shape[1])
        elif len(shape) == 4:
            v = v.rearrange("p (a b c) -> p a b c", a=shape[1], b=shape[2])
        return TK(v, name)


def col_index(L):
    o = {}
    o["n1"] = 0
    o["n2"] = L * 8
    o["nf"] = 2 * L * 8
    o["cw"] = o["nf"] + 8
    o["cb"] = o["cw"] + L * 66
    o["gn"] = o["cb"] + L * 22
    o["ps"] = o["gn"] + L * 4
    o["n"] = o["ps"] + L * 2
    return o


def build_program(cfg):
    from contextlib import ExitStack
    T = cfg["T"]
    L = cfg["L"]
    NT = T // TILE
    NH = T // HB
    do_mixer = cfg.get("mixer", True)
    do_ffn = cfg.get("ffn", True)
    CI = col_index(L)
    nc = bass.Bass("TRN2", target_bir_lowering=False)
    R = Rec()

    def din(name, shape, dt=F32):
        return nc.dram_tensor(name, list(shape), dt, kind="ExternalInput").ap()

    def dscr(name, shape, dt=F32):
        return nc.dram_tensor(name, list(shape), dt, kind="Internal").ap()

    x_in = din("x_tok", [T, D])
    y_out = nc.dram_tensor("y_tok", [T, D], F32, kind="ExternalOutput").ap()
    ident_d = din("ident", [128, 128])
    maskf_d = din("maskf", [128, 512])
    maskb_d = din("maskb", [128, 512])
    cols_d = din("cols", [128, CI["n"]])
    fTL_d = din("flagT_L", [128, T // TILE2])
    fTR_d = din("flagT_R", [128, T // TILE2])
    fHL_d = din("flagH_L", [128, NH])
    fHR_d = din("flagH_R", [128, NH])
    if L > 0:
        invc_d = din("inv_cnt", [128, 2, T])
        w_in_d = din("w_in", [L, D, INW])
        w_o_d = din("w_o", [L, D, D])
        w_f1_d = din("w_ffn_in", [L, D, 2 * DFF])
        w_f2_d = din("w_ffn_out", [L, DFF, D])
        w2aug_d = din("w2aug", [L, 33, 512])
        wsT_d = din("wsT", [L, 128, 4, 128])
        sgub_d = din("sgu_bias", [L, 128, 2, 128])
        lng_d = din("lng", [L, 128, 256])
        lnb_d = din("lnb", [L, 128, 256])
        bands_d = din("bands", [128, 3, 4, 128])
        poolw_d = din("poolw", [L, 128, 2, 64])
    xa = dscr("xa_scr", [D, T + 2])
    xb = dscr("xb_scr", [D, T + 2])
    xa_v = xa.rearrange("(k p) t -> p k t", p=128)
    xb_v = xb.rearrange("(k p) t -> p k t", p=128)
    sbst_d = dscr("sbst_scr", [T // TILE2, 128, 256])
    stash_d = dscr("stash_scr", [T // TILE2, 128, 256], BF16)

    with ExitStack() as es:
        def sbt(name, shape, dt=F32):
            return TK(es.enter_context(nc.sbuf_tensor("s_" + name, list(shape), dt)), name)

        sems = {}
        for n in ["pe", "act", "dve", "pool"]:
            sems[n] = es.enter_context(nc.semaphore(n))
        banks = [es.enter_context(nc.psum_tensor(f"bank{i}", [128, 512], F32)) for i in range(8)]
        bbank = [Buf(f"bank{i}", excl=True) for i in range(8)]
        bctr = [0]

        def nb():
            i = bctr[0] % 8
            bctr[0] += 1
            return i

        def MM(out, lhsT, rhs, start, stop, rd, wr):
            R.add("pe", lambda h: h.matmul(out, lhsT=lhsT, rhs=rhs, start=start, stop=stop), reads=rd, writes=wr)

        def TR(out, in_, idn, rd, wr):
            R.add("pe", lambda h: h.transpose(out=out, in_=in_, identity=idn), reads=rd, writes=wr)

        def ACT(out, in_, func, rd, wr, scale=None, bias=None):
            kw = {}
            if scale is not None:
                kw["scale"] = scale
            if bias is not None:
                kw["bias"] = bias
            R.add("act", lambda h: h.activation(out=out, in_=in_, func=func, **kw), reads=rd, writes=wr)

        def CP(eng, out, in_, rd, wr):
            if eng == "act":
                R.add("act", lambda h: h.copy(out=out, in_=in_), reads=rd, writes=wr)
            else:
                R.add(eng, lambda h: h.tensor_copy(out=out, in_=in_), reads=rd, writes=wr)

        def TT(eng, out, in0, in1, op, rd, wr):
            R.add(eng, lambda h: h.tensor_tensor(out=out, in0=in0, in1=in1, op=op), reads=rd, writes=wr)

        def STT(out, in0, scalar, in1, op0, op1, rd, wr):
            R.add("dve", lambda h: h.scalar_tensor_tensor(out=out, in0=in0, scalar=scalar, in1=in1, op0=op0, op1=op1),
                  reads=rd, writes=wr)

        def TS(eng, out, in0, s1, op0, rd, wr, s2=None, op1=None):
            if op1 is None and eng == "pool" and op0 == ALU.mult:
                s2, op1 = 1.0, ALU.mult
            if op1 is None:
                R.add(eng, lambda h: h.tensor_scalar(out=out, in0=in0, scalar1=s1, scalar2=None, op0=op0), reads=rd, writes=wr)
            else:
                R.add(eng, lambda h: h.tensor_scalar(out=out, in0=in0, scalar1=s1, scalar2=s2, op0=op0, op1=op1),
                      reads=rd, writes=wr)

        def MSET(eng, ap, val, wr):
            R.add(eng, lambda h: h.memset(ap, val), writes=wr)

        def DMA(q, out, in_, rd, wr, sem, **kw):
            R.add(q, lambda h: h.dma_start(out=out, in_=in_, **kw), reads=rd, writes=wr, dma_sem=sem)

        ident = sbt("ident", [128, 128])
        ident_bf = sbt("ident_bf", [128, 128], BF16)
        maskf = sbt("maskf", [128, 512])
        maskb = sbt("maskb", [128, 512])
        ones_bf = sbt("ones_bf", [128, 128], BF16)
        epsc = sbt("epsc", [128, 1])
        onec = sbt("onec", [128, 1])
        cols = sbt("cols", [128, CI["n"]])
        fTL = sbt("fTL", [128, T // TILE2])
        fTR = sbt("fTR", [128, T // TILE2])
        fHL = sbt("fHL", [128, NH])
        fHR = sbt("fHR", [128, NH])
        zer = sbt("zer", [128, 256])
        for tk, d_ in ((ident, ident_d), (maskf, maskf_d), (maskb, maskb_d), (cols, cols_d), (fTL, fTL_d),
                       (fTR, fTR_d), (fHL, fHL_d), (fHR, fHR_d)):
            DMA("sp", tk.t[:], d_, [], [tk.b], "ld_" + tk.b.name)
        MSET("dve", ones_bf.t[:], 1.0, [ones_bf.b])
        MSET("dve", epsc.t[:], EPS, [epsc.b])
        MSET("dve", onec.t[:], 1.0, [onec.b])
        MSET("dve", zer.t[:], 0.0, [zer.b])
        CP("dve", ident_bf.t[:], ident.t[:], [ident.b], [ident_bf.b])
        trif = maskf.t[:, 0:128]
        trib = maskb.t[:, 0:128]
        for scr, nm in ((xa, "a"), (xb, "b")):
            for kk in range(KC):
                DMA("sp", scr[kk * 128:(kk + 1) * 128, 0:1], zer.t[:, 0:1], [zer.b], [], "st_zer",
                    allow_slow_non_contiguous=True)
                DMA("sp", scr[kk * 128:(kk + 1) * 128, T + 1:T + 2], zer.t[:, 0:1], [zer.b], [], "st_zer",
                    allow_slow_non_contiguous=True)

        def ccol(i):
            return cols.t[:, i:i + 1]

        ARN = 34500
        arena_t = es.enter_context(nc.sbuf_tensor("arena", [128, ARN], F32))

        def norm_tile(x3, xbuf, n, gbase, sq, rstd, out3, outbuf):
            ACT(sq.t[:, :, 0:n], x3, AF.Square, [xbuf], [sq.b])
            bk = nb()
            for k in range(KC):
                MM(banks[bk][:, 0:n], ones_bf.t[:], sq.t[:, k, 0:n], k == 0, k == KC - 1, [sq.b, ones_bf.b], [bbank[bk]])
            ACT(rstd.t[:, 0:n], banks[bk][:, 0:n], AF.Sqrt, [bbank[bk], epsc.b], [rstd.b], scale=1.0 / D, bias=epsc.t[:])
            R.add("dve", lambda h: h.reciprocal(out=rstd.t[:, 0:n], in_=rstd.t[:, 0:n]), reads=[rstd.b], writes=[rstd.b])
            for k in range(KC):
                STT(out3[:, k, :], x3[:, k, :], ccol(gbase + k), rstd.t[:, 0:n], ALU.mult, ALU.mult,
                    [xbuf, rstd.b, cols.b], [outbuf])

        A0 = Arena(arena_t, ARN)
        xin_t = [A0.alloc(f"xin{i}", [128, 4, D]) for i in range(2)]
        xT0 = [A0.alloc(f"xT0_{i}", [128, KC, TILE]) for i in range(2)]
        x_in_v = x_in.rearrange("(n g p) d -> n p g d", p=128, g=4)
        for t in range(NT):
            s = t % 2
            DMA("sp", xin_t[s].t[:], x_in_v[t], [], [xin_t[s].b], f"ld_xin{s}")
            for k in range(KC):
                bk = nb()
                for g in range(4):
                    TR(banks[bk][:, g * 128:(g + 1) * 128], xin_t[s].t[:, g, k * 128:(k + 1) * 128], ident.t[:],
                       [xin_t[s].b, ident.b], [bbank[bk]])
                CP("act" if k % 2 == 0 else "dve", xT0[s].t[:, k, :], banks[bk][:], [bbank[bk]], [xT0[s].b])
            DMA("sp", xa_v[:, :, 1 + t * TILE:1 + (t + 1) * TILE], xT0[s].t[:], [xT0[s].b], [], f"st_xT0_{s}")
        R.barrier()

        if L > 0:
            win = sbt("win", [128, KC, INW], BF16)
            wo = sbt("wo", [128, KC, D], BF16)
            w2aug = sbt("w2aug", [33, 512])
            wsT = sbt("wsT", [128, 4, 128], BF16)
            sgub = sbt("sgub", [128, 2, 128])
            lng = sbt("lng", [128, 256])
            lnb = sbt("lnb", [128, 256])
            bands = sbt("bands", [128, 3, 4, 128], BF16)
            poolw = sbt("poolw", [128, 2, 64], BF16)
            DMA("pool", bands.t[:], bands_d, [], [bands.b], "ld_bands")

        for l in range(L):
            if do_mixer:
                wv = w_in_d[l].rearrange("(k p) c -> p k c", p=128)
                for k in range(KC):
                    DMA("pool", win.t[:, k, :], wv[:, k, :], [], [win.b], "ld_win")
                wv = w_o_d[l].rearrange("(k p) c -> p k c", p=128)
                for k in range(0, KC, 2):
                    DMA("pool", wo.t[:, k:k + 2, :], wv[:, k:k + 2, :], [], [wo.b], "ld_wo")
                DMA("sp", w2aug.t[:], w2aug_d[l], [], [w2aug.b], "ld_w2aug")
                DMA("pool", wsT.t[:], wsT_d[l], [], [wsT.b], "ld_wsT")
                DMA("sp", sgub.t[:], sgub_d[l], [], [sgub.b], "ld_sgub")
                DMA("sp", lng.t[:], lng_d[l], [], [lng.b], "ld_lng")
                DMA("sp", lnb.t[:], lnb_d[l], [], [lnb.b], "ld_lnb")
                DMA("pool", poolw.t[:], poolw_d[l], [], [poolw.b], "ld_poolw")

            if do_mixer:
                A = Arena(arena_t, ARN)
                T2 = TILE2
                NT2 = T // T2

                class _Set:
                    pass

                sets = []
                for i_ in range(2):
                    S_ = _Set()
                    S_.i = i_
                    S_.qT = A.alloc(f"qT{i_}", [128, 2, T2])
                    S_.kT = A.alloc(f"kT{i_}", [128, 2, T2])
                    S_.lrT = A.alloc(f"lrT{i_}", [33, T2])
                    S_.sg = A.alloc(f"sg{i_}", [128, 4, T2])
                    S_.gu = A.alloc(f"gu{i_}", [128, 2, T2])
                    S_.xpT = A.alloc(f"xpT{i_}", [128, 2, T2])
                    S_.vtok = [A.alloc(f"vtok{i_}{g}", [128, 512], BF16) for g in range(GP)]
                    S_.vn = [A.alloc(f"vn{i_}{g}", [128, 256], BF16) for g in range(GP)]
                    S_.xpk = [A.alloc(f"xpk{i_}{g}", [128, 256], BF16) for g in range(GP)]
                    S_.xpl = A.alloc(f"xpl{i_}", [128, 256], BF16)
                    S_.xpr = A.alloc(f"xpr{i_}", [128, 256], BF16)
                    S_.stsh = A.alloc(f"stsh{i_}", [128, 256], BF16)
                    S_.invc = A.alloc(f"invc{i_}", [128, 2, T2])
                    S_.Sstg = A.alloc(f"Sstg{i_}", [128, 2, 128])
                    sets.append(S_)
                xT3 = [A.alloc(f"xT{i_}", [128, KC, T2]) for i_ in range(3)]

                def ld_x(t):
                    X_ = xT3[t % 3]
                    DMA("sp", X_.t[:], xa_v[:, :, 1 + t * T2:1 + (t + 1) * T2], [], [X_.b], f"ld_xT{t % 3}")

                hT = A.alloc("hT", [128, KC, T2], BF16)
                sq = hT
                rstd = A.alloc("rstd", [128, T2])
                e_tL = [A.alloc(f"e_t{g}", [128, 512]) for g in range(GP)]
                sp_tL = [A.alloc(f"sp_t{g}", [128, 512]) for g in range(GP)]
                EpL = [A.alloc(f"Ep{g}", [128, 4, 128]) for g in range(GP)]
                EmL = [A.alloc(f"Em{g}", [128, 4, 128]) for g in range(GP)]
                qd = [A.alloc(f"qd{g}", [128, 4, 128], BF16) for g in range(GP)]
                kd = [A.alloc(f"kd{g}", [128, 4, 128], BF16) for g in range(GP)]
                kdt = [A.alloc(f"kdt{g}", [128, 512], BF16) for g in range(GP)]
                egl = [A.alloc(f"egl{g}", [128, 4]) for g in range(GP)]
                AT = [A.alloc(f"AT{g}", [128, 8, 128], BF16) for g in range(GP)]
                Sf = A.alloc("Sf", [128, 2, 128])
                Sb = A.alloc("Sb", [128, 2, 128])
                Stmp = A.alloc("Stmp", [128, 2, 128])
                kvs = [[A.alloc(f"kvs{d_}{g}", [128, 2, 128]) for g in range(GP)] for d_ in range(2)]
                Sbf = [[A.alloc(f"Sbf{d_}{g}", [128, 2, 2, 128], BF16) for g in range(GP)] for d_ in range(2)]
                osbL = [A.alloc(f"osb{g}", [128, 512]) for g in range(GP)]
                osqL = [A.alloc(f"osq{g}", [128, 512], BF16) for g in range(GP)]
                rsL = [A.alloc(f"rs{g}", [128, 512]) for g in range(GP)]
                otmpL = [A.alloc(f"otmp{g}", [128, 512]) for g in range(GP)]
                vsg = A.alloc("vsg", [128, 256])
                vtmp = A.alloc("vtmp", [128, 256])
                stats = A.alloc("stats", [128, 8])
                mv = A.alloc("mv", [128, 4])
                stmp = A.alloc("stmp", [128, 2, 128])
                ptmp = A.alloc("ptmp", [128, 2, 128])
                dT = A.alloc("dT", [128, 2, T2], BF16)
                mixT = A.alloc("mixT", [128, KC, T2], BF16)

                for S_ in sets:
                    MSET("pool", S_.lrT.t[32:33, :], 1.0, [S_.lrT.b])
                for d_ in range(2):
                    for g in range(GP):
                        MSET("pool", Sbf[d_][g].t[:], 0.0, [Sbf[d_][g].b])

                def proj_fm(c0, m):
                    bk = nb()
                    for k in range(KC):
                        MM(banks[bk][0:m, 0:T2], win.t[:, k, c0:c0 + m], hT.t[:, k, :], k == 0, k == KC - 1,
                           [win.b, hT.b], [bbank[bk]])
                    return bk

                def proj_tm(g, c0, n):
                    bk = nb()
                    for k in range(KC):
                        MM(banks[bk][:, 0:n], hT.t[:, k, g * 128:(g + 1) * 128], win.t[:, k, c0:c0 + n], k == 0,
                           k == KC - 1, [win.b, hT.b], [bbank[bk]])
                    return bk

                def chunk_gen(S_, g, dirs, want_q, want_kv):
                    e_t, sp_t, Ep, Em = e_tL[g], sp_tL[g], EpL[g], EmL[g]
                    gc = slice(g * 128, (g + 1) * 128)
                    c0 = 0 if 0 in dirs else 256
                    n = 256 * len(dirs)
                    bk = nb()
                    MM(banks[bk][:, 0:n], S_.lrT.t[:, gc], w2aug.t[:, c0:c0 + n], True, True,
                       [S_.lrT.b, w2aug.b], [bbank[bk]])
                    ACT(e_t.t[:, 0:n], banks[bk][:, 0:n], AF.Exp, [bbank[bk]], [e_t.b], scale=-1.0)
                    ACT(sp_t.t[:, 0:n], e_t.t[:, 0:n], AF.Ln, [e_t.b, onec.b], [sp_t.b], bias=onec.t[:])
                    yield
                    bk2 = nb()
                    for i_, d_ in enumerate(dirs):
                        for hh in range(2):
                            bi = d_ * 2 + hh
                            MM(banks[bk2][:, bi * 128:(bi + 1) * 128], sp_t.t[:, (i_ * 2 + hh) * 128:(i_ * 2 + hh + 1) * 128],
                               trif if d_ == 0 else trib, True, True, [sp_t.b, maskf.b, maskb.b], [bbank[bk2]])
                    b0 = dirs[0] * 2
                    b1 = dirs[-1] * 2 + 2
                    gv = banks[bk2][:, b0 * 128:b1 * 128].rearrange("p (a b) -> p a b", b=128)
                    ACT(Ep.t[:, b0:b1, :], gv, AF.Exp, [bbank[bk2]], [Ep.b], scale=-1.0 / 16)
                    ACT(Em.t[:, b0:b1, :], gv, AF.Exp, [bbank[bk2]], [Em.b], scale=1.0 / 16)
                    if 0 in dirs:
                        CP("pool", egl[g].t[:, 0:2], Ep.t[:, 0:2, 127], [Ep.b], [egl[g].b])
                    if 1 in dirs:
                        CP("pool", egl[g].t[:, 2:4], Ep.t[:, 2:4, 0], [Ep.b], [egl[g].b])
                    yield
                    for d_ in dirs:
                        TT("dve", kd[g].t[:, d_ * 2:d_ * 2 + 2, :], S_.kT.t[:, :, gc],
                           Em.t[:, d_ * 2:d_ * 2 + 2, :], ALU.mult, [S_.kT.b, Em.b], [kd[g].b])
                        if want_q:
                            TT("dve", qd[g].t[:, d_ * 2:d_ * 2 + 2, :], S_.qT.t[:, :, gc], Ep.t[:, d_ * 2:d_ * 2 + 2, :], ALU.mult,
                               [S_.qT.b, Ep.b], [qd[g].b])
                    bk = nb()
                    bv = banks[bk][:].bitcast(BF16)
                    for bi in range(b0, b1):
                        TR(bv[:, bi * 128:(bi + 1) * 128], kd[g].t[:, bi, :], ident_bf.t[:], [kd[g].b, ident_bf.b], [bbank[bk]])
                    CP("act", kdt[g].t[:, b0 * 128:b1 * 128], bv[:, b0 * 128:b1 * 128], [bbank[bk]], [kdt[g].b])
                    yield
                    for d_ in dirs:
                        if not want_kv[d_]:
                            continue
                        bk = nb()
                        for hd in range(4):
                            p0 = (hd % 2) * 64
                            MM(banks[bk][p0:p0 + 64, (hd // 2) * 128:(hd // 2 + 1) * 128],
                               kdt[g].t[:, d_ * 256 + hd * 64:d_ * 256 + (hd + 1) * 64], S_.vtok[g].t[:, hd * 128:(hd + 1) * 128],
                               True, True, [kdt[g].b, S_.vtok[g].b], [bbank[bk]])
                        for hh in range(2):
                            ACT(kvs[d_][g].t[:, hh, :], banks[bk][:, hh * 128:(hh + 1) * 128], AF.Identity,
                                [bbank[bk], egl[g].b], [kvs[d_][g].b], scale=egl[g].t[:, d_ * 2 + hh:d_ * 2 + hh + 1])
                    yield
                    if want_q:
                        for e in range(2):
                            bk = nb()
                            p0 = e * 64
                            for bi in range(4):
                                MM(banks[bk][:, bi * 128:(bi + 1) * 128], kd[g].t[p0:p0 + 64, bi, :], qd[g].t[p0:p0 + 64, bi, :],
                                   True, True, [kd[g].b, qd[g].b], [bbank[bk]])
                            av = banks[bk][:].rearrange("p (a b) -> p a b", b=128)
                            TT("dve", AT[g].t[:, e * 4:e * 4 + 2, :], av[:, 0:2, :], maskf.t[:, 0:256].rearrange("p (a b) -> p a b", b=128),
                               ALU.mult, [bbank[bk], maskf.b], [AT[g].b])
                            TT("dve", AT[g].t[:, e * 4 + 2:e * 4 + 4, :], av[:, 2:4, :], maskb.t[:, 0:256].rearrange("p (a b) -> p a b", b=128),
                               ALU.mult, [bbank[bk], maskb.b], [AT[g].b])
                            yield

                def out_gen(S_, g):
                    osb, osq, rs, otmp = osbL[g], osqL[g], rsL[g], otmpL[g]
                    gc = slice(g * 128, (g + 1) * 128)
                    bk = nb()
                    for hd in range(4):
                        hh = hd // 2
                        e = hd % 2
                        o_ap = banks[bk][:, hd * 128:(hd + 1) * 128]
                        MM(o_ap, S_.vtok[g].t[:, hd * 128:(hd + 1) * 128], AT[g].t[:, e * 4 + 0 + hh, :], True, False,
                           [S_.vtok[g].b, AT[g].b], [bbank[bk]])
                        MM(o_ap, S_.vtok[g].t[:, hd * 128:(hd + 1) * 128], AT[g].t[:, e * 4 + 2 + hh, :], False, False,
                           [S_.vtok[g].b, AT[g].b], [bbank[bk]])
                        MM(o_ap, Sbf[0][g].t[:, hh, e, :], qd[g].t[:, 0 + hh, :], False, False,
                           [Sbf[0][g].b, qd[g].b], [bbank[bk]])
                        MM(o_ap, Sbf[1][g].t[:, hh, e, :], qd[g].t[:, 2 + hh, :], False, True,
                           [Sbf[1][g].b, qd[g].b], [bbank[bk]])
                    CP("act", osb.t[:], banks[bk][:], [bbank[bk]], [osb.b])
                    ACT(osq.t[:], banks[bk][:], AF.Square, [bbank[bk]], [osq.b])
                    yield
                    bk2 = nb()
                    MM(banks[bk2][:], ones_bf.t[:], osq.t[:], True, True, [osq.b, ones_bf.b], [bbank[bk2]])
                    ACT(rs.t[:], banks[bk2][:], AF.Sqrt, [bbank[bk2], epsc.b], [rs.b], scale=1.0 / 128, bias=epsc.t[:])
                    R.add("dve", lambda h: h.reciprocal(out=rs.t[:], in_=rs.t[:]), reads=[rs.b], writes=[rs.b])
                    TT("dve", otmp.t[:], osb.t[:], rs.t[:], ALU.mult, [osb.b, rs.b], [otmp.b])
                    yield
                    for hd in range(4):
                        STT(mixT.t[:, hd, gc], otmp.t[:, hd * 128:(hd + 1) * 128], ccol(CI["gn"] + l * 4 + hd), S_.sg.t[:, hd, gc],
                            ALU.mult, ALU.mult, [otmp.b, cols.b, S_.sg.b], [mixT.b])
                    yield

                def lockstep(gens):
                    gens = list(gens)
                    while gens:
                        for g_ in list(gens):
                            try:
                                next(g_)
                                yield
                            except StopIteration:
                                gens.remove(g_)

                def state_step(S, g, d_):
                    for hh in range(2):
                        STT(S.t[:, hh, :], S.t[:, hh, :], egl[g].t[:, d_ * 2 + hh:d_ * 2 + hh + 1], kvs[d_][g].t[:, hh, :],
                            ALU.mult, ALU.add, [S.b, egl[g].b, kvs[d_][g].b], [S.b])

                def state_to_bf(S, dst):
                    CP("act", dst.t[0:64, :, 0, :], S.t[0:64, :, :], [S.b], [dst.b])
                    CP("act", dst.t[64:128, :, 1, :], S.t[64:128, :, :], [S.b], [dst.b])

                def bank3(bk):
                    return banks[bk][:, 0:256].rearrange("p (a b) -> p a b", b=128)

                def pipeline(order, genA, genB):
                    for _ in genA(order[0]):
                        pass
                    for i_, t in enumerate(order):
                        gb = genB(t)
                        ga = genA(order[i_ + 1]) if i_ + 1 < len(order) else iter(())
                        tog = 0
                        RATIO = cfg.get("ratio", 2)
                        for hint in gb:
                            if hint is None:
                                tog += 1
                                hint = 1 if tog % RATIO == 0 else 0
                            for _ in range(hint):
                                try:
                                    next(ga)
                                except StopIteration:
                                    break
                        for _ in ga:
                            pass

                def s1A(t):
                    S_ = sets[t % 2]
                    X_ = xT3[t % 3]
                    if t - 1 >= 0:
                        ld_x(t - 1)
                    norm_tile(X_.t[:], X_.b, T2, CI["n1"] + l * 8, sq, rstd, hT.t, hT.b)
                    yield
                    for c in range(2):
                        bk = proj_fm(W_K + c * 128, 128)
                        CP("act", S_.kT.t[:, c, :], banks[bk][:, 0:T2], [bbank[bk]], [S_.kT.b])
                    bk = proj_fm(W_LR, 32)
                    CP("act", S_.lrT.t[0:32, :], banks[bk][0:32, 0:T2], [bbank[bk]], [S_.lrT.b])
                    yield
                    for g in range(GP):
                        bk = proj_tm(g, W_V, 512)
                        CP("dve", S_.vtok[g].t[:], banks[bk][:], [bbank[bk]], [S_.vtok[g].b])
                        yield
                    if t >= 1:
                        bk = proj_tm(0, W_XP, 256)
                        TS("dve", S_.stsh.t[:], banks[bk][:, 0:256], fTR.t[:, t - 1:t], ALU.mult, [bbank[bk], fTR.b], [S_.stsh.b])
                        DMA("sp", stash_d[t - 1], S_.stsh.t[:], [S_.stsh.b], [], f"st_stsh{S_.i}")

                def s1B(t):
                    S_ = sets[t % 2]
                    TS("dve", Sb.t[:], Sb.t[:], fTR.t[:, t:t + 1], ALU.mult, [Sb.b, fTR.b], [Sb.b])
                    CP("pool", S_.Sstg.t[:], Sb.t[:], [Sb.b], [S_.Sstg.b])
                    DMA("sp", sbst_d[t].rearrange("p (a b) -> p a b", b=128), S_.Sstg.t[:], [S_.Sstg.b], [], f"st_Sstg{S_.i}")
                    yield from lockstep([chunk_gen(S_, g, (1,), False, {1: True}) for g in reversed(range(GP))])
                    for g in reversed(range(GP)):
                        state_step(Sb, g, 1)
                    yield 2

                MSET("dve", Sb.t[:], 0.0, [Sb.b])
                DMA("sp", stash_d[NT2 - 1], zer.t[:, 0:128].bitcast(BF16), [zer.b], [], "st_zer")
                if cfg.get("s1", 9) > 0:
                    ld_x(NT2 - 1)
                    pipeline(list(reversed(range(NT2))), s1A, s1B)
                R.barrier()

                def s2A(t):
                    S_ = sets[t % 2]
                    O_ = sets[1 - t % 2]
                    sl = S_.i
                    X_ = xT3[t % 3]
                    if t + 1 < NT2:
                        ld_x(t + 1)
                    DMA("sp", S_.invc.t[:], invc_d[:, :, t * T2:(t + 1) * T2], [], [S_.invc.b], f"ld_invc{sl}")
                    DMA("sp", S_.Sstg.t[:], sbst_d[t].rearrange("p (a b) -> p a b", b=128), [], [S_.Sstg.b], f"ld_Sstg{sl}")
                    DMA("sp", S_.xpr.t[:], stash_d[t], [], [S_.xpr.b], f"ld_xpr{sl}")
                    if t >= 1:
                        TS("pool", S_.xpl.t[:], O_.xpk[GP - 1].t[:], fTL.t[:, t:t + 1], ALU.mult, [O_.xpk[GP - 1].b, fTL.b], [S_.xpl.b])
                    else:
                        MSET("pool", S_.xpl.t[:], 0.0, [S_.xpl.b])
                    norm_tile(X_.t[:], X_.b, T2, CI["n1"] + l * 8, sq, rstd, hT.t, hT.b)
                    yield
                    for c in range(2):
                        bk = proj_fm(W_Q + c * 128, 128)
                        ACT(S_.qT.t[:, c, :], banks[bk][:, 0:T2], AF.Copy, [bbank[bk]], [S_.qT.b], scale=0.125)
                    for c in range(2):
                        bk = proj_fm(W_K + c * 128, 128)
                        CP("act", S_.kT.t[:, c, :], banks[bk][:, 0:T2], [bbank[bk]], [S_.kT.b])
                    yield
                    bk = proj_fm(W_LR, 32)
                    CP("act", S_.lrT.t[0:32, :], banks[bk][0:32, 0:T2], [bbank[bk]], [S_.lrT.b])
                    for c in range(4):
                        bk = proj_fm(W_G + c * 128, 128)
                        ACT(S_.sg.t[:, c, :], banks[bk][:, 0:T2], AF.Silu, [bbank[bk]], [S_.sg.b])
                        if c == 1:
                            yield
                    yield
                    for c in range(2):
                        bk = proj_fm(W_U + c * 128, 128)
                        ACT(S_.gu.t[:, c, :], banks[bk][:, 0:T2], AF.Gelu_apprx_tanh, [bbank[bk]], [S_.gu.b])
                    for c in range(2):
                        bk = proj_fm(W_XP + c * 128, 128)
                        CP("act", S_.xpT.t[:, c, :], banks[bk][:, 0:T2], [bbank[bk]], [S_.xpT.b])
                    yield
                    for g in range(GP):
                        bk = proj_tm(g, W_V, 512)
                        CP("dve", S_.vtok[g].t[:], banks[bk][:], [bbank[bk]], [S_.vtok[g].b])
                        yield
                        bk = proj_tm(g, W_VS, 512)
                        ACT(vsg.t[:], banks[bk][:, 0:256], AF.Gelu_apprx_tanh, [bbank[bk]], [vsg.b])
                        CP("act", S_.xpk[g].t[:], banks[bk][:, 256:512], [bbank[bk]], [S_.xpk[g].b])
                        R.add("dve", lambda h: h.bn_stats(out=stats.t[:, 0:6], in_=vsg.t[:]), reads=[vsg.b], writes=[stats.b])
                        R.add("dve", lambda h: h.bn_aggr(out=mv.t[:, 0:2], in_=stats.t[:, 0:6]), reads=[stats.b], writes=[mv.b])
                        ACT(mv.t[:, 2:3], mv.t[:, 1:2], AF.Sqrt, [mv.b, epsc.b], [mv.b], bias=epsc.t[:])
                        R.add("dve", lambda h: h.reciprocal(out=mv.t[:, 3:4], in_=mv.t[:, 2:3]), reads=[mv.b], writes=[mv.b])
                        TS("dve", vtmp.t[:], vsg.t[:], mv.t[:, 0:1], ALU.subtract, [vsg.b, mv.b], [vtmp.b], s2=mv.t[:, 3:4], op1=ALU.mult)
                        TT("pool", vtmp.t[:], vtmp.t[:], lng.t[:], ALU.mult, [vtmp.b, lng.b], [vtmp.b])
                        TT("pool", S_.vn[g].t[:], vtmp.t[:], lnb.t[:], ALU.add, [vtmp.b, lnb.b], [S_.vn[g].b])
                        yield

                def s2B(t):
                    S_ = sets[t % 2]
                    sl = S_.i
                    X_ = xT3[t % 3]
                    for g in range(GP):
                        gc = slice(g * 128, (g + 1) * 128)
                        bk = nb()
                        for hd in range(4):
                            p0 = (hd % 2) * 64
                            MM(banks[bk][p0:p0 + 64, (hd // 2) * 128:(hd // 2 + 1) * 128], S_.vn[g].t[:, hd * 64:(hd + 1) * 64],
                               wsT.t[:, hd, :], True, True, [S_.vn[g].b, wsT.b], [bbank[bk]])
                        TT("dve", stmp.t[:], bank3(bk), sgub.t[:], ALU.add, [bbank[bk], sgub.b], [stmp.b])
                        TT("dve", mixT.t[:, 4:6, gc], stmp.t[:], S_.gu.t[:, :, gc], ALU.mult, [stmp.b, S_.gu.b], [mixT.b])
                    yield
                    for g in range(GP):
                        gc = slice(g * 128, (g + 1) * 128)
                        prv = S_.xpl if g == 0 else S_.xpk[g - 1]
                        nxt = S_.xpr if g == GP - 1 else S_.xpk[g + 1]
                        cur = S_.xpk[g]
                        bk = nb()
                        for pg in range(4):
                            p0 = (pg % 2) * 64
                            o_ap = banks[bk][p0:p0 + 64, (pg // 2) * 128:(pg // 2 + 1) * 128]
                            for wi, src in enumerate((prv, cur, nxt)):
                                MM(o_ap, src.t[:, pg * 64:(pg + 1) * 64], bands.t[:, wi, pg, :], wi == 0, wi == 2,
                                   [src.b, bands.b], [bbank[bk]])
                        TT("dve", ptmp.t[:], bank3(bk), S_.invc.t[:, :, gc], ALU.mult, [bbank[bk], S_.invc.b], [ptmp.b])
                        TT("dve", dT.t[:, :, gc], ptmp.t[:], S_.xpT.t[:, :, gc], ALU.subtract, [ptmp.b, S_.xpT.b], [dT.b])
                    yield
                    yield from lockstep([chunk_gen(S_, g, (0, 1), True, {0: True, 1: g >= 1}) for g in range(GP)])
                    for j in range(2):
                        bk = nb()
                        for e in range(2):
                            p0 = e * 64
                            MM(banks[bk][p0:p0 + 64, 0:T2], poolw.t[p0:p0 + 64, j, :], dT.t[p0:p0 + 64, j, :], True, True,
                               [poolw.b, dT.b], [bbank[bk]])
                        ACT(mixT.t[:, 6 + j, :], banks[bk][:, 0:T2], AF.Identity, [bbank[bk], cols.b], [mixT.b],
                            scale=ccol(CI["ps"] + l * 2 + j))
                    if t >= 1:
                        TS("dve", Sf.t[:], Sf.t[:], fTL.t[:, t:t + 1], ALU.mult, [Sf.b, fTL.b], [Sf.b])
                    for g in range(GP):
                        state_to_bf(Sf, Sbf[0][g])
                        state_step(Sf, g, 0)
                    CP("dve", Sb.t[:], S_.Sstg.t[:], [S_.Sstg.b], [Sb.b])
                    state_to_bf(Sb, Sbf[1][GP - 1])
                    for g in range(GP - 1, 0, -1):
                        state_step(Sb, g, 1)
                        state_to_bf(Sb, Sbf[1][g - 1])
                    yield 3
                    yield from lockstep([out_gen(S_, g) for g in range(GP)])
                    for dch in range(KC):
                        bk = nb()
                        for k in range(KC):
                            MM(banks[bk][:, 0:T2], wo.t[:, k, dch * 128:(dch + 1) * 128], mixT.t[:, k, :], k == 0, k == KC - 1,
                               [wo.b, mixT.b], [bbank[bk]])
                        TT("dve", X_.t[:, dch, :], X_.t[:, dch, :], banks[bk][:, 0:T2], ALU.add, [X_.b, bbank[bk]], [X_.b])
                        if dch % 2 == 1:
                            yield
                    DMA("sp", xb_v[:, :, 1 + t * T2:1 + (t + 1) * T2], X_.t[:], [X_.b], [], f"st_xT{t % 3}")

                MSET("dve", Sf.t[:], 0.0, [Sf.b])
                if cfg.get("s2", 9) > 0:
                    ld_x(0)
                    pipeline(list(range(NT2)), s2A, s2B)
                R.barrier()
            else:
                A = Arena(arena_t, ARN)
                xT = [A.alloc(f"xT{i}", [128, KC, TILE]) for i in range(2)]
                for t in range(NT):
                    s = t % 2
                    DMA("sp", xT[s].t[:], xa_v[:, :, 1 + t * TILE:1 + (t + 1) * TILE], [], [xT[s].b], f"ld_xT{s}")
                    DMA("sp", xb_v[:, :, 1 + t * TILE:1 + (t + 1) * TILE], xT[s].t[:], [xT[s].b], [], f"st_xT{s}")
                R.barrier()

            if do_ffn:
                A = Arena(arena_t, ARN)
                h2T = A.alloc("h2T", [128, KC, HB + 2], BF16)
                gT = A.alloc("gT", [128, NF, HB], BF16)
                xw = [A.alloc(f"xw{i}", [128, KC, 342]) for i in range(2)]
                sq3 = A.alloc("sq3", [128, KC, 342], BF16)
                rstd3 = A.alloc("rstd3", [128, 342])
                w1a = [A.alloc(f"w1a{i}", [128, KC, 256], BF16) for i in range(2)]
                w1u = [A.alloc(f"w1u{i}", [128, KC, 256], BF16) for i in range(2)]
                w2c = [A.alloc(f"w2c{i}", [128, NF, 128], BF16) for i in range(2)]
                xd = [A.alloc(f"xd{i}", [128, HB]) for i in range(2)]
                t1 = [A.alloc(f"t1_{i}", [128, 344]) for i in range(4)]
                t2 = [A.alloc(f"t2_{i}", [128, 344]) for i in range(4)]
                w1v = w_f1_d[l].rearrange("(k p) c -> p k c", p=128)
                w2v = w_f2_d[l].rearrange("(f p) c -> p f c", p=128)
                cnt3 = [0]

                def ld_xw(hb, w):
                    s_ = (hb * 3 + w) % 2
                    DMA("sp", xw[s_].t[:], xb_v[:, :, hb * HB + 342 * w:hb * HB + 342 * (w + 1)], [], [xw[s_].b], f"ld_xw{s_}")

                def norm_w(hb, w):
                    s_ = (hb * 3 + w) % 2
                    norm_tile(xw[s_].t[:], xw[s_].b, 342, CI["n2"] + l * 8, sq3, rstd3, h2T.t[:, :, 342 * w:342 * (w + 1)], h2T.b)

                ld_xw(0, 0)
                ld_xw(0, 1)
                norm_w(0, 0)
                ld_xw(0, 2)
                norm_w(0, 1)
                norm_w(0, 2)
                for hb in range(NH):
                    cb = hb * HB

                    def ld_b(dch):
                        ds__ = dch % 2
                        DMA("pool", w2c[ds__].t[:], w2v[:, :, dch * 128:(dch + 1) * 128], [], [w2c[ds__].b], f"ld_w2c{ds__}")
                        DMA("sp", xd[ds__].t[:], xb[dch * 128:(dch + 1) * 128, 1 + cb:1 + cb + HB], [], [xd[ds__].b], f"ld_xd{ds__}")

                    pend = [None]
                    for fp in range(NF // 2):
                        ws_ = (hb * (NF // 2) + fp) % 2
                        for k in range(0, KC, 4):
                            DMA("pool", w1a[ws_].t[:, k:k + 4, :], w1v[:, k:k + 4, fp * 256:(fp + 1) * 256], [], [w1a[ws_].b], f"ld_w1a{ws_}")
                            DMA("pool", w1u[ws_].t[:, k:k + 4, :], w1v[:, k:k + 4, DFF + fp * 256:DFF + (fp + 1) * 256], [], [w1u[ws_].b], f"ld_w1u{ws_}")
                        if fp == NF // 2 - 1:
                            ld_b(0)
                        for fi in range(2):
                            f = fp * 2 + fi
                            for w in range(3):
                                nv = WV[w]
                                v0 = WS[w]
                                ts_ = cnt3[0] % 4
                                cnt3[0] += 1
                                bka = nb()
                                for k in range(KC):
                                    MM(banks[bka][:, 0:nv + 2], w1a[ws_].t[:, k, fi * 128:(fi + 1) * 128], h2T.t[:, k, v0:v0 + nv + 2],
                                       k == 0, k == KC - 1, [w1a[ws_].b, h2T.b], [bbank[bka]])
                                bku = nb()
                                for k in range(KC):
                                    MM(banks[bku][:, 0:nv], w1u[ws_].t[:, k, fi * 128:(fi + 1) * 128], h2T.t[:, k, v0 + 1:v0 + 1 + nv],
                                       k == 0, k == KC - 1, [w1u[ws_].b, h2T.b], [bbank[bku]])
                                if w == 0:
                                    TS("dve", banks[bka][:, 0:1], banks[bka][:, 0:1], fHL.t[:, hb:hb + 1], ALU.mult,
                                       [bbank[bka], fHL.b], [bbank[bka]])
                                if w == 2:
                                    TS("dve", banks[bka][:, nv + 1:nv + 2], banks[bka][:, nv + 1:nv + 2], fHR.t[:, hb:hb + 1], ALU.mult,
                                       [bbank[bka], fHR.b], [bbank[bka]])
                                cwi = CI["cw"] + l * 66
                                ACT(t1[ts_].t[:, 0:nv], banks[bka][:, 1:nv + 1], AF.Identity, [bbank[bka], cols.b], [t1[ts_].b],
                                    scale=ccol(cwi + 22 + f), bias=ccol(CI["cb"] + l * 22 + f))
                                STT(t2[ts_].t[:, 0:nv], banks[bka][:, 0:nv], ccol(cwi + f), t1[ts_].t[:, 0:nv], ALU.mult, ALU.add,
                                    [bbank[bka], cols.b, t1[ts_].b], [t2[ts_].b])
                                STT(t1[ts_].t[:, 0:nv], banks[bka][:, 2:nv + 2], ccol(cwi + 44 + f), t2[ts_].t[:, 0:nv], ALU.mult, ALU.add,
                                    [bbank[bka], cols.b, t2[ts_].b], [t1[ts_].b])
                                if pend[0] is not None:
                                    pend[0]()

                                def stage2(ts_=ts_, nv=nv, v0=v0, f=f, bku=bku):
                                    ACT(t2[ts_].t[:, 0:nv], t1[ts_].t[:, 0:nv], AF.Silu, [t1[ts_].b], [t2[ts_].b])
                                    TT("dve", gT.t[:, f, v0:v0 + nv], t2[ts_].t[:, 0:nv], banks[bku][:, 0:nv], ALU.mult,
                                       [t2[ts_].b, bbank[bku]], [gT.b])
                                pend[0] = stage2
                    pend[0]()
                    pend[0] = None
                    for dch in range(KC):
                        ds_ = dch % 2
                        if dch + 1 < KC:
                            ld_b(dch + 1)
                        if hb + 1 < NH and dch == 0:
                            ld_xw(hb + 1, 0)
                            ld_xw(hb + 1, 1)
                        for q in range(2):
                            bk = nb()
                            for f in range(NF):
                                MM(banks[bk][:], w2c[ds_].t[:, f, :], gT.t[:, f, q * 512:(q + 1) * 512], f == 0, f == NF - 1,
                                   [w2c[ds_].b, gT.b], [bbank[bk]])
                            TT("dve", xd[ds_].t[:, q * 512:(q + 1) * 512], xd[ds_].t[:, q * 512:(q + 1) * 512], banks[bk][:], ALU.add,
                               [xd[ds_].b, bbank[bk]], [xd[ds_].b])
                        DMA("sp", xa[dch * 128:(dch + 1) * 128, 1 + cb:1 + cb + HB], xd[ds_].t[:], [xd[ds_].b], [], f"st_xd{ds_}")
                        if hb + 1 < NH:
                            if dch == 1:
                                norm_w(hb + 1, 0)
                                ld_xw(hb + 1, 2)
                            if dch == 3:
                                norm_w(hb + 1, 1)
                            if dch == 5:
                                norm_w(hb + 1, 2)
                R.barrier()
            else:
                A = Arena(arena_t, ARN)
                xTc = [A.alloc(f"xTc{i}", [128, KC, TILE]) for i in range(2)]
                for t in range(NT):
                    s = t % 2
                    DMA("sp", xTc[s].t[:], xb_v[:, :, 1 + t * TILE:1 + (t + 1) * TILE], [], [xTc[s].b], f"ld_xTc{s}")
                    DMA("sp", xa_v[:, :, 1 + t * TILE:1 + (t + 1) * TILE], xTc[s].t[:], [xTc[s].b], [], f"st_xTc{s}")
                R.barrier()

        A = Arena(arena_t, ARN)
        xTf = [A.alloc(f"xTf{i}", [128, KC, TILE]) for i in range(2)]
        sqf = A.alloc("sqf", [128, KC, TILE], BF16)
        rstdf = A.alloc("rstdf", [128, TILE])
        yT = A.alloc("yT", [128, KC, TILE])
        yo = [A.alloc(f"yo{i}", [128, 4, D]) for i in range(2)]
        y_out_v = y_out.rearrange("(n g p) d -> n p g d", p=128, g=4)
        for t in range(NT):
            s = t % 2
            DMA("sp", xTf[s].t[:], xa_v[:, :, 1 + t * TILE:1 + (t + 1) * TILE], [], [xTf[s].b], f"ld_xTf{s}")
            ACT(sqf.t[:], xTf[s].t[:], AF.Square, [xTf[s].b], [sqf.b])
            bk = nb()
            for k in range(KC):
                MM(banks[bk][:], ones_bf.t[:], sqf.t[:, k, :], k == 0, k == KC - 1, [sqf.b, ones_bf.b], [bbank[bk]])
            ACT(rstdf.t[:], banks[bk][:], AF.Sqrt, [bbank[bk], epsc.b], [rstdf.b], scale=1.0 / D, bias=epsc.t[:])
            R.add("dve", lambda h: h.reciprocal(out=rstdf.t[:], in_=rstdf.t[:]), reads=[rstdf.b], writes=[rstdf.b])
            for k in range(KC):
                STT(yT.t[:, k, :], xTf[s].t[:, k, :], ccol(CI["nf"] + k), rstdf.t[:], ALU.mult, ALU.mult,
                    [xTf[s].b, rstdf.b, cols.b], [yT.b])
            for g in range(4):
                for half in range(2):
                    bk = nb()
                    for kk in range(4):
                        k = half * 4 + kk
                        TR(banks[bk][:, kk * 128:(kk + 1) * 128], yT.t[:, k, g * 128:(g + 1) * 128], ident.t[:],
                           [yT.b, ident.b], [bbank[bk]])
                    CP("act" if half == 0 else "dve", yo[s].t[:, g, half * 512:(half + 1) * 512], banks[bk][:], [bbank[bk]], [yo[s].b])
            DMA("sp", y_out_v[t], yo[s].t[:], [yo[s].b], [], f"st_yo{s}")

        R.finalize()
        for n in R.dma_counts:
            sems[n] = es.enter_context(nc.semaphore(n))

        with nc.Block() as block:
            @block.tensor
            def _(h):
                R.emit("pe", h, sems)

            @block.scalar
            def _(h):
                R.emit("act", h, sems)

            @block.vector
            def _(h):
                R.emit("dve", h, sems)

            @block.gpsimd
            def _(h):
                R.emit("pool", h, sems)

            @block.sync
            def _(h):
                R.emit("sp", h, sems)
                for k_, v_ in R.dma_counts.items():
                    h.wait_ge(sems[k_], v_)
    return nc


POOL_WINDOWS = (2, 4, 8, 16)


def shared_inputs(inp, L):
    f32 = np.float32
    CI = col_index(L)
    cols = np.zeros((128, CI["n"]), f32)

    def colmaj(v):
        return np.ascontiguousarray(np.asarray(v, f32).reshape(-1, 128).T)

    for l in range(L):
        cols[:, CI["n1"] + l * 8:CI["n1"] + (l + 1) * 8] = colmaj(inp["norm1_g"][l])
        cols[:, CI["n2"] + l * 8:CI["n2"] + (l + 1) * 8] = colmaj(inp["norm2_g"][l])
        for j in range(3):
            cols[:, CI["cw"] + (l * 3 + j) * 22:CI["cw"] + (l * 3 + j + 1) * 22] = colmaj(inp["conv_w"][l, j])
        cols[:, CI["cb"] + l * 22:CI["cb"] + (l + 1) * 22] = colmaj(inp["conv_b"][l])
        cols[:, CI["gn"] + l * 4:CI["gn"] + (l + 1) * 4] = colmaj(inp["gla_norm_g"][l])
        cols[:, CI["ps"] + l * 2:CI["ps"] + (l + 1) * 2] = colmaj(inp["pool_scale"][l])
    cols[:, CI["nf"]:CI["nf"] + 8] = colmaj(inp["norm_f"])
    jj = np.arange(128)
    trif = (jj[:, None] <= jj[None, :]).astype(f32)
    trib = (jj[:, None] >= jj[None, :]).astype(f32)
    sh = {
        "ident": np.eye(128, dtype=f32),
        "maskf": np.ascontiguousarray(np.tile(trif, (1, 4))),
        "maskb": np.ascontiguousarray(np.tile(trib, (1, 4))),
        "cols": cols,
    }
    if L == 0:
        return sh
    sh["w_in"] = np.ascontiguousarray(inp["w_in"][:L], f32)
    sh["w_o"] = np.ascontiguousarray(inp["w_o"][:L], f32)
    sh["w_ffn_in"] = np.ascontiguousarray(inp["w_ffn_in"][:L], f32)
    sh["w_ffn_out"] = np.ascontiguousarray(inp["w_ffn_out"][:L], f32)
    w2aug = np.zeros((L, 33, 512), f32)
    w2aug[:, 0:16, 0:256] = inp["gla_gate_w2"][:L, 0]
    w2aug[:, 16:32, 256:512] = inp["gla_gate_w2"][:L, 1]
    w2aug[:, 32, 0:256] = inp["gla_gate_b"][:L, 0]
    w2aug[:, 32, 256:512] = inp["gla_gate_b"][:L, 1]
    sh["w2aug"] = w2aug
    sh["wsT"] = np.ascontiguousarray(np.transpose(np.asarray(inp["sgu_w"][:L], f32), (0, 3, 1, 2)))
    sb = np.asarray(inp["sgu_b"][:L], f32)
    sgub = np.zeros((L, 128, 2, 128), f32)
    for j in range(2):
        for e in range(2):
            sgub[:, e * 64:(e + 1) * 64, j, :] = sb[:, 2 * j + e][:, None, :]
    sh["sgu_bias"] = sgub
    sh["lng"] = np.ascontiguousarray(np.broadcast_to(np.asarray(inp["sgu_ln_g"][:L], f32)[:, None, :], (L, 128, 256)))
    sh["lnb"] = np.ascontiguousarray(np.broadcast_to(np.asarray(inp["sgu_ln_b"][:L], f32)[:, None, :], (L, 128, 256)))
    bands = np.zeros((128, 3, 4, 128), f32)
    s_ = jj[:, None]
    t_ = jj[None, :]
    for pg, w in enumerate(POOL_WINDOWS):
        hw = w // 2
        for wi, off in enumerate((-128, 0, 128)):
            ss = s_ + off
            bands[:, wi, pg, :] = ((ss >= t_ - hw) & (ss <= t_ + hw - 1)).astype(f32)
    sh["bands"] = bands
    pw = np.asarray(inp["pool_w"][:L], f32)
    poolw = np.zeros((L, 128, 2, 64), f32)
    for j in range(2):
        for e in range(2):
            poolw[:, e * 64:(e + 1) * 64, j, :] = pw[:, 2 * j + e]
    sh["poolw"] = poolw
    return sh


def core_inputs(blocks, T, L):
    f32 = np.float32
    NB = T // BLK
    NT = T // TILE
    NH = T // HB
    assert len(blocks) == NB
    x = np.zeros((T, D), f32)
    contL = np.zeros(NB, f32)
    contR = np.zeros(NB, f32)
    invc = np.ones((4, T), f32)
    for i, (sid, bi, nbs, xb_) in enumerate(blocks):
        if xb_ is not None:
            x[i * BLK:(i + 1) * BLK] = xb_
        if i > 0 and sid is not None and blocks[i - 1][0] == sid and blocks[i - 1][1] == bi - 1:
            contL[i] = 1.0
        if i + 1 < NB and sid is not None and blocks[i + 1][0] == sid and blocks[i + 1][1] == bi + 1:
            contR[i] = 1.0
        S = nbs * BLK
        pos = bi * BLK + np.arange(BLK)
        for pg, w in enumerate(POOL_WINDOWS):
            hw = w // 2
            lo = np.clip(pos - hw, 0, S)
            hi = np.clip(pos + hw, 0, S)
            invc[pg, i * BLK:(i + 1) * BLK] = 1.0 / (hi - lo).astype(f32)
    NT2 = T // TILE2
    TPB = BLK // TILE2
    fTL = np.ones(NT2, f32)
    fTR = np.ones(NT2, f32)
    fHL = np.ones(NH, f32)
    fHR = np.ones(NH, f32)
    for b in range(NB):
        fTL[b * TPB] = contL[b]
        fTR[b * TPB + TPB - 1] = contR[b]
        fHL[b * 2] = contL[b]
        fHR[b * 2 + 1] = contR[b]
    bc = lambda v: np.ascontiguousarray(np.broadcast_to(v[None, :], (128, v.shape[0])))
    d = {"x_tok": x, "flagT_L": bc(fTL), "flagT_R": bc(fTR), "flagH_L": bc(fHL), "flagH_R": bc(fHR)}
    if L > 0:
        ic = np.zeros((128, 2, T), f32)
        for j in range(2):
            for e in range(2):
                ic[e * 64:(e + 1) * 64, j, :] = invc[2 * j + e][None, :]
        d["inv_cnt"] = ic
    return d


_PROG = {}


def run_cores(inp, core_blocks, T, L, cfg_extra=None):
    cfg = dict(T=T, L=L)
    if cfg_extra:
        cfg.update(cfg_extra)
    key = tuple(sorted(cfg.items()))
    if key not in _PROG:
        _PROG[key] = build_program(cfg)
    nc = _PROG[key]
    sh = shared_inputs(inp, L)
    in_maps = []
    for blocks in core_blocks:
        m = dict(sh)
        m.update(core_inputs(blocks, T, L))
        in_maps.append(m)
    res = run_bass_kernel_spmd(nc, in_maps, core_ids=list(range(len(core_blocks))))
    return [r["y_tok"] for r in res.results]


def kernel(**inputs):
    inp = {k: np.asarray(v) for k, v in inputs.items()}
    xp = inp["x_prompt"]
    xs = inp["x_sample"]
    T = 16384
    NB = T // BLK
    core_blocks = []
    for c in range(2):
        core_blocks.append([(("s", c), b, NB, xs[c, b * BLK:(b + 1) * BLK]) for b in range(NB)])
    counts = [3, 3, 3, 3, 2, 2]
    nxt = 0
    owner = {}
    for c, n in enumerate(counts):
        bl = []
        for i in range(n):
            owner[nxt] = (c + 2, i)
            bl.append((("p", nxt), 0, 1, xp[nxt]))
            nxt += 1
        while len(bl) < NB:
            bl.append((None, 0, 1, None))
        core_blocks.append(bl)
    ys = run_cores(inp, core_blocks, T, DEPTH)
    y_prompt = np.zeros(xp.shape, np.float32)
    y_sample = np.zeros(xs.shape, np.float32)
    for c in range(2):
        y_sample[c] = ys[c]
    for sid, (c, i) in owner.items():
        y_prompt[sid] = ys[c][i * BLK:(i + 1) * BLK]
    return (y_prompt, y_sample)
```

```python
import numpy as np
import concourse.bass as bass
import concourse.mybir as mybir
from concourse.bass_utils import run_bass_kernel_spmd

F32 = mybir.dt.float32
BF16 = mybir.dt.bfloat16
AF = mybir.ActivationFunctionType
ALU = mybir.AluOpType

D = 1024
KC = 8
DEPTH = 4
BLK = 2048
TILE = 512
TILE2 = 256
GP = 2
EPS = 1e-6


class Buf:
    __slots__ = ("name", "writer", "readers", "excl")

    def __init__(self, name, excl=False):
        self.name = name
        self.writer = None
        self.readers = []
        self.excl = excl


class Op:
    __slots__ = ("eng", "fn", "deps", "semkey", "val", "needs_inc", "is_dma", "inc")


class Rec:
    ENGS = ("pe", "act", "dve", "pool", "sp")

    def __init__(self):
        self.ops = {e: [] for e in self.ENGS}
        self.dma_counts = {}
        self.last_dma = {}
        self.pending = {e: [] for e in self.ENGS}

    def _dep(self, op, d, raw):
        if d is None or d is op:
            return
        if not d.is_dma and d.eng == op.eng:
            if op.eng in ("pe", "sp") or not raw:
                return
        op.deps.append(d)
        d.needs_inc = True

    def add(self, eng, fn, reads=(), writes=(), dma_sem=None):
        op = Op()
        op.eng = eng
        op.fn = fn
        op.deps = []
        op.needs_inc = False
        op.is_dma = dma_sem is not None
        op.val = None
        if op.is_dma:
            c = self.dma_counts.get(dma_sem, 0) + 16
            self.dma_counts[dma_sem] = c
            op.semkey = dma_sem
            op.val = c
            op.inc = 16
            op.needs_inc = True
        else:
            op.semkey = eng
            op.inc = 1
        if self.pending[eng]:
            for d in self.pending[eng]:
                if d.is_dma or d.eng != eng:
                    op.deps.append(d)
                    d.needs_inc = True
            self.pending[eng] = []
        if op.is_dma:
            self.last_dma[dma_sem] = op
        for b in reads:
            self._dep(op, b.writer, True)
            if b.excl:
                for r in b.readers:
                    if r.eng != eng:
                        self._dep(op, r, False)
        for b in writes:
            self._dep(op, b.writer, False)
            for r in b.readers:
                self._dep(op, r, False)
        for b in reads:
            b.readers.append(op)
        for b in writes:
            b.writer = op
            b.readers = []
        self.ops[eng].append(op)
        return op

    def barrier(self):
        deps = []
        for e in self.ENGS:
            for op in reversed(self.ops[e]):
                if not op.is_dma:
                    deps.append(op)
                    break
        deps += list(self.last_dma.values())
        for e in self.ENGS:
            self.pending[e] = list(deps)

    def finalize(self):
        for e in self.ENGS:
            c = 0
            for op in self.ops[e]:
                if not op.is_dma and op.needs_inc:
                    c += 1
                    op.val = c

    def emit(self, eng, handle, sems):
        waited = {}
        for op in self.ops[eng]:
            need = {}
            for d in op.deps:
                v = need.get(d.semkey, 0)
                if d.val > v:
                    need[d.semkey] = d.val
            for k, v in need.items():
                if waited.get(k, 0) < v:
                    handle.wait_ge(sems[k], v)
                    waited[k] = v
            inst = op.fn(handle)
            if op.needs_inc:
                inst.then_inc(sems[op.semkey], op.inc)
        return waited


W_Q, W_K, W_V, W_G, W_LR, W_U, W_VS, W_XP = 0, 256, 512, 1024, 1536, 1568, 1824, 2080
INW = 2336
DFF = 2816
NF = 22
HB = 1024
WV = (342, 341, 341)
WS = (0, 342, 683)


class TK:
    __slots__ = ("t", "b")

    def __init__(self, t, name):
        self.t = t
        self.b = Buf(name)


class Arena:
    def __init__(self, t, nf32):
        self.t = t
        self.n = nf32
        self.off = 0

    def alloc(self, name, shape, dt=F32):
        nel = 1
        for s_ in shape[1:]:
            nel *= s_
        nbytes = nel * (2 if dt == BF16 else 4)
        nf = (nbytes + 3) // 4
        if self.off % 2:
            self.off += 1
        v = self.t[0:shape[0], self.off:self.off + nf]
        self.off += nf
        assert self.off <= self.n, (name, self.off, self.n)
        if dt == BF16:
            v = v.bitcast(BF16)[:, 0:nel]
        if len(shape) == 3:
            v = v.rearrange("p (a b) -> p a b", a=shape[1])
        elif len(shape) == 4:
            v = v.rearrange("p (a b c) -> p a b c", a=shape[1], b=shape[2])
        return TK(v, name)


def col_index(L):
    o = {}
    o["n1"] = 0
    o["n2"] = L * 8
    o["nf"] = 2 * L * 8
    o["cw"] = o["nf"] + 8
    o["cb"] = o["cw"] + L * 66
    o["gn"] = o["cb"] + L * 22
    o["ps"] = o["gn"] + L * 4
    o["n"] = o["ps"] + L * 2
    return o


def build_program(cfg):
    from contextlib import ExitStack
    T = cfg["T"]
    L = cfg["L"]
    NT = T // TILE
    NH = T // HB
    do_mixer = cfg.get("mixer", True)
    do_ffn = cfg.get("ffn", True)
    CI = col_index(L)
    nc = bass.Bass("TRN2", target_bir_lowering=False)
    R = Rec()

    def din(name, shape, dt=F32):
        return nc.dram_tensor(name, list(shape), dt, kind="ExternalInput").ap()

    def dscr(name, shape, dt=F32):
        return nc.dram_tensor(name, list(shape), dt, kind="Internal").ap()

    x_in = din("x_tok", [T, D])
    y_out = nc.dram_tensor("y_tok", [T, D], F32, kind="ExternalOutput").ap()
    ident_d = din("ident", [128, 128])
    maskf_d = din("maskf", [128, 512])
    maskb_d = din("maskb", [128, 512])
    cols_d = din("cols", [128, CI["n"]])
    fTL_d = din("flagT_L", [128, T // TILE2])
    fTR_d = din("flagT_R", [128, T // TILE2])
    fHL_d = din("flagH_L", [128, NH])
    fHR_d = din("flagH_R", [128, NH])
    if L > 0:
        invc_d = din("inv_cnt", [128, 2, T])
        w_in_d = din("w_in", [L, D, INW])
        w_o_d = din("w_o", [L, D, D])
        w_f1_d = din("w_ffn_in", [L, D, 2 * DFF])
        w_f2_d = din("w_ffn_out", [L, DFF, D])
        w2aug_d = din("w2aug", [L, 33, 512])
        wsT_d = din("wsT", [L, 128, 4, 128])
        sgub_d = din("sgu_bias", [L, 128, 2, 128])
        lng_d = din("lng", [L, 128, 256])
        lnb_d = din("lnb", [L, 128, 256])
        bands_d = din("bands", [128, 3, 4, 128])
        poolw_d = din("poolw", [L, 128, 2, 64])
    xa = dscr("xa_scr", [D, T + 2])
    xb = dscr("xb_scr", [D, T + 2])
    xa_v = xa.rearrange("(k p) t -> p k t", p=128)
    xb_v = xb.rearrange("(k p) t -> p k t", p=128)
    sbst_d = dscr("sbst_scr", [T // TILE2, 128, 256])
    stash_d = dscr("stash_scr", [T // TILE2, 128, 256], BF16)

    with ExitStack() as es:
        def sbt(name, shape, dt=F32):
            return TK(es.enter_context(nc.sbuf_tensor("s_" + name, list(shape), dt)), name)

        sems = {}
        for n in ["pe", "act", "dve", "pool"]:
            sems[n] = es.enter_context(nc.semaphore(n))
        banks = [es.enter_context(nc.psum_tensor(f"bank{i}", [128, 512], F32)) for i in range(8)]
        bbank = [Buf(f"bank{i}", excl=True) for i in range(8)]
        bctr = [0]

        def nb():
            i = bctr[0] % 8
            bctr[0] += 1
            return i

        def MM(out, lhsT, rhs, start, stop, rd, wr):
            R.add("pe", lambda h: h.matmul(out, lhsT=lhsT, rhs=rhs, start=start, stop=stop), reads=rd, writes=wr)

        def TR(out, in_, idn, rd, wr):
            R.add("pe", lambda h: h.transpose(out=out, in_=in_, identity=idn), reads=rd, writes=wr)

        def ACT(out, in_, func, rd, wr, scale=None, bias=None):
            kw = {}
            if scale is not None:
                kw["scale"] = scale
            if bias is not None:
                kw["bias"] = bias
            R.add("act", lambda h: h.activation(out=out, in_=in_, func=func, **kw), reads=rd, writes=wr)

        def CP(eng, out, in_, rd, wr):
            if eng == "act":
                R.add("act", lambda h: h.copy(out=out, in_=in_), reads=rd, writes=wr)
            else:
                R.add(eng, lambda h: h.tensor_copy(out=out, in_=in_), reads=rd, writes=wr)

        def TT(eng, out, in0, in1, op, rd, wr):
            R.add(eng, lambda h: h.tensor_tensor(out=out, in0=in0, in1=in1, op=op), reads=rd, writes=wr)

        def STT(out, in0, scalar, in1, op0, op1, rd, wr):
            R.add("dve", lambda h: h.scalar_tensor_tensor(out=out, in0=in0, scalar=scalar, in1=in1, op0=op0, op1=op1),
                  reads=rd, writes=wr)

        def TS(eng, out, in0, s1, op0, rd, wr, s2=None, op1=None):
            if op1 is None and eng == "pool" and op0 == ALU.mult:
                s2, op1 = 1.0, ALU.mult
            if op1 is None:
                R.add(eng, lambda h: h.tensor_scalar(out=out, in0=in0, scalar1=s1, scalar2=None, op0=op0), reads=rd, writes=wr)
            else:
                R.add(eng, lambda h: h.tensor_scalar(out=out, in0=in0, scalar1=s1, scalar2=s2, op0=op0, op1=op1),
                      reads=rd, writes=wr)

        def MSET(eng, ap, val, wr):
            R.add(eng, lambda h: h.memset(ap, val), writes=wr)

        def DMA(q, out, in_, rd, wr, sem, **kw):
            R.add(q, lambda h: h.dma_start(out=out, in_=in_, **kw), reads=rd, writes=wr, dma_sem=sem)

        ident = sbt("ident", [128, 128])
        ident_bf = sbt("ident_bf", [128, 128], BF16)
        maskf = sbt("maskf", [128, 512])
        maskb = sbt("maskb", [128, 512])
        ones_bf = sbt("ones_bf", [128, 128], BF16)
        epsc = sbt("epsc", [128, 1])
        onec = sbt("onec", [128, 1])
        cols = sbt("cols", [128, CI["n"]])
        fTL = sbt("fTL", [128, T // TILE2])
        fTR = sbt("fTR", [128, T // TILE2])
        fHL = sbt("fHL", [128, NH])
        fHR = sbt("fHR", [128, NH])
        zer = sbt("zer", [128, 256])
        for tk, d_ in ((ident, ident_d), (maskf, maskf_d), (maskb, maskb_d), (cols, cols_d), (fTL, fTL_d),
                       (fTR, fTR_d), (fHL, fHL_d), (fHR, fHR_d)):
            DMA("sp", tk.t[:], d_, [], [tk.b], "ld_" + tk.b.name)
        MSET("dve", ones_bf.t[:], 1.0, [ones_bf.b])
        MSET("dve", epsc.t[:], EPS, [epsc.b])
        MSET("dve", onec.t[:], 1.0, [onec.b])
        MSET("dve", zer.t[:], 0.0, [zer.b])
        CP("dve", ident_bf.t[:], ident.t[:], [ident.b], [ident_bf.b])
        trif = maskf.t[:, 0:128]
        trib = maskb.t[:, 0:128]
        for scr, nm in ((xa, "a"), (xb, "b")):
            for kk in range(KC):
                DMA("sp", scr[kk * 128:(kk + 1) * 128, 0:1], zer.t[:, 0:1], [zer.b], [], "st_zer",
                    allow_slow_non_contiguous=True)
                DMA("sp", scr[kk * 128:(kk + 1) * 128, T + 1:T + 2], zer.t[:, 0:1], [zer.b], [], "st_zer",
                    allow_slow_non_contiguous=True)

        def ccol(i):
            return cols.t[:, i:i + 1]

        ARN = 34500
        arena_t = es.enter_context(nc.sbuf_tensor("arena", [128, ARN], F32))

        def norm_tile(x3, xbuf, n, gbase, sq, rstd, out3, outbuf):
            ACT(sq.t[:, :, 0:n], x3, AF.Square, [xbuf], [sq.b])
            bk = nb()
            for k in range(KC):
                MM(banks[bk][:, 0:n], ones_bf.t[:], sq.t[:, k, 0:n], k == 0, k == KC - 1, [sq.b, ones_bf.b], [bbank[bk]])
            ACT(rstd.t[:, 0:n], banks[bk][:, 0:n], AF.Ln, [bbank[bk], epsc.b], [rstd.b], scale=1.0 / D, bias=epsc.t[:])
            ACT(rstd.t[:, 0:n], rstd.t[:, 0:n], AF.Exp, [rstd.b], [rstd.b], scale=-0.5)
            for k in range(KC):
                STT(out3[:, k, :], x3[:, k, :], ccol(gbase + k), rstd.t[:, 0:n], ALU.mult, ALU.mult,
                    [xbuf, rstd.b, cols.b], [outbuf])

        A0 = Arena(arena_t, ARN)
        xin_t = [A0.alloc(f"xin{i}", [128, 4, D]) for i in range(2)]
        xT0 = [A0.alloc(f"xT0_{i}", [128, KC, TILE]) for i in range(2)]
        x_in_v = x_in.rearrange("(n g p) d -> n p g d", p=128, g=4)
        DMA("sp", xin_t[0].t[:], x_in_v[0], [], [xin_t[0].b], "ld_xin0")
        for t in range(NT):
            s = t % 2
            if t + 1 < NT:
                DMA("sp", xin_t[1 - s].t[:], x_in_v[t + 1], [], [xin_t[1 - s].b], f"ld_xin{1 - s}")
            for k in range(KC):
                bk = nb()
                for g in range(4):
                    TR(banks[bk][:, g * 128:(g + 1) * 128], xin_t[s].t[:, g, k * 128:(k + 1) * 128], ident.t[:],
                       [xin_t[s].b, ident.b], [bbank[bk]])
                CP("act" if k % 2 == 0 else "dve", xT0[s].t[:, k, :], banks[bk][:], [bbank[bk]], [xT0[s].b])
            DMA("sp", xa_v[:, :, 1 + t * TILE:1 + (t + 1) * TILE], xT0[s].t[:], [xT0[s].b], [], f"st_xT0_{s}")
        R.barrier()

        if L > 0:
            win = sbt("win", [128, KC, INW], BF16)
            wo = sbt("wo", [128, KC, D], BF16)
            w2aug = sbt("w2aug", [33, 512])
            wsT = sbt("wsT", [128, 4, 128], BF16)
            sgub = sbt("sgub", [128, 2, 128])
            lng = sbt("lng", [128, 256])
            lnb = sbt("lnb", [128, 256])
            bands = sbt("bands", [128, 3, 4, 128], BF16)
            poolw = sbt("poolw", [128, 2, 64], BF16)
            DMA("pool", bands.t[:], bands_d, [], [bands.b], "ld_bands")

        for l in range(L):
            if do_mixer:
                wv = w_in_d[l].rearrange("(k p) c -> p k c", p=128)
                for k in range(KC):
                    DMA("pool", win.t[:, k, :], wv[:, k, :], [], [win.b], "ld_win")
                wv = w_o_d[l].rearrange("(k p) c -> p k c", p=128)
                for k in range(0, KC, 2):
                    DMA("pool", wo.t[:, k:k + 2, :], wv[:, k:k + 2, :], [], [wo.b], "ld_wo")
                DMA("sp", w2aug.t[:], w2aug_d[l], [], [w2aug.b], "ld_w2aug")
                DMA("pool", wsT.t[:], wsT_d[l], [], [wsT.b], "ld_wsT")
                DMA("sp", sgub.t[:], sgub_d[l], [], [sgub.b], "ld_sgub")
                DMA("sp", lng.t[:], lng_d[l], [], [lng.b], "ld_lng")
                DMA("sp", lnb.t[:], lnb_d[l], [], [lnb.b], "ld_lnb")
                DMA("pool", poolw.t[:], poolw_d[l], [], [poolw.b], "ld_poolw")

            if do_mixer:
                A = Arena(arena_t, ARN)
                T2 = TILE2
                NT2 = T // T2

                class _Set:
                    pass

                sets = []
                for i_ in range(2):
                    S_ = _Set()
                    S_.i = i_
                    S_.qT = A.alloc(f"qT{i_}", [128, 2, T2])
                    S_.kT = A.alloc(f"kT{i_}", [128, 2, T2])
                    S_.lrT = A.alloc(f"lrT{i_}", [33, T2])
                    S_.sg = A.alloc(f"sg{i_}", [128, 4, T2])
                    S_.gu = A.alloc(f"gu{i_}", [128, 2, T2])
                    S_.xpT = A.alloc(f"xpT{i_}", [128, 2, T2])
                    S_.vtok = [A.alloc(f"vtok{i_}{g}", [128, 512], BF16) for g in range(GP)]
                    S_.vn = [A.alloc(f"vn{i_}{g}", [128, 256], BF16) for g in range(GP)]
                    S_.xpk = [A.alloc(f"xpk{i_}{g}", [128, 256], BF16) for g in range(GP)]
                    S_.xpl = A.alloc(f"xpl{i_}", [128, 256], BF16)
                    S_.xpr = A.alloc(f"xpr{i_}", [128, 256], BF16)
                    S_.stsh = A.alloc(f"stsh{i_}", [128, 256], BF16)
                    S_.invc = A.alloc(f"invc{i_}", [128, 2, T2])
                    S_.Sstg = A.alloc(f"Sstg{i_}", [128, 2, 128])
                    sets.append(S_)
                xT3 = [A.alloc(f"xT{i_}", [128, KC, T2]) for i_ in range(3)]

                def ld_x(t):
                    X_ = xT3[t % 3]
                    DMA("sp", X_.t[:], xa_v[:, :, 1 + t * T2:1 + (t + 1) * T2], [], [X_.b], f"ld_xT{t % 3}")

                hT = A.alloc("hT", [128, KC, T2], BF16)
                sq = hT
                rstd = A.alloc("rstd", [128, T2])
                e_tL = [A.alloc(f"e_t{g}", [128, 512]) for g in range(GP)]
                sp_tL = [A.alloc(f"sp_t{g}", [128, 512]) for g in range(GP)]
                EpL = [A.alloc(f"Ep{g}", [128, 4, 128]) for g in range(GP)]
                EmL = [A.alloc(f"Em{g}", [128, 4, 128]) for g in range(GP)]
                qd = [A.alloc(f"qd{g}", [128, 4, 128], BF16) for g in range(GP)]
                kd = [A.alloc(f"kd{g}", [128, 4, 128], BF16) for g in range(GP)]
                kdt = [A.alloc(f"kdt{g}", [128, 512], BF16) for g in range(GP)]
                egl = [A.alloc(f"egl{g}", [128, 4]) for g in range(GP)]
                AT = [A.alloc(f"AT{g}", [128, 8, 128], BF16) for g in range(GP)]
                Sf = A.alloc("Sf", [128, 2, 128])
                Sb = A.alloc("Sb", [128, 2, 128])
                Stmp = A.alloc("Stmp", [128, 2, 128])
                kvs = [[A.alloc(f"kvs{d_}{g}", [128, 2, 128]) for g in range(GP)] for d_ in range(2)]
                Sbf = [[A.alloc(f"Sbf{d_}{g}", [128, 2, 2, 128], BF16) for g in range(GP)] for d_ in range(2)]
                osbL = [A.alloc(f"osb{g}", [128, 512]) for g in range(GP)]
                osqL = [A.alloc(f"osq{g}", [128, 512], BF16) for g in range(GP)]
                rsL = [A.alloc(f"rs{g}", [128, 512]) for g in range(GP)]
                otmpL = [A.alloc(f"otmp{g}", [128, 512]) for g in range(GP)]
                vsg = A.alloc("vsg", [128, 256])
                vtmp = A.alloc("vtmp", [128, 256])
                stats = A.alloc("stats", [128, 8])
                mv = A.alloc("mv", [128, 4])
                stmp = A.alloc("stmp", [128, 2, 128])
                ptmp = A.alloc("ptmp", [128, 2, 128])
                dT = A.alloc("dT", [128, 2, T2], BF16)
                mixT = A.alloc("mixT", [128, KC, T2], BF16)

                for S_ in sets:
                    MSET("pool", S_.lrT.t[32:33, :], 1.0, [S_.lrT.b])
                for d_ in range(2):
                    for g in range(GP):
                        MSET("pool", Sbf[d_][g].t[:], 0.0, [Sbf[d_][g].b])

                def proj_fm(c0, m):
                    bk = nb()
                    for k in range(KC):
                        MM(banks[bk][0:m, 0:T2], win.t[:, k, c0:c0 + m], hT.t[:, k, :], k == 0, k == KC - 1,
                           [win.b, hT.b], [bbank[bk]])
                    return bk

                def proj_tm(g, c0, n):
                    bk = nb()
                    for k in range(KC):
                        MM(banks[bk][:, 0:n], hT.t[:, k, g * 128:(g + 1) * 128], win.t[:, k, c0:c0 + n], k == 0,
                           k == KC - 1, [win.b, hT.b], [bbank[bk]])
                    return bk

                def chunk_gen(S_, g, dirs, want_q, want_kv):
                    e_t, sp_t, Ep, Em = e_tL[g], sp_tL[g], EpL[g], EmL[g]
                    gc = slice(g * 128, (g + 1) * 128)
                    c0 = 0 if 0 in dirs else 256
                    n = 256 * len(dirs)
                    bk = nb()
                    MM(banks[bk][:, 0:n], S_.lrT.t[:, gc], w2aug.t[:, c0:c0 + n], True, True,
                       [S_.lrT.b, w2aug.b], [bbank[bk]])
                    ACT(e_t.t[:, 0:n], banks[bk][:, 0:n], AF.Exp, [bbank[bk]], [e_t.b], scale=-1.0)
                    ACT(sp_t.t[:, 0:n], e_t.t[:, 0:n], AF.Ln, [e_t.b, onec.b], [sp_t.b], bias=onec.t[:])
                    yield
                    bk2 = nb()
                    for i_, d_ in enumerate(dirs):
                        for hh in range(2):
                            bi = d_ * 2 + hh
                            MM(banks[bk2][:, bi * 128:(bi + 1) * 128], sp_t.t[:, (i_ * 2 + hh) * 128:(i_ * 2 + hh + 1) * 128],
                               trif if d_ == 0 else trib, True, True, [sp_t.b, maskf.b, maskb.b], [bbank[bk2]])
                    b0 = dirs[0] * 2
                    b1 = dirs[-1] * 2 + 2
                    gv = banks[bk2][:, b0 * 128:b1 * 128].rearrange("p (a b) -> p a b", b=128)
                    ACT(Ep.t[:, b0:b1, :], gv, AF.Exp, [bbank[bk2]], [Ep.b], scale=-1.0 / 16)
                    ACT(Em.t[:, b0:b1, :], gv, AF.Exp, [bbank[bk2]], [Em.b], scale=1.0 / 16)
                    if 0 in dirs:
                        CP("pool", egl[g].t[:, 0:2], Ep.t[:, 0:2, 127], [Ep.b], [egl[g].b])
                    if 1 in dirs:
                        CP("pool", egl[g].t[:, 2:4], Ep.t[:, 2:4, 0], [Ep.b], [egl[g].b])
                    yield
                    for d_ in dirs:
                        TT("dve", kd[g].t[:, d_ * 2:d_ * 2 + 2, :], S_.kT.t[:, :, gc],
                           Em.t[:, d_ * 2:d_ * 2 + 2, :], ALU.mult, [S_.kT.b, Em.b], [kd[g].b])
                        if want_q:
                            TT("dve", qd[g].t[:, d_ * 2:d_ * 2 + 2, :], S_.qT.t[:, :, gc], Ep.t[:, d_ * 2:d_ * 2 + 2, :], ALU.mult,
                               [S_.qT.b, Ep.b], [qd[g].b])
                    bk = nb()
                    bv = banks[bk][:].bitcast(BF16)
                    for bi in range(b0, b1):
                        TR(bv[:, bi * 128:(bi + 1) * 128], kd[g].t[:, bi, :], ident_bf.t[:], [kd[g].b, ident_bf.b], [bbank[bk]])
                    CP("act", kdt[g].t[:, b0 * 128:b1 * 128], bv[:, b0 * 128:b1 * 128], [bbank[bk]], [kdt[g].b])
                    yield
                    for d_ in dirs:
                        if not want_kv[d_]:
                            continue
                        bk = nb()
                        for hd in range(4):
                            p0 = (hd % 2) * 64
                            MM(banks[bk][p0:p0 + 64, (hd // 2) * 128:(hd // 2 + 1) * 128],
                               kdt[g].t[:, d_ * 256 + hd * 64:d_ * 256 + (hd + 1) * 64], S_.vtok[g].t[:, hd * 128:(hd + 1) * 128],
                               True, True, [kdt[g].b, S_.vtok[g].b], [bbank[bk]])
                        for hh in range(2):
                            ACT(kvs[d_][g].t[:, hh, :], banks[bk][:, hh * 128:(hh + 1) * 128], AF.Identity,
                                [bbank[bk], egl[g].b], [kvs[d_][g].b], scale=egl[g].t[:, d_ * 2 + hh:d_ * 2 + hh + 1])
                    yield
                    if want_q:
                        for e in range(2):
                            bk = nb()
                            p0 = e * 64
                            for bi in range(4):
                                MM(banks[bk][:, bi * 128:(bi + 1) * 128], kd[g].t[p0:p0 + 64, bi, :], qd[g].t[p0:p0 + 64, bi, :],
                                   True, True, [kd[g].b, qd[g].b], [bbank[bk]])
                            av = banks[bk][:].rearrange("p (a b) -> p a b", b=128)
                            TT("dve", AT[g].t[:, e * 4:e * 4 + 2, :], av[:, 0:2, :], maskf.t[:, 0:256].rearrange("p (a b) -> p a b", b=128),
                               ALU.mult, [bbank[bk], maskf.b], [AT[g].b])
                            TT("dve", AT[g].t[:, e * 4 + 2:e * 4 + 4, :], av[:, 2:4, :], maskb.t[:, 0:256].rearrange("p (a b) -> p a b", b=128),
                               ALU.mult, [bbank[bk], maskb.b], [AT[g].b])
                            yield

                def out_gen(S_, g):
                    osb, osq, rs, otmp = osbL[g], osqL[g], rsL[g], otmpL[g]
                    gc = slice(g * 128, (g + 1) * 128)
                    bk = nb()
                    for hd in range(4):
                        hh = hd // 2
                        e = hd % 2
                        o_ap = banks[bk][:, hd * 128:(hd + 1) * 128]
                        MM(o_ap, S_.vtok[g].t[:, hd * 128:(hd + 1) * 128], AT[g].t[:, e * 4 + 0 + hh, :], True, False,
                           [S_.vtok[g].b, AT[g].b], [bbank[bk]])
                        MM(o_ap, S_.vtok[g].t[:, hd * 128:(hd + 1) * 128], AT[g].t[:, e * 4 + 2 + hh, :], False, False,
                           [S_.vtok[g].b, AT[g].b], [bbank[bk]])
                        MM(o_ap, Sbf[0][g].t[:, hh, e, :], qd[g].t[:, 0 + hh, :], False, False,
                           [Sbf[0][g].b, qd[g].b], [bbank[bk]])
                        MM(o_ap, Sbf[1][g].t[:, hh, e, :], qd[g].t[:, 2 + hh, :], False, True,
                           [Sbf[1][g].b, qd[g].b], [bbank[bk]])
                    CP("act", osb.t[:], banks[bk][:], [bbank[bk]], [osb.b])
                    ACT(osq.t[:], banks[bk][:], AF.Square, [bbank[bk]], [osq.b])
                    yield
                    bk2 = nb()
                    MM(banks[bk2][:], ones_bf.t[:], osq.t[:], True, True, [osq.b, ones_bf.b], [bbank[bk2]])
                    ACT(rs.t[:], banks[bk2][:], AF.Ln, [bbank[bk2], epsc.b], [rs.b], scale=1.0 / 128, bias=epsc.t[:])
                    ACT(rs.t[:], rs.t[:], AF.Exp, [rs.b], [rs.b], scale=-0.5)
                    TT("dve", otmp.t[:], osb.t[:], rs.t[:], ALU.mult, [osb.b, rs.b], [otmp.b])
                    yield
                    for hd in range(4):
                        STT(mixT.t[:, hd, gc], otmp.t[:, hd * 128:(hd + 1) * 128], ccol(CI["gn"] + l * 4 + hd), S_.sg.t[:, hd, gc],
                            ALU.mult, ALU.mult, [otmp.b, cols.b, S_.sg.b], [mixT.b])
                    yield

                def lockstep(gens):
                    gens = list(gens)
                    while gens:
                        for g_ in list(gens):
                            try:
                                next(g_)
                                yield
                            except StopIteration:
                                gens.remove(g_)

                def state_step(S, g, d_):
                    for hh in range(2):
                        STT(S.t[:, hh, :], S.t[:, hh, :], egl[g].t[:, d_ * 2 + hh:d_ * 2 + hh + 1], kvs[d_][g].t[:, hh, :],
                            ALU.mult, ALU.add, [S.b, egl[g].b, kvs[d_][g].b], [S.b])

                def state_to_bf(S, dst):
                    CP("act", dst.t[0:64, :, 0, :], S.t[0:64, :, :], [S.b], [dst.b])
                    CP("act", dst.t[64:128, :, 1, :], S.t[64:128, :, :], [S.b], [dst.b])

                def bank3(bk):
                    return banks[bk][:, 0:256].rearrange("p (a b) -> p a b", b=128)

                def pipeline(order, genA, genB):
                    for _ in genA(order[0]):
                        pass
                    for i_, t in enumerate(order):
                        gb = genB(t)
                        ga = genA(order[i_ + 1]) if i_ + 1 < len(order) else iter(())
                        tog = 0
                        RATIO = cfg.get("ratio", 2)
                        for hint in gb:
                            if hint is None:
                                tog += 1
                                hint = 1 if tog % RATIO == 0 else 0
                            for _ in range(hint):
                                try:
                                    next(ga)
                                except StopIteration:
                                    break
                        for _ in ga:
                            pass

                def s1A(t):
                    S_ = sets[t % 2]
                    X_ = xT3[t % 3]
                    if t - 1 >= 0:
                        ld_x(t - 1)
                    norm_tile(X_.t[:], X_.b, T2, CI["n1"] + l * 8, sq, rstd, hT.t, hT.b)
                    yield
                    for c in range(2):
                        bk = proj_fm(W_K + c * 128, 128)
                        CP("act", S_.kT.t[:, c, :], banks[bk][:, 0:T2], [bbank[bk]], [S_.kT.b])
                    bk = proj_fm(W_LR, 32)
                    CP("act", S_.lrT.t[0:32, :], banks[bk][0:32, 0:T2], [bbank[bk]], [S_.lrT.b])
                    yield
                    for g in range(GP):
                        bk = proj_tm(g, W_V, 512)
                        CP("dve", S_.vtok[g].t[:], banks[bk][:], [bbank[bk]], [S_.vtok[g].b])
                        yield
                    if t >= 1:
                        bk = proj_tm(0, W_XP, 256)
                        TS("dve", S_.stsh.t[:], banks[bk][:, 0:256], fTR.t[:, t - 1:t], ALU.mult, [bbank[bk], fTR.b], [S_.stsh.b])
                        DMA("sp", stash_d[t - 1], S_.stsh.t[:], [S_.stsh.b], [], f"st_stsh{S_.i}")

                def s1B(t):
                    S_ = sets[t % 2]
                    TS("dve", Sb.t[:], Sb.t[:], fTR.t[:, t:t + 1], ALU.mult, [Sb.b, fTR.b], [Sb.b])
                    CP("pool", S_.Sstg.t[:], Sb.t[:], [Sb.b], [S_.Sstg.b])
                    DMA("sp", sbst_d[t].rearrange("p (a b) -> p a b", b=128), S_.Sstg.t[:], [S_.Sstg.b], [], f"st_Sstg{S_.i}")
                    yield from lockstep([chunk_gen(S_, g, (1,), False, {1: True}) for g in reversed(range(GP))])
                    for g in reversed(range(GP)):
                        state_step(Sb, g, 1)
                    yield 2

                MSET("dve", Sb.t[:], 0.0, [Sb.b])
                DMA("sp", stash_d[NT2 - 1], zer.t[:, 0:128].bitcast(BF16), [zer.b], [], "st_zer")
                if cfg.get("s1", 9) > 0:
                    ld_x(NT2 - 1)
                    pipeline(list(reversed(range(NT2))), s1A, s1B)
                R.barrier()

                def s2A(t):
                    S_ = sets[t % 2]
                    O_ = sets[1 - t % 2]
                    sl = S_.i
                    X_ = xT3[t % 3]
                    if t + 1 < NT2:
                        ld_x(t + 1)
                    DMA("sp", S_.invc.t[:], invc_d[:, :, t * T2:(t + 1) * T2], [], [S_.invc.b], f"ld_invc{sl}")
                    DMA("sp", S_.Sstg.t[:], sbst_d[t].rearrange("p (a b) -> p a b", b=128), [], [S_.Sstg.b], f"ld_Sstg{sl}")
                    DMA("sp", S_.xpr.t[:], stash_d[t], [], [S_.xpr.b], f"ld_xpr{sl}")
                    if t >= 1:
                        TS("pool", S_.xpl.t[:], O_.xpk[GP - 1].t[:], fTL.t[:, t:t + 1], ALU.mult, [O_.xpk[GP - 1].b, fTL.b], [S_.xpl.b])
                    else:
                        MSET("pool", S_.xpl.t[:], 0.0, [S_.xpl.b])
                    norm_tile(X_.t[:], X_.b, T2, CI["n1"] + l * 8, sq, rstd, hT.t, hT.b)
                    yield
                    for c in range(2):
                        bk = proj_fm(W_Q + c * 128, 128)
                        ACT(S_.qT.t[:, c, :], banks[bk][:, 0:T2], AF.Copy, [bbank[bk]], [S_.qT.b], scale=0.125)
                    for c in range(2):
                        bk = proj_fm(W_K + c * 128, 128)
                        CP("act", S_.kT.t[:, c, :], banks[bk][:, 0:T2], [bbank[bk]], [S_.kT.b])
                    yield
                    bk = proj_fm(W_LR, 32)
                    CP("act", S_.lrT.t[0:32, :], banks[bk][0:32, 0:T2], [bbank[bk]], [S_.lrT.b])
                    for c in range(4):
                        bk = proj_fm(W_G + c * 128, 128)
                        ACT(S_.sg.t[:, c, :], banks[bk][:, 0:T2], AF.Silu, [bbank[bk]], [S_.sg.b])
                        if c == 1:
                            yield
                    yield
                    for c in range(2):
                        bk = proj_fm(W_U + c * 128, 128)
                        ACT(S_.gu.t[:, c, :], banks[bk][:, 0:T2], AF.Gelu_apprx_tanh, [bbank[bk]], [S_.gu.b])
                    for c in range(2):
                        bk = proj_fm(W_XP + c * 128, 128)
                        CP("act", S_.xpT.t[:, c, :], banks[bk][:, 0:T2], [bbank[bk]], [S_.xpT.b])
                    yield
                    for g in range(GP):
                        bk = proj_tm(g, W_V, 512)
                        CP("dve", S_.vtok[g].t[:], banks[bk][:], [bbank[bk]], [S_.vtok[g].b])
                        yield
                        bk = proj_tm(g, W_VS, 512)
                        ACT(vsg.t[:], banks[bk][:, 0:256], AF.Gelu_apprx_tanh, [bbank[bk]], [vsg.b])
                        CP("act", S_.xpk[g].t[:], banks[bk][:, 256:512], [bbank[bk]], [S_.xpk[g].b])
                        R.add("dve", lambda h: h.bn_stats(out=stats.t[:, 0:6], in_=vsg.t[:]), reads=[vsg.b], writes=[stats.b])
                        R.add("dve", lambda h: h.bn_aggr(out=mv.t[:, 0:2], in_=stats.t[:, 0:6]), reads=[stats.b], writes=[mv.b])
                        ACT(mv.t[:, 2:3], mv.t[:, 1:2], AF.Ln, [mv.b, epsc.b], [mv.b], bias=epsc.t[:])
                        ACT(mv.t[:, 3:4], mv.t[:, 2:3], AF.Exp, [mv.b], [mv.b], scale=-0.5)
                        TS("dve", vtmp.t[:], vsg.t[:], mv.t[:, 0:1], ALU.subtract, [vsg.b, mv.b], [vtmp.b], s2=mv.t[:, 3:4], op1=ALU.mult)
                        TT("pool", vtmp.t[:], vtmp.t[:], lng.t[:], ALU.mult, [vtmp.b, lng.b], [vtmp.b])
                        TT("pool", S_.vn[g].t[:], vtmp.t[:], lnb.t[:], ALU.add, [vtmp.b, lnb.b], [S_.vn[g].b])
                        yield

                def s2B(t):
                    S_ = sets[t % 2]
                    sl = S_.i
                    X_ = xT3[t % 3]
                    for g in range(GP):
                        gc = slice(g * 128, (g + 1) * 128)
                        bk = nb()
                        for hd in range(4):
                            p0 = (hd % 2) * 64
                            MM(banks[bk][p0:p0 + 64, (hd // 2) * 128:(hd // 2 + 1) * 128], S_.vn[g].t[:, hd * 64:(hd + 1) * 64],
                               wsT.t[:, hd, :], True, True, [S_.vn[g].b, wsT.b], [bbank[bk]])
                        TT("dve", stmp.t[:], bank3(bk), sgub.t[:], ALU.add, [bbank[bk], sgub.b], [stmp.b])
                        TT("dve", mixT.t[:, 4:6, gc], stmp.t[:], S_.gu.t[:, :, gc], ALU.mult, [stmp.b, S_.gu.b], [mixT.b])
                    yield
                    for g in range(GP):
                        gc = slice(g * 128, (g + 1) * 128)
                        prv = S_.xpl if g == 0 else S_.xpk[g - 1]
                        nxt = S_.xpr if g == GP - 1 else S_.xpk[g + 1]
                        cur = S_.xpk[g]
                        bk = nb()
                        for pg in range(4):
                            p0 = (pg % 2) * 64
                            o_ap = banks[bk][p0:p0 + 64, (pg // 2) * 128:(pg // 2 + 1) * 128]
                            for wi, src in enumerate((prv, cur, nxt)):
                                MM(o_ap, src.t[:, pg * 64:(pg + 1) * 64], bands.t[:, wi, pg, :], wi == 0, wi == 2,
                                   [src.b, bands.b], [bbank[bk]])
                        TT("dve", ptmp.t[:], bank3(bk), S_.invc.t[:, :, gc], ALU.mult, [bbank[bk], S_.invc.b], [ptmp.b])
                        TT("dve", dT.t[:, :, gc], ptmp.t[:], S_.xpT.t[:, :, gc], ALU.subtract, [ptmp.b, S_.xpT.b], [dT.b])
                    yield
                    yield from lockstep([chunk_gen(S_, g, (0, 1), True, {0: True, 1: g >= 1}) for g in range(GP)])
                    for j in range(2):
                        bk = nb()
                        for e in range(2):
                            p0 = e * 64
                            MM(banks[bk][p0:p0 + 64, 0:T2], poolw.t[p0:p0 + 64, j, :], dT.t[p0:p0 + 64, j, :], True, True,
                               [poolw.b, dT.b], [bbank[bk]])
                        ACT(mixT.t[:, 6 + j, :], banks[bk][:, 0:T2], AF.Identity, [bbank[bk], cols.b], [mixT.b],
                            scale=ccol(CI["ps"] + l * 2 + j))
                    if t >= 1:
                        TS("dve", Sf.t[:], Sf.t[:], fTL.t[:, t:t + 1], ALU.mult, [Sf.b, fTL.b], [Sf.b])
                    for g in range(GP):
                        state_to_bf(Sf, Sbf[0][g])
                        state_step(Sf, g, 0)
                    CP("dve", Sb.t[:], S_.Sstg.t[:], [S_.Sstg.b], [Sb.b])
                    state_to_bf(Sb, Sbf[1][GP - 1])
                    for g in range(GP - 1, 0, -1):
                        state_step(Sb, g, 1)
                        state_to_bf(Sb, Sbf[1][g - 1])
                    yield 3
                    yield from lockstep([out_gen(S_, g) for g in range(GP)])
                    for dch in range(KC):
                        bk = nb()
                        for k in range(KC):
                            MM(banks[bk][:, 0:T2], wo.t[:, k, dch * 128:(dch + 1) * 128], mixT.t[:, k, :], k == 0, k == KC - 1,
                               [wo.b, mixT.b], [bbank[bk]])
                        TT("dve", X_.t[:, dch, :], X_.t[:, dch, :], banks[bk][:, 0:T2], ALU.add, [X_.b, bbank[bk]], [X_.b])
                        if dch % 2 == 1:
                            yield
                    DMA("sp", xb_v[:, :, 1 + t * T2:1 + (t + 1) * T2], X_.t[:], [X_.b], [], f"st_xT{t % 3}")

                MSET("dve", Sf.t[:], 0.0, [Sf.b])
                if cfg.get("s2", 9) > 0:
                    ld_x(0)
                    pipeline(list(range(NT2)), s2A, s2B)
                R.barrier()
            else:
                A = Arena(arena_t, ARN)
                xT = [A.alloc(f"xT{i}", [128, KC, TILE]) for i in range(2)]
                for t in range(NT):
                    s = t % 2
                    DMA("sp", xT[s].t[:], xa_v[:, :, 1 + t * TILE:1 + (t + 1) * TILE], [], [xT[s].b], f"ld_xT{s}")
                    DMA("sp", xb_v[:, :, 1 + t * TILE:1 + (t + 1) * TILE], xT[s].t[:], [xT[s].b], [], f"st_xT{s}")
                R.barrier()

            if do_ffn:
                A = Arena(arena_t, ARN)
                h2T = A.alloc("h2T", [128, KC, HB + 2], BF16)
                gT = A.alloc("gT", [128, NF, HB], BF16)
                xw = [A.alloc(f"xw{i}", [128, KC, 342]) for i in range(2)]
                sq3 = A.alloc("sq3", [128, KC, 342], BF16)
                rstd3 = A.alloc("rstd3", [128, 342])
                w1a = [A.alloc(f"w1a{i}", [128, KC, 256], BF16) for i in range(2)]
                w1u = [A.alloc(f"w1u{i}", [128, KC, 256], BF16) for i in range(2)]
                w2c = [A.alloc(f"w2c{i}", [128, NF, 128], BF16) for i in range(2)]
                xd = [A.alloc(f"xd{i}", [128, HB]) for i in range(2)]
                t1 = [A.alloc(f"t1_{i}", [128, 344]) for i in range(4)]
                t2 = [A.alloc(f"t2_{i}", [128, 344]) for i in range(4)]
                w1v = w_f1_d[l].rearrange("(k p) c -> p k c", p=128)
                w2v = w_f2_d[l].rearrange("(f p) c -> p f c", p=128)
                cnt3 = [0]

                def ld_xw(hb, w):
                    s_ = (hb * 3 + w) % 2
                    DMA("sp", xw[s_].t[:], xb_v[:, :, hb * HB + 342 * w:hb * HB + 342 * (w + 1)], [], [xw[s_].b], f"ld_xw{s_}")

                def norm_w(hb, w):
                    s_ = (hb * 3 + w) % 2
                    norm_tile(xw[s_].t[:], xw[s_].b, 342, CI["n2"] + l * 8, sq3, rstd3, h2T.t[:, :, 342 * w:342 * (w + 1)], h2T.b)

                ld_xw(0, 0)
                ld_xw(0, 1)
                norm_w(0, 0)
                ld_xw(0, 2)
                norm_w(0, 1)
                norm_w(0, 2)
                for hb in range(NH):
                    cb = hb * HB

                    def ld_b(dch):
                        ds__ = dch % 2
                        DMA("pool", w2c[ds__].t[:], w2v[:, :, dch * 128:(dch + 1) * 128], [], [w2c[ds__].b], f"ld_w2c{ds__}")
                        DMA("sp", xd[ds__].t[:], xb[dch * 128:(dch + 1) * 128, 1 + cb:1 + cb + HB], [], [xd[ds__].b], f"ld_xd{ds__}")

                    pend = [None]
                    for fp in range(NF // 2):
                        ws_ = (hb * (NF // 2) + fp) % 2
                        for k in range(0, KC, 4):
                            DMA("pool", w1a[ws_].t[:, k:k + 4, :], w1v[:, k:k + 4, fp * 256:(fp + 1) * 256], [], [w1a[ws_].b], f"ld_w1a{ws_}")
                            DMA("pool", w1u[ws_].t[:, k:k + 4, :], w1v[:, k:k + 4, DFF + fp * 256:DFF + (fp + 1) * 256], [], [w1u[ws_].b], f"ld_w1u{ws_}")
                        if fp == NF // 2 - 1:
                            ld_b(0)
                        for fi in range(2):
                            f = fp * 2 + fi
                            for w in range(3):
                                nv = WV[w]
                                v0 = WS[w]
                                ts_ = cnt3[0] % 4
                                cnt3[0] += 1
                                bka = nb()
                                for k in range(KC):
                                    MM(banks[bka][:, 0:nv + 2], w1a[ws_].t[:, k, fi * 128:(fi + 1) * 128], h2T.t[:, k, v0:v0 + nv + 2],
                                       k == 0, k == KC - 1, [w1a[ws_].b, h2T.b], [bbank[bka]])
                                bku = nb()
                                for k in range(KC):
                                    MM(banks[bku][:, 0:nv], w1u[ws_].t[:, k, fi * 128:(fi + 1) * 128], h2T.t[:, k, v0 + 1:v0 + 1 + nv],
                                       k == 0, k == KC - 1, [w1u[ws_].b, h2T.b], [bbank[bku]])
                                if w == 0:
                                    TS("dve", banks[bka][:, 0:1], banks[bka][:, 0:1], fHL.t[:, hb:hb + 1], ALU.mult,
                                       [bbank[bka], fHL.b], [bbank[bka]])
                                if w == 2:
                                    TS("dve", banks[bka][:, nv + 1:nv + 2], banks[bka][:, nv + 1:nv + 2], fHR.t[:, hb:hb + 1], ALU.mult,
                                       [bbank[bka], fHR.b], [bbank[bka]])
                                cwi = CI["cw"] + l * 66
                                ACT(t1[ts_].t[:, 0:nv], banks[bka][:, 1:nv + 1], AF.Identity, [bbank[bka], cols.b], [t1[ts_].b],
                                    scale=ccol(cwi + 22 + f), bias=ccol(CI["cb"] + l * 22 + f))
                                STT(t2[ts_].t[:, 0:nv], banks[bka][:, 0:nv], ccol(cwi + f), t1[ts_].t[:, 0:nv], ALU.mult, ALU.add,
                                    [bbank[bka], cols.b, t1[ts_].b], [t2[ts_].b])
                                STT(t1[ts_].t[:, 0:nv], banks[bka][:, 2:nv + 2], ccol(cwi + 44 + f), t2[ts_].t[:, 0:nv], ALU.mult, ALU.add,
                                    [bbank[bka], cols.b, t2[ts_].b], [t1[ts_].b])
                                if pend[0] is not None:
                                    pend[0]()

                                def stage2(ts_=ts_, nv=nv, v0=v0, f=f, bku=bku):
                                    ACT(t2[ts_].t[:, 0:nv], t1[ts_].t[:, 0:nv], AF.Silu, [t1[ts_].b], [t2[ts_].b])
                                    TT("dve", gT.t[:, f, v0:v0 + nv], t2[ts_].t[:, 0:nv], banks[bku][:, 0:nv], ALU.mult,
                                       [t2[ts_].b, bbank[bku]], [gT.b])
                                pend[0] = stage2
                    pend[0]()
                    pend[0] = None
                    for dch in range(KC):
                        ds_ = dch % 2
                        if dch + 1 < KC:
                            ld_b(dch + 1)
                        if hb + 1 < NH and dch == 0:
                            ld_xw(hb + 1, 0)
                            ld_xw(hb + 1, 1)
                        for q in range(2):
                            bk = nb()
                            for f in range(NF):
                                MM(banks[bk][:], w2c[ds_].t[:, f, :], gT.t[:, f, q * 512:(q + 1) * 512], f == 0, f == NF - 1,
                                   [w2c[ds_].b, gT.b], [bbank[bk]])
                            TT("dve", xd[ds_].t[:, q * 512:(q + 1) * 512], xd[ds_].t[:, q * 512:(q + 1) * 512], banks[bk][:], ALU.add,
                               [xd[ds_].b, bbank[bk]], [xd[ds_].b])
                        DMA("sp", xa[dch * 128:(dch + 1) * 128, 1 + cb:1 + cb + HB], xd[ds_].t[:], [xd[ds_].b], [], f"st_xd{ds_}")
                        if hb + 1 < NH:
                            if dch == 1:
                                norm_w(hb + 1, 0)
                                ld_xw(hb + 1, 2)
                            if dch == 3:
                                norm_w(hb + 1, 1)
                            if dch == 5:
                                norm_w(hb + 1, 2)
                R.barrier()
            else:
                A = Arena(arena_t, ARN)
                xTc = [A.alloc(f"xTc{i}", [128, KC, TILE]) for i in range(2)]
                for t in range(NT):
                    s = t % 2
                    DMA("sp", xTc[s].t[:], xb_v[:, :, 1 + t * TILE:1 + (t + 1) * TILE], [], [xTc[s].b], f"ld_xTc{s}")
                    DMA("sp", xa_v[:, :, 1 + t * TILE:1 + (t + 1) * TILE], xTc[s].t[:], [xTc[s].b], [], f"st_xTc{s}")
                R.barrier()

        A = Arena(arena_t, ARN)
        xTf = [A.alloc(f"xTf{i}", [128, KC, TILE]) for i in range(2)]
        sqf = A.alloc("sqf", [128, KC, TILE], BF16)
        rstdf = A.alloc("rstdf", [128, TILE])
        yT = A.alloc("yT", [128, KC, TILE])
        yo = [A.alloc(f"yo{i}", [128, 4, D]) for i in range(2)]
        y_out_v = y_out.rearrange("(n g p) d -> n p g d", p=128, g=4)
        DMA("sp", xTf[0].t[:], xa_v[:, :, 1:1 + TILE], [], [xTf[0].b], "ld_xTf0")
        for t in range(NT):
            s = t % 2
            if t + 1 < NT:
                DMA("sp", xTf[1 - s].t[:], xa_v[:, :, 1 + (t + 1) * TILE:1 + (t + 2) * TILE], [], [xTf[1 - s].b], f"ld_xTf{1 - s}")
            ACT(sqf.t[:], xTf[s].t[:], AF.Square, [xTf[s].b], [sqf.b])
            bk = nb()
            for k in range(KC):
                MM(banks[bk][:], ones_bf.t[:], sqf.t[:, k, :], k == 0, k == KC - 1, [sqf.b, ones_bf.b], [bbank[bk]])
            ACT(rstdf.t[:], banks[bk][:], AF.Ln, [bbank[bk], epsc.b], [rstdf.b], scale=1.0 / D, bias=epsc.t[:])
            ACT(rstdf.t[:], rstdf.t[:], AF.Exp, [rstdf.b], [rstdf.b], scale=-0.5)
            for k in range(KC):
                STT(yT.t[:, k, :], xTf[s].t[:, k, :], ccol(CI["nf"] + k), rstdf.t[:], ALU.mult, ALU.mult,
                    [xTf[s].b, rstdf.b, cols.b], [yT.b])
            for g in range(4):
                for half in range(2):
                    bk = nb()
                    for kk in range(4):
                        k = half * 4 + kk
                        TR(banks[bk][:, kk * 128:(kk + 1) * 128], yT.t[:, k, g * 128:(g + 1) * 128], ident.t[:],
                           [yT.b, ident.b], [bbank[bk]])
                    CP("act" if half == 0 else "dve", yo[s].t[:, g, half * 512:(half + 1) * 512], banks[bk][:], [bbank[bk]], [yo[s].b])
            DMA("sp", y_out_v[t], yo[s].t[:], [yo[s].b], [], f"st_yo{s}")

        R.finalize()
        for n in R.dma_counts:
            sems[n] = es.enter_context(nc.semaphore(n))

        with nc.Block() as block:
            @block.tensor
            def _(h):
                R.emit("pe", h, sems)

            @block.scalar
            def _(h):
                R.emit("act", h, sems)

            @block.vector
            def _(h):
                R.emit("dve", h, sems)

            @block.gpsimd
            def _(h):
                R.emit("pool", h, sems)

            @block.sync
            def _(h):
                R.emit("sp", h, sems)
                for k_, v_ in R.dma_counts.items():
                    h.wait_ge(sems[k_], v_)
    return nc


POOL_WINDOWS = (2, 4, 8, 16)


def shared_inputs(inp, L):
    f32 = np.float32
    CI = col_index(L)
    cols = np.zeros((128, CI["n"]), f32)

    def colmaj(v):
        return np.ascontiguousarray(np.asarray(v, f32).reshape(-1, 128).T)

    for l in range(L):
        cols[:, CI["n1"] + l * 8:CI["n1"] + (l + 1) * 8] = colmaj(inp["norm1_g"][l])
        cols[:, CI["n2"] + l * 8:CI["n2"] + (l + 1) * 8] = colmaj(inp["norm2_g"][l])
        for j in range(3):
            cols[:, CI["cw"] + (l * 3 + j) * 22:CI["cw"] + (l * 3 + j + 1) * 22] = colmaj(inp["conv_w"][l, j])
        cols[:, CI["cb"] + l * 22:CI["cb"] + (l + 1) * 22] = colmaj(inp["conv_b"][l])
        cols[:, CI["gn"] + l * 4:CI["gn"] + (l + 1) * 4] = colmaj(inp["gla_norm_g"][l])
        cols[:, CI["ps"] + l * 2:CI["ps"] + (l + 1) * 2] = colmaj(inp["pool_scale"][l])
    cols[:, CI["nf"]:CI["nf"] + 8] = colmaj(inp["norm_f"])
    jj = np.arange(128)
    trif = (jj[:, None] <= jj[None, :]).astype(f32)
    trib = (jj[:, None] >= jj[None, :]).astype(f32)
    sh = {
        "ident": np.eye(128, dtype=f32),
        "maskf": np.ascontiguousarray(np.tile(trif, (1, 4))),
        "maskb": np.ascontiguousarray(np.tile(trib, (1, 4))),
        "cols": cols,
    }
    if L == 0:
        return sh
    sh["w_in"] = np.ascontiguousarray(inp["w_in"][:L], f32)
    sh["w_o"] = np.ascontiguousarray(inp["w_o"][:L], f32)
    sh["w_ffn_in"] = np.ascontiguousarray(inp["w_ffn_in"][:L], f32)
    sh["w_ffn_out"] = np.ascontiguousarray(inp["w_ffn_out"][:L], f32)
    w2aug = np.zeros((L, 33, 512), f32)
    w2aug[:, 0:16, 0:256] = inp["gla_gate_w2"][:L, 0]
    w2aug[:, 16:32, 256:512] = inp["gla_gate_w2"][:L, 1]
    w2aug[:, 32, 0:256] = inp["gla_gate_b"][:L, 0]
    w2aug[:, 32, 256:512] = inp["gla_gate_b"][:L, 1]
    sh["w2aug"] = w2aug
    sh["wsT"] = np.ascontiguousarray(np.transpose(np.asarray(inp["sgu_w"][:L], f32), (0, 3, 1, 2)))
    sb = np.asarray(inp["sgu_b"][:L], f32)
    sgub = np.zeros((L, 128, 2, 128), f32)
    for j in range(2):
        for e in range(2):
            sgub[:, e * 64:(e + 1) * 64, j, :] = sb[:, 2 * j + e][:, None, :]
    sh["sgu_bias"] = sgub
    sh["lng"] = np.ascontiguousarray(np.broadcast_to(np.asarray(inp["sgu_ln_g"][:L], f32)[:, None, :], (L, 128, 256)))
    sh["lnb"] = np.ascontiguousarray(np.broadcast_to(np.asarray(inp["sgu_ln_b"][:L], f32)[:, None, :], (L, 128, 256)))
    bands = np.zeros((128, 3, 4, 128), f32)
    s_ = jj[:, None]
    t_ = jj[None, :]
    for pg, w in enumerate(POOL_WINDOWS):
        hw = w // 2
        for wi, off in enumerate((-128, 0, 128)):
            ss = s_ + off
            bands[:, wi, pg, :] = ((ss >= t_ - hw) & (ss <= t_ + hw - 1)).astype(f32)
    sh["bands"] = bands
    pw = np.asarray(inp["pool_w"][:L], f32)
    poolw = np.zeros((L, 128, 2, 64), f32)
    for j in range(2):
        for e in range(2):
            poolw[:, e * 64:(e + 1) * 64, j, :] = pw[:, 2 * j + e]
    sh["poolw"] = poolw
    return sh


def core_inputs(blocks, T, L):
    f32 = np.float32
    NB = T // BLK
    NT = T // TILE
    NH = T // HB
    assert len(blocks) == NB
    x = np.zeros((T, D), f32)
    contL = np.zeros(NB, f32)
    contR = np.zeros(NB, f32)
    invc = np.ones((4, T), f32)
    for i, (sid, bi, nbs, xb_) in enumerate(blocks):
        if xb_ is not None:
            x[i * BLK:(i + 1) * BLK] = xb_
        if i > 0 and sid is not None and blocks[i - 1][0] == sid and blocks[i - 1][1] == bi - 1:
            contL[i] = 1.0
        if i + 1 < NB and sid is not None and blocks[i + 1][0] == sid and blocks[i + 1][1] == bi + 1:
            contR[i] = 1.0
        S = nbs * BLK
        pos = bi * BLK + np.arange(BLK)
        for pg, w in enumerate(POOL_WINDOWS):
            hw = w // 2
            lo = np.clip(pos - hw, 0, S)
            hi = np.clip(pos + hw, 0, S)
            invc[pg, i * BLK:(i + 1) * BLK] = 1.0 / (hi - lo).astype(f32)
    NT2 = T // TILE2
    TPB = BLK // TILE2
    fTL = np.ones(NT2, f32)
    fTR = np.ones(NT2, f32)
    fHL = np.ones(NH, f32)
    fHR = np.ones(NH, f32)
    for b in range(NB):
        fTL[b * TPB] = contL[b]
        fTR[b * TPB + TPB - 1] = contR[b]
        fHL[b * 2] = contL[b]
        fHR[b * 2 + 1] = contR[b]
    bc = lambda v: np.ascontiguousarray(np.broadcast_to(v[None, :], (128, v.shape[0])))
    d = {"x_tok": x, "flagT_L": bc(fTL), "flagT_R": bc(fTR), "flagH_L": bc(fHL), "flagH_R": bc(fHR)}
    if L > 0:
        ic = np.zeros((128, 2, T), f32)
        for j in range(2):
            for e in range(2):
                ic[e * 64:(e + 1) * 64, j, :] = invc[2 * j + e][None, :]
        d["inv_cnt"] = ic
    return d


_PROG = {}


def run_cores(inp, core_blocks, T, L, cfg_extra=None):
    cfg = dict(T=T, L=L)
    if cfg_extra:
        cfg.update(cfg_extra)
    key = tuple(sorted(cfg.items()))
    if key not in _PROG:
        _PROG[key] = build_program(cfg)
    nc = _PROG[key]
    sh = shared_inputs(inp, L)
    in_maps = []
    for blocks in core_blocks:
        m = dict(sh)
        m.update(core_inputs(blocks, T, L))
        in_maps.append(m)
    res = run_bass_kernel_spmd(nc, in_maps, core_ids=list(range(len(core_blocks))))
    return [r["y_tok"] for r in res.results]


def kernel(**inputs):
    inp = {k: np.asarray(v) for k, v in inputs.items()}
    xp = inp["x_prompt"]
    xs = inp["x_sample"]
    T = 16384
    NB = T // BLK
    core_blocks = []
    for c in range(2):
        core_blocks.append([(("s", c), b, NB, xs[c, b * BLK:(b + 1) * BLK]) for b in range(NB)])
    counts = [3, 3, 3, 3, 2, 2]
    nxt = 0
    owner = {}
    for c, n in enumerate(counts):
        bl = []
        for i in range(n):
            owner[nxt] = (c + 2, i)
            bl.append((("p", nxt), 0, 1, xp[nxt]))
            nxt += 1
        while len(bl) < NB:
            bl.append((None, 0, 1, None))
        core_blocks.append(bl)
    ys = run_cores(inp, core_blocks, T, DEPTH)
    y_prompt = np.zeros(xp.shape, np.float32)
    y_sample = np.zeros(xs.shape, np.float32)
    for c in range(2):
        y_sample[c] = ys[c]
    for sid, (c, i) in owner.items():
        y_prompt[sid] = ys[c][i * BLK:(i + 1) * BLK]
    return (y_prompt, y_sample)
```
